# Optimizing a Trainium2 kernel written in Bass

```python
import math
import jax, jax.numpy as jnp
from jax import lax
import numpy as np

D_MODEL = 1024
BATCH = 16
SEQ = 2048
DEPTH = 1

CHUNK = 64
PLE_DIM = 256
POOL_WIDTH = D_MODEL
POOL_WINDOWS = (2, 4, 8, 16)
N_POOL_GROUPS = 4
POOL_GROUP = POOL_WIDTH // N_POOL_GROUPS
SSM_WIDTH = D_MODEL // 2
SSM_GROUP_CH = 16
SSM_GROUPS = SSM_WIDTH // SSM_GROUP_CH
SSM_STATE = 64
DT_MIN = 1e-3
DT_MAX = 1e-1
LN_EPS = 1e-5
DEEPNORM_ALPHA = (2.0 * DEPTH) ** 0.25
DEEPNORM_BETA = (8.0 * DEPTH) ** -0.25
SPLIT_POINTS = (
    POOL_WIDTH,
    2 * POOL_WIDTH,
    2 * POOL_WIDTH + SSM_WIDTH,
    2 * POOL_WIDTH + 2 * SSM_WIDTH,
    2 * POOL_WIDTH + 2 * SSM_WIDTH + D_MODEL,
    2 * POOL_WIDTH + 2 * SSM_WIDTH + 2 * D_MODEL,
)
IN_WIDTH = 2 * POOL_WIDTH + 2 * SSM_WIDTH + 3 * D_MODEL

kernel_name = "hybrid_pool_s5_gated_deepnorm"


def layer_norm(x, g, b):
    x32 = x.astype(jnp.float32)
    mu = jnp.mean(x32, axis=-1, keepdims=True)
    var = jnp.mean(jnp.square(x32 - mu), axis=-1, keepdims=True)
    y = (x32 - mu) * lax.rsqrt(var + LN_EPS)
    return (y * g.astype(jnp.float32) + b.astype(jnp.float32)).astype(x.dtype)


def pool_mixer(u, w_groups, scale):
    bsz, s, _ = u.shape
    ug = u.astype(jnp.float32).reshape(bsz, s, N_POOL_GROUPS, POOL_GROUP)
    cs = jnp.cumsum(ug, axis=1)
    t = jnp.arange(s)
    outs = []
    for gi, w in enumerate(POOL_WINDOWS):
        c = cs[:, :, gi, :]
        prev = jnp.pad(c, ((0, 0), (w, 0), (0, 0)))[:, :s]
        cnt = jnp.minimum(t + 1, w).astype(jnp.float32)[None, :, None]
        outs.append((c - prev) / cnt - ug[:, :, gi, :])
    d = jnp.stack(outs, axis=2).astype(u.dtype)
    y = jnp.einsum("bsgi,gio->bsgo", d, w_groups).reshape(bsz, s, POOL_WIDTH)
    return y * scale


def cmul(ar, ai, br, bi):
    return ar * br - ai * bi, ar * bi + ai * br


def s5_mixer(u, a_re, a_im, log_dt, b_re, b_im, c_re, c_im, d_skip, glu_w, glu_b):
    bsz, s, _ = u.shape
    f32 = jnp.float32
    u32 = u.astype(f32).reshape(bsz, s, SSM_GROUPS, SSM_GROUP_CH)
    dt = jnp.exp(log_dt.astype(f32))[:, None]
    lr = a_re.astype(f32)
    li = a_im.astype(f32)
    mag = jnp.exp(lr * dt)
    abar_r = mag * jnp.cos(li * dt)
    abar_i = mag * jnp.sin(li * dt)
    den = lr * lr + li * li
    zr, zi = cmul(abar_r - 1.0, abar_i, lr, -li)
    zr = zr / den
    zi = zi / den
    bbar_r, bbar_i = cmul(zr[..., None], zi[..., None], b_re.astype(f32), b_im.astype(f32))
    bu_r = jnp.einsum("bsgh,gph->sbgp", u32, bbar_r)
    bu_i = jnp.einsum("bsgh,gph->sbgp", u32, bbar_i)
    a_r = jnp.broadcast_to(abar_r[None, None], (s, 1, SSM_GROUPS, SSM_STATE))
    a_i = jnp.broadcast_to(abar_i[None, None], (s, 1, SSM_GROUPS, SSM_STATE))

    def combine(left, right):
        al_r, al_i, bl_r, bl_i = left
        ar_r, ar_i, br_r, br_i = right
        na_r, na_i = cmul(ar_r, ar_i, al_r, al_i)
        t_r, t_i = cmul(ar_r, ar_i, bl_r, bl_i)
        return na_r, na_i, t_r + br_r, t_i + br_i

    _, _, xr, xi = lax.associative_scan(combine, (a_r, a_i, bu_r, bu_i), axis=0)
    y = (jnp.einsum("sbgp,ghp->bsgh", xr, c_re.astype(f32))
         - jnp.einsum("sbgp,ghp->bsgh", xi, c_im.astype(f32)))
    y = y.reshape(bsz, s, SSM_WIDTH) + d_skip.astype(f32) * u32.reshape(bsz, s, SSM_WIDTH)
    g = jax.nn.gelu(y)
    out = g * jax.nn.sigmoid(g @ glu_w.astype(f32) + glu_b.astype(f32))
    return out.astype(u.dtype)


def setup_inputs(seed: int = 0) -> dict:
    key = jax.random.key(seed)
    ks = jax.random.split(key, 24)
    f32 = jnp.float32
    L = DEPTH
    nrm = lambda k, shape: jax.random.normal(k, shape, f32)
    x = nrm(ks[0], (BATCH, SEQ, D_MODEL))
    p = nrm(ks[1], (DEPTH, BATCH, SEQ, PLE_DIM))
    w_in = nrm(ks[2], (L, D_MODEL, IN_WIDTH)) * D_MODEL ** -0.5
    pool_w = nrm(ks[3], (L, N_POOL_GROUPS, POOL_GROUP, POOL_GROUP)) * POOL_GROUP ** -0.5
    pool_scale = 1.0 + 0.02 * nrm(ks[4], (L, POOL_WIDTH))
    n = jnp.arange(SSM_STATE, dtype=f32)
    ssm_a_re = -0.5 + 0.01 * nrm(ks[5], (L, SSM_GROUPS, SSM_STATE))
    ssm_a_im = math.pi * n[None, None, :] + 0.01 * nrm(ks[6], (L, SSM_GROUPS, SSM_STATE))
    ssm_log_dt = (math.log(DT_MIN) + jax.random.uniform(ks[7], (L, SSM_GROUPS), f32)
                  * (math.log(DT_MAX) - math.log(DT_MIN)))
    b_std = (2.0 * SSM_GROUP_CH) ** -0.5
    ssm_b_re = nrm(ks[8], (L, SSM_GROUPS, SSM_STATE, SSM_GROUP_CH)) * b_std
    ssm_b_im = nrm(ks[9], (L, SSM_GROUPS, SSM_STATE, SSM_GROUP_CH)) * b_std
    c_std = (2.0 * SSM_STATE) ** -0.5
    ssm_c_re = nrm(ks[10], (L, SSM_GROUPS, SSM_GROUP_CH, SSM_STATE)) * c_std
    ssm_c_im = nrm(ks[11], (L, SSM_GROUPS, SSM_GROUP_CH, SSM_STATE)) * c_std
    ssm_d = nrm(ks[12], (L, SSM_WIDTH))
    glu_w = nrm(ks[13], (L, SSM_WIDTH, SSM_WIDTH)) * SSM_WIDTH ** -0.5
    glu_b = 0.01 * nrm(ks[14], (L, SSM_WIDTH))
    w_branch_pool = nrm(ks[15], (L, POOL_WIDTH, D_MODEL)) * POOL_WIDTH ** -0.5 * DEEPNORM_BETA
    w_branch_ssm = nrm(ks[16], (L, SSM_WIDTH, D_MODEL)) * SSM_WIDTH ** -0.5 * DEEPNORM_BETA
    w_out = nrm(ks[17], (L, D_MODEL, D_MODEL)) * D_MODEL ** -0.5 * DEEPNORM_BETA
    w_ple = nrm(ks[18], (L, PLE_DIM, D_MODEL)) * PLE_DIM ** -0.5 * DEEPNORM_BETA
    ln_g = 1.0 + 0.02 * nrm(ks[19], (L, D_MODEL))
    ln_b = 0.01 * nrm(ks[20], (L, D_MODEL))
    return {"x": x, "p": p, "w_in": w_in, "pool_w": pool_w, "pool_scale": pool_scale,
            "ssm_a_re": ssm_a_re, "ssm_a_im": ssm_a_im, "ssm_log_dt": ssm_log_dt,
            "ssm_b_re": ssm_b_re, "ssm_b_im": ssm_b_im, "ssm_c_re": ssm_c_re, "ssm_c_im": ssm_c_im,
            "ssm_d": ssm_d, "glu_w": glu_w, "glu_b": glu_b,
            "w_branch_pool": w_branch_pool, "w_branch_ssm": w_branch_ssm, "w_out": w_out,
            "w_ple": w_ple, "ln_g": ln_g, "ln_b": ln_b}


def reference(x, p, w_in, pool_w, pool_scale, ssm_a_re, ssm_a_im, ssm_log_dt,
              ssm_b_re, ssm_b_im, ssm_c_re, ssm_c_im, ssm_d, glu_w, glu_b,
              w_branch_pool, w_branch_ssm, w_out, w_ple, ln_g, ln_b):
    for i in range(DEPTH):
        proj = jnp.einsum("bsd,dn->bsn", x, w_in[i])
        pool_in, pool_gate, ssm_in, ssm_gate, g_pool, g_ssm, ple_gate = jnp.split(
            proj, SPLIT_POINTS, axis=-1)
        y_pool = pool_mixer(pool_in, pool_w[i], pool_scale[i]) * jax.nn.silu(pool_gate)
        y_ssm = s5_mixer(ssm_in, ssm_a_re[i], ssm_a_im[i], ssm_log_dt[i], ssm_b_re[i],
                         ssm_b_im[i], ssm_c_re[i], ssm_c_im[i], ssm_d[i], glu_w[i],
                         glu_b[i]) * jax.nn.silu(ssm_gate)
        merged = (jax.nn.sigmoid(g_pool) * (y_pool @ w_branch_pool[i])
                  + jax.nn.sigmoid(g_ssm) * (y_ssm @ w_branch_ssm[i]))
        mix = merged @ w_out[i]
        ple = jax.nn.sigmoid(ple_gate) * (p[i] @ w_ple[i])
        x = layer_norm(DEEPNORM_ALPHA * x + mix + ple, ln_g[i], ln_b[i])
    return x
```

```python
import math
from contextlib import ExitStack

import numpy as np
import concourse.bass as bass
import concourse.mybir as mybir
from concourse.bass_utils import run_bass_kernel_spmd

F32 = mybir.dt.float32
BF16 = mybir.dt.bfloat16
I32 = mybir.dt.int32
AF = mybir.ActivationFunctionType
ALU = mybir.AluOpType

D = 1024
PLE = 256
SSM_W = 512
NGRP = 32
NPAIR = 16
LCH = 8
NT = 512
NCH = NT // LCH
ALPHA = 2.0 ** 0.25
LN_EPS = 1e-5
TWO_PI = 2.0 * math.pi
GELU_C0 = math.sqrt(2.0 / math.pi)
GELU_C1 = 0.044715
POOL_WINDOWS = (2, 4, 8, 16)
SLOT = 2048
NRING = 7
NTMP = 5
TMPW = 528


class Buf:
    __slots__ = ("name", "w", "r")

    def __init__(self, name):
        self.name = name
        self.w = []
        self.r = []


class Sched:
    def __init__(self, nc, es):
        self.nc = nc
        self.engs = {"pe": nc.tensor, "dve": nc.vector, "act": nc.scalar, "pool": nc.gpsimd, "sp": nc.sync}
        self.sem = {e: es.enter_context(nc.semaphore("prog_" + e)) for e in ("pe", "dve", "act", "pool")}
        self.cnt = {e: 0 for e in ("pe", "dve", "act", "pool")}
        self.waited = {e: {} for e in self.engs}
        self.dsems = {
            "sp": [es.enter_context(nc.semaphore("dsp%d" % i)) for i in range(8)],
            "pool": [es.enter_context(nc.semaphore("dpl%d" % i)) for i in range(14)],
        }
        self.dn = {"sp": 0, "pool": 0}
        self.last_dma_tok = {}

    def _wait(self, e, tok):
        key, sem, val = tok
        if e == "pe" and key == "pe":
            return
        if self.waited[e].get(key, 0) >= val:
            return
        self.engs[e].wait_ge(sem, val)
        self.waited[e][key] = val

    def _deps(self, e, reads, writes):
        for b in reads:
            for t in b.w:
                self._wait(e, t)
        for b in writes:
            for t in b.w:
                self._wait(e, t)
            for t in b.r:
                self._wait(e, t)

    @staticmethod
    def _add(lst, tok):
        for i, t in enumerate(lst):
            if t[0] == tok[0]:
                if t[2] < tok[2]:
                    lst[i] = tok
                return
        lst.append(tok)

    def _commit(self, tok, reads, writes):
        for b in reads:
            self._add(b.r, tok)
        for b in writes:
            b.w = [tok]
            b.r = []

    def op(self, e, reads, writes, fn):
        self._deps(e, reads, writes)
        ins = fn()
        self.cnt[e] += 1
        ins.then_inc(self.sem[e], 1)
        self._commit((e, self.sem[e], self.cnt[e]), reads, writes)

    def group(self, e, reads, writes, fns):
        self._deps(e, reads, writes)
        ins = None
        for f in fns:
            ins = f()
        self.cnt[e] += 1
        ins.then_inc(self.sem[e], 1)
        self._commit((e, self.sem[e], self.cnt[e]), reads, writes)

    def dma(self, q, reads, writes, fn):
        n = self.dn[q]
        sems = self.dsems[q]
        r = n % len(sems)
        prev = 16 * (n // len(sems))
        key = "d%s%d" % (q, r)
        if prev > 0:
            self._wait(q, (key, sems[r], prev))
        self._deps(q, reads, writes)
        ins = fn()
        ins.then_inc(sems[r], 16)
        self.dn[q] += 1
        tok = (key, sems[r], prev + 16)
        self.last_dma_tok[key] = tok
        self._commit(tok, reads, writes)

    def barrier(self):
        toks = [(e, self.sem[e], self.cnt[e]) for e in self.cnt if self.cnt[e] > 0]
        toks += [t for k, t in self.last_dma_tok.items() if k.startswith("dsp")]
        for e in self.engs:
            for t in toks:
                if t[0] == e:
                    continue
                self._wait(e, t)

    def final_wait(self, q="sp"):
        for t in self.last_dma_tok.values():
            self._wait(q, t)


def slot_plan():
    plan = []

    def win(name, col0, ncol=256):
        plan.append((name, [(0, 8, ncol, "w_in", 0, col0)]))

    win("ssm_in_0", 2048)
    win("ssm_in_1", 2304)
    win("ssm_gate_0", 2560)
    win("ssm_gate_1", 2816)
    for gi in (3, 2, 1, 0):
        win("pool_in_%d" % gi, gi * 256)
        win("pool_gate_%d" % gi, 1024 + gi * 256)
    for jp in range(4):
        win("g_pool_%d" % jp, 3072 + jp * 256)
        win("g_ssm_%d" % jp, 4096 + jp * 256)
        plan.append(("w_bp_%d" % jp, [(0, 8, 256, "w_bp", 0, jp * 256)]))
        if jp % 2 == 0:
            plan.append(("w_bs_%d" % (jp // 2), [(0, 4, 512, "w_bs", 0, (jp // 2) * 512)]))
    for h in range(2):
        plan.append(("ple_gate_%d_lo" % h, [(0, 4, 512, "w_in", 0, 5120 + h * 512)]))
        plan.append(("ple_gate_%d_hi" % h, [(0, 4, 512, "w_in", 512, 5120 + h * 512)]))
        plan.append(("w_out_%d_lo" % h, [(0, 4, 512, "w_out", 0, h * 512)]))
        plan.append(("w_out_%d_hi" % h, [(0, 4, 512, "w_out", 512, h * 512)]))
    return plan


def build_nc(nseq=2, seqlen=2048, phase=99.0):
    assert seqlen % NT == 0
    tps = seqlen // NT
    nc = bass.Bass("TRN2", target_bir_lowering=False)

    def din(name, shape):
        return nc.dram_tensor(name, list(shape), F32, kind="ExternalInput")

    x_t = din("x", [nseq, seqlen, D])
    p_t = din("p", [nseq, seqlen, PLE])
    w_in_t = din("w_in", [D, 6144])
    pool_w_t = din("pool_w", [4, 256, 256])
    pool_scale_t = din("pool_scale", [D])
    a_re_t = din("ssm_a_re", [NGRP, 64])
    a_im_t = din("ssm_a_im", [NGRP, 64])
    log_dt_t = din("ssm_log_dt", [NGRP])
    b_re_t = din("ssm_b_re", [NGRP, 64, 16])
    b_im_t = din("ssm_b_im", [NGRP, 64, 16])
    c_re_t = din("ssm_c_re", [NGRP, 16, 64])
    c_im_t = din("ssm_c_im", [NGRP, 16, 64])
    ssm_d_t = din("ssm_d", [SSM_W])
    glu_w_t = din("glu_w", [SSM_W, SSM_W])
    glu_b_t = din("glu_b", [SSM_W])
    w_bp_t = din("w_branch_pool", [D, D])
    w_bs_t = din("w_branch_ssm", [SSM_W, D])
    w_out_t = din("w_out", [D, D])
    w_ple_t = din("w_ple", [PLE, D])
    ln_g_t = din("ln_g", [D])
    ln_b_t = din("ln_b", [D])
    out_t = nc.dram_tensor("out", [nseq, seqlen, D], F32, kind="ExternalOutput")

    plan = slot_plan()
    nslot = len(plan)
    wscr_t = nc.dram_tensor("wscr", [nslot, 128, SLOT], BF16, kind="Internal")
    wsrc = {"w_in": w_in_t, "w_bp": w_bp_t, "w_bs": w_bs_t, "w_out": w_out_t}

    with ExitStack() as es:
        S = Sched(nc, es)
        V, A, G, PE, SP = nc.vector, nc.scalar, nc.gpsimd, nc.tensor, nc.sync

        def sbt(name, shape, dt, stack=es):
            return stack.enter_context(nc.sbuf_tensor(name, list(shape), dt))

        ident = sbt("ident", [128, 128], BF16)
        Win_sb = sbt("Win_sb", [128, 4, 8, 2, 128], BF16)
        Wout_sb = sbt("Wout_sb", [128, 16, 8, 2, 32], BF16)
        Toep_sb = sbt("Toep_sb", [128, 4, 8, 128], BF16)
        Ec = sbt("Ec", [128, 16, NCH], F32)
        Es = sbt("Es", [128, 16, NCH], F32)
        Rf = sbt("Rf", [128, 16, NCH], F32)
        R_sb = sbt("R_sb", [128, 16], F32)
        pw_sb = sbt("pw_sb", [128, 4, 2, 256], BF16)
        gluw_sb = sbt("gluw_sb", [128, 4, 512], BF16)
        wple_sb = sbt("wple_sb", [128, 2, 1024], BF16)
        lng_sb = sbt("lng_sb", [128, D], F32)
        lnb_sb = sbt("lnb_sb", [128, D], F32)
        dsk = sbt("dsk", [128, 4], F32)
        glub = sbt("glub", [128, 4], F32)
        pscale = sbt("pscale", [128, 8], F32)
        invc = sbt("invc", [128, 16], F32)

        B_const = Buf("const")
        xbf = sbt("xbf", [128, 4, D], BF16); B_xbf = Buf("xbf")
        pbf = sbt("pbf", [128, 4, PLE], BF16); B_pbf = Buf("pbf")
        x_ap = x_t.ap()
        p_ap = p_t.ap()

        def load_x(mt):
            b, ti = divmod(mt, tps)
            tok0 = ti * NT
            S.dma("pool", [], [B_xbf], lambda: G.dma_start(
                out=xbf[:], in_=x_ap[b, tok0:tok0 + NT, :].rearrange("(t p) d -> p t d", p=128)))
            S.dma("pool", [], [B_pbf], lambda: G.dma_start(
                out=pbf[:], in_=p_ap[b, tok0:tok0 + NT, :].rearrange("(t p) d -> p t d", p=128)))
        psum = [es.enter_context(nc.psum_tensor("ps%d" % i, [128, 512], F32)) for i in range(8)]
        psB = [Buf("ps%d" % i) for i in range(8)]
        mm_banks = list(range(8))
        tr_banks = list(range(8))
        rr = {"mm": 0, "tr": 0, "tmp": 0, "ring": 0}

        def next_mm():
            i = mm_banks[rr["mm"] % len(mm_banks)]
            rr["mm"] += 1
            return psum[i], psB[i]

        def next_tr():
            return next_mm()

        S.op("pool", [], [B_const], lambda: G.memset(ident[:], 0.0))
        S.op("pool", [B_const], [B_const], lambda: G.affine_select(
            out=ident[:], in_=ident[:], pattern=[[-1, 128]], compare_op=ALU.not_equal,
            fill=1.0, base=0, channel_multiplier=1))
        for t in range(16):
            S.op("pool", [], [B_const], lambda t=t: G.memset(invc[:, t:t + 1], 1.0 / (t + 1)))
        S.dma("pool", [], [B_const], lambda: G.dma_start(
            out=pw_sb[:], in_=pool_w_t.ap().rearrange("g (ic p) o -> p g ic o", p=128)))
        S.dma("pool", [], [B_const], lambda: G.dma_start(
            out=gluw_sb[:], in_=glu_w_t.ap().rearrange("(kc p) n -> p kc n", p=128)))
        S.dma("pool", [], [B_const], lambda: G.dma_start(
            out=wple_sb[:], in_=w_ple_t.ap().rearrange("(kc p) n -> p kc n", p=128)))

        load_x(0)
        scrB = [Buf("scr%d" % i) for i in range(nslot)]

        def emit_casts(lo, hi):
            for si in range(lo, min(hi, nslot)):
                name, parts = plan[si]
                for (dst_off, kcn, ncol, src, row0, col0) in parts:
                    src_ap = wsrc[src].ap()[row0:row0 + kcn * 128, col0:col0 + ncol].rearrange("(kc p) n -> p kc n", p=128)
                    dst_ap = wscr_t.ap()[si, :, dst_off:dst_off + kcn * ncol].rearrange("p (kc n) -> p kc n", kc=kcn)
                    S.dma("pool", [], [scrB[si]], lambda d=dst_ap, s=src_ap: G.dma_start(out=d, in_=s))

        with ExitStack() as ps_:
            def pt(name, shape, dt=F32):
                return sbt("pre_" + name, shape, dt, stack=ps_)

            lr = pt("lr", [128, 16]); li = pt("li", [128, 16]); ldt = pt("ldt", [128, 16])
            dtt = pt("dtt", [128, 16]); ang = pt("ang", [128, 16]); lrdt = pt("lrdt", [128, 16])
            mag = pt("mag", [128, 9, 16]); xs = pt("xs", [128, 2, 9, 16]); xi32 = pt("xi32", [128, 2, 9, 16], I32)
            xk = pt("xk", [128, 2, 9, 16]); sc = pt("sc", [128, 2, 9, 16])
            Pr = pt("Pr", [128, 9, 16]); Pi = pt("Pi", [128, 9, 16])
            t16 = [pt("t16_%d" % i, [128, 16]) for i in range(6)]
            zr = pt("zr", [128, 16]); zi = pt("zi", [128, 16])
            BTr = pt("BTr", [128, 16, 32]); BTi = pt("BTi", [128, 16, 32])
            BbTr = pt("BbTr", [128, 16, 32]); BbTi = pt("BbTi", [128, 16, 32])
            CTr = pt("CTr", [128, 16, 32]); CTi = pt("CTi", [128, 16, 32])
            Cn = [pt("Cn%d" % r, [128, 4, 64]) for r in range(2)]
            Zc = [pt("Zc%d" % r, [128, 4, 128]) for r in range(2)]
            m0 = pt("m0", [128, 1]); m1 = pt("m1", [128, 1])
            identf = pt("identf", [128, 128])
            tA = pt("tA", [128, 16, 32]); tB = pt("tB", [128, 16, 32])
            tA9 = pt("tA9", [128, 9, 16, 32]); tB9 = pt("tB9", [128, 9, 16, 32])
            G32 = pt("G32", [128, 9, 2, 16, 32])
            Qb = pt("Qb", [128, 8, 2, 16, 32], BF16)
            tT = [pt("tT%d" % i, [128, 16, 32]) for i in range(2)]
            P = Buf("pre"); P_trig = Buf("pre_trig"); P_C = Buf("pre_C"); P_B = Buf("pre_B")
            P_G = Buf("pre_G"); P_Q = Buf("pre_Q"); P_tab = Buf("pre_tab"); P_m = Buf("pre_m")

            def dv(fn, r=(), w=()):
                S.op("dve", list(r), list(w), fn)

            def ac(fn, r=(), w=()):
                S.op("act", list(r), list(w), fn)

            def pl(fn, r=(), w=()):
                S.op("pool", list(r), list(w), fn)

            stg = {nm: pt("stg_" + nm, [16, 128]) for nm in ("lr", "li", "ldt", "dsk", "glub", "psc")}
            ldt2 = pt("ldt2", [16, 2])
            P_stg = Buf("pre_stg")
            S.dma("sp", [], [P_stg], lambda: SP.dma_start(out=stg["lr"][:], in_=a_re_t.ap().rearrange("(q g) p -> q (g p)", g=2)))
            S.dma("sp", [], [P_stg], lambda: SP.dma_start(out=stg["li"][:], in_=a_im_t.ap().rearrange("(q g) p -> q (g p)", g=2)))
            S.dma("sp", [], [P_stg], lambda: SP.dma_start(out=ldt2[:], in_=log_dt_t.ap().rearrange("(q g) -> q g", g=2)))
            S.dma("sp", [], [P_stg], lambda: SP.dma_start(out=stg["dsk"][0:4, :], in_=ssm_d_t.ap().rearrange("(k p) -> k p", p=128)))
            S.dma("sp", [], [P_stg], lambda: SP.dma_start(out=stg["glub"][0:4, :], in_=glu_b_t.ap().rearrange("(k p) -> k p", p=128)))
            S.dma("sp", [], [P_stg], lambda: SP.dma_start(out=stg["psc"][0:8, :], in_=pool_scale_t.ap().rearrange("(k p) -> k p", p=128)))
            for r, src in enumerate((c_re_t, c_im_t)):
                S.dma("sp", [], [P_C], lambda r=r, src=src: SP.dma_start(
                    out=Cn[r][:], in_=src.ap().rearrange("(k g) h p -> (g h) k p", g=8)))
            pl(lambda: G.memset(BTr[:], 0.0), w=[P_B])
            pl(lambda: G.memset(BTi[:], 0.0), r=[P_B], w=[P_B])
            for g2 in range(2):
                for (src, dst) in ((b_re_t, BTr), (b_im_t, BTi)):
                    S.dma("sp", [], [P_B], lambda g2=g2, src=src, dst=dst: SP.dma_start(
                        out=dst[g2 * 64:(g2 + 1) * 64, :, g2 * 16:(g2 + 1) * 16],
                        in_=bass.AP(src, g2 * 1024, [[16, 64], [2048, 16], [1, 16]])))
            S.dma("sp", [], [B_const], lambda: SP.dma_start(
                out=lng_sb[:], in_=bass.AP(ln_g_t, 0, [[0, 128], [1, D]])))
            S.dma("sp", [], [B_const], lambda: SP.dma_start(
                out=lnb_sb[:], in_=bass.AP(ln_b_t, 0, [[0, 128], [1, D]])))
            pl(lambda: G.memset(identf[:], 0.0), w=[P_m])
            pl(lambda: G.affine_select(out=identf[:], in_=identf[:], pattern=[[-1, 128]], compare_op=ALU.not_equal,
                                       fill=1.0, base=0, channel_multiplier=1), r=[P_m], w=[P_m])
            dv(lambda: V.tensor_copy(out=stg["ldt"][:].rearrange("q (g p) -> q g p", g=2),
                                     in_=ldt2[:].unsqueeze(2).to_broadcast([16, 2, 64])), r=[P_stg], w=[P_stg])
            for nm, n, dst, db in (("lr", 16, lr, P), ("li", 16, li, P), ("ldt", 16, ldt, P),
                                   ("dsk", 4, dsk, B_const), ("glub", 4, glub, B_const), ("psc", 8, pscale, B_const)):
                pst, psb = next_mm()
                S.group("pe", [P_stg, P_m], [psb], [
                    (lambda nm=nm, n=n, pst=pst: PE.transpose(pst[:, 0:n], stg[nm][0:n, :], identf[0:n, 0:n]))])
                ac(lambda dst=dst, n=n, pst=pst: A.activation(out=dst[:, 0:n], in_=pst[:, 0:n], func=AF.Copy), r=[psb], w=[db])
            pl(lambda: G.memset(m0[:], 0.0), r=[P_m], w=[P_m])
            for i in range(4):
                pl(lambda i=i: G.memset(m0[32 * i:32 * i + 16, :], 1.0), r=[P_m], w=[P_m])
            pl(lambda: G.tensor_scalar(out=m1[:], in0=m0[:], scalar1=-1.0, scalar2=1.0, op0=ALU.mult, op1=ALU.add),
               r=[P_m], w=[P_m])
            pl(lambda: G.memset(Toep_sb[:], 0.0), w=[B_const])
            pl(lambda: G.memset(Rf[:], 0.0), r=[B_const], w=[B_const])
            emit_casts(0, 10)
            for r in range(2):
                dv(lambda r=r: V.tensor_scalar(out=Zc[r][:, :, 0:64], in0=Cn[r][:], scalar1=m0[:, 0:1], scalar2=None, op0=ALU.mult),
                   r=[P_C, P_m], w=[P_C])
                dv(lambda r=r: V.tensor_scalar(out=Zc[r][:, :, 64:128], in0=Cn[r][:], scalar1=m1[:, 0:1], scalar2=None, op0=ALU.mult),
                   r=[P_C, P_m], w=[P_C])
            for r, dst in enumerate((CTr, CTi)):
                pst, psb = next_mm()
                S.group("pe", [P_C, P_m], [psb], [
                    (lambda k=k, r=r, pst=pst: PE.transpose(pst[:, k * 128:(k + 1) * 128], Zc[r][:, k, :], identf[:]))
                    for k in range(4)])
                ac(lambda dst=dst, pst=pst: A.activation(out=dst[:].rearrange("p q c -> p (q c)"), in_=pst[:], func=AF.Copy),
                   r=[psb], w=[P_G])
            ac(lambda: A.activation(out=dtt[:], in_=ldt[:], func=AF.Exp), r=[P], w=[P])
            dv(lambda: V.tensor_tensor(out=ang[:], in0=li[:], in1=dtt[:], op=ALU.mult), r=[P], w=[P])
            dv(lambda: V.tensor_tensor(out=lrdt[:], in0=lr[:], in1=dtt[:], op=ALU.mult), r=[P], w=[P])
            for j in range(9):
                ac(lambda j=j: A.activation(out=mag[:, j, :], in_=lrdt[:], func=AF.Exp, scale=float(j)), r=[P], w=[P_trig])
                dv(lambda j=j: V.tensor_scalar(out=xs[:, 0, j, :], in0=ang[:], scalar1=float(j) / TWO_PI,
                                               scalar2=None, op0=ALU.mult), r=[P], w=[P])
            dv(lambda: V.tensor_scalar(out=xs[:, 1], in0=xs[:, 0], scalar1=0.25, scalar2=None, op0=ALU.add), r=[P], w=[P])
            dv(lambda: V.tensor_copy(out=xi32[:], in_=xs[:]), r=[P], w=[P])
            dv(lambda: V.tensor_copy(out=xk[:], in_=xi32[:]), r=[P], w=[P])
            dv(lambda: V.tensor_tensor(out=xs[:], in0=xs[:], in1=xk[:], op=ALU.subtract), r=[P], w=[P])
            ac(lambda: A.activation(out=sc[:], in_=xs[:], func=AF.Sin, scale=TWO_PI), r=[P], w=[P_trig])
            dv(lambda: V.tensor_tensor(out=Pr[:], in0=mag[:], in1=sc[:, 1], op=ALU.mult), r=[P_trig], w=[P_trig])
            dv(lambda: V.tensor_tensor(out=Pi[:], in0=mag[:], in1=sc[:, 0], op=ALU.mult), r=[P_trig], w=[P_trig])
            pl(lambda: G.tensor_copy(out=R_sb[:], in_=mag[:, 8, :]), r=[P_trig, B_const], w=[B_const])
            pl(lambda: G.tensor_copy(out=Ec[:, :, 0], in_=sc[:, 1, 8, :]), r=[P_trig, B_const], w=[B_const])
            pl(lambda: G.tensor_copy(out=Es[:, :, 0], in_=sc[:, 0, 8, :]), r=[P_trig, B_const], w=[B_const])
            m = 1
            while m < NCH:
                umr = Ec[:, :, m - 1:m].to_broadcast([128, 16, m])
                umi = Es[:, :, m - 1:m].to_broadcast([128, 16, m])
                e0, s0 = Ec[:, :, 0:m], Es[:, :, 0:m]
                ta, tb = tT[0][:, :, 0:m], tT[1][:, :, 0:m]
                rw = dict(r=[B_const, P_tab], w=[B_const, P_tab])
                pl(lambda e0=e0, umr=umr, ta=ta: G.tensor_tensor(out=ta, in0=e0, in1=umr, op=ALU.mult), **rw)
                pl(lambda s0=s0, umi=umi, tb=tb: G.tensor_tensor(out=tb, in0=s0, in1=umi, op=ALU.mult), **rw)
                pl(lambda m=m, ta=ta, tb=tb: G.tensor_tensor(out=Ec[:, :, m:2 * m], in0=ta, in1=tb, op=ALU.subtract), **rw)
                pl(lambda e0=e0, umi=umi, ta=ta: G.tensor_tensor(out=ta, in0=e0, in1=umi, op=ALU.mult), **rw)
                pl(lambda s0=s0, umr=umr, tb=tb: G.tensor_tensor(out=tb, in0=s0, in1=umr, op=ALU.mult), **rw)
                pl(lambda m=m, ta=ta, tb=tb: G.tensor_tensor(out=Es[:, :, m:2 * m], in0=ta, in1=tb, op=ALU.add), **rw)
                m *= 2
            pl(lambda: G.tensor_copy(out=Rf[:, :, 1:NCH], in_=R_sb[:].unsqueeze(2).to_broadcast([128, 16, NCH - 1])),
               r=[B_const], w=[B_const])
            emit_casts(10, nslot)
            a1r, den, u1, u2, u3, u4 = t16
            pz = dict(r=[P, P_trig], w=[P])
            dv(lambda: V.tensor_scalar(out=a1r[:], in0=Pr[:, 1, :], scalar1=-1.0, scalar2=None, op0=ALU.add), **pz)
            dv(lambda: V.tensor_tensor(out=u1[:], in0=lr[:], in1=lr[:], op=ALU.mult), **pz)
            dv(lambda: V.tensor_tensor(out=u2[:], in0=li[:], in1=li[:], op=ALU.mult), **pz)
            dv(lambda: V.tensor_tensor(out=den[:], in0=u1[:], in1=u2[:], op=ALU.add), **pz)
            dv(lambda: V.reciprocal(out=den[:], in_=den[:]), **pz)
            dv(lambda: V.tensor_tensor(out=u1[:], in0=a1r[:], in1=lr[:], op=ALU.mult), **pz)
            dv(lambda: V.tensor_tensor(out=u2[:], in0=Pi[:, 1, :], in1=li[:], op=ALU.mult), **pz)
            dv(lambda: V.tensor_tensor(out=u1[:], in0=u1[:], in1=u2[:], op=ALU.add), **pz)
            dv(lambda: V.tensor_tensor(out=zr[:], in0=u1[:], in1=den[:], op=ALU.mult), **pz)
            dv(lambda: V.tensor_tensor(out=u3[:], in0=Pi[:, 1, :], in1=lr[:], op=ALU.mult), **pz)
            dv(lambda: V.tensor_tensor(out=u4[:], in0=a1r[:], in1=li[:], op=ALU.mult), **pz)
            dv(lambda: V.tensor_tensor(out=u3[:], in0=u3[:], in1=u4[:], op=ALU.subtract), **pz)
            dv(lambda: V.tensor_tensor(out=zi[:], in0=u3[:], in1=den[:], op=ALU.mult), **pz)

            def bc(ap16):
                return ap16.unsqueeze(2).to_broadcast([128, 16, 32])

            pb_ = dict(r=[P, P_B], w=[P_B])
            dv(lambda: V.tensor_tensor(out=tA[:], in0=BTr[:], in1=bc(zr[:]), op=ALU.mult), **pb_)
            dv(lambda: V.tensor_tensor(out=tB[:], in0=BTi[:], in1=bc(zi[:]), op=ALU.mult), **pb_)
            dv(lambda: V.tensor_tensor(out=BbTr[:], in0=tA[:], in1=tB[:], op=ALU.subtract), **pb_)
            dv(lambda: V.tensor_tensor(out=tA[:], in0=BTr[:], in1=bc(zi[:]), op=ALU.mult), **pb_)
            dv(lambda: V.tensor_tensor(out=tB[:], in0=BTi[:], in1=bc(zr[:]), op=ALU.mult), **pb_)
            dv(lambda: V.tensor_tensor(out=BbTi[:], in0=tA[:], in1=tB[:], op=ALU.add), **pb_)

            def big_cmul(n, out_r, out_i, ar3, ai3, neg_i, rds, wrs):
                ta, tb = tA9[:, 0:n], tB9[:, 0:n]
                ab = lambda a: a.unsqueeze(1).to_broadcast([128, n, 16, 32])
                pr = Pr[:, 0:n, :].unsqueeze(3).to_broadcast([128, n, 16, 32])
                pi = Pi[:, 0:n, :].unsqueeze(3).to_broadcast([128, n, 16, 32])
                kw = dict(r=list(rds) + [P_trig], w=list(wrs))
                dv(lambda: V.tensor_tensor(out=ta, in0=ab(ar3), in1=pr, op=ALU.mult), **kw)
                dv(lambda: V.tensor_tensor(out=tb, in0=ab(ai3), in1=pi, op=ALU.mult), **kw)
                dv(lambda: V.tensor_tensor(out=out_r, in0=ta, in1=tb, op=ALU.subtract), **kw)
                dv(lambda: V.tensor_tensor(out=ta, in0=ab(ar3), in1=pi, op=ALU.mult), **kw)
                dv(lambda: V.tensor_tensor(out=tb, in0=ab(ai3), in1=pr, op=ALU.mult), **kw)
                if neg_i:
                    dv(lambda: V.scalar_tensor_tensor(out=out_i, in0=ta, scalar=-1.0, in1=tb, op0=ALU.mult, op1=ALU.subtract), **kw)
                else:
                    dv(lambda: V.tensor_tensor(out=out_i, in0=ta, in1=tb, op=ALU.add), **kw)

            big_cmul(9, G32[:, :, 0], G32[:, :, 1], CTr[:], CTi[:], True, [P_G], [P_G])
            big_cmul(8, Qb[:, :, 0], Qb[:, :, 1], BbTr[:], BbTi[:], False, [P_B, P_G], [P_Q, P_G])
            for tau in range(8):
                ac(lambda tau=tau: A.activation(out=Wout_sb[:, :, tau, :, :].rearrange("p q r c -> p r q c"),
                                                in_=G32[:, tau + 1], func=AF.Copy), r=[P_G], w=[B_const])
            for k in range(4):
                for ri in range(2):
                    pst, psb = next_tr()
                    pv = pst[:].bitcast(BF16).rearrange("p (s c) -> p s c", s=8)
                    S.group("pe", [P_Q, B_const], [psb], [
                        (lambda s=s, k=k, ri=ri, pv=pv: PE.transpose(
                            pv[:, s, :], Qb[:, 7 - s, ri, 4 * k:4 * k + 4, :].rearrange("p q c -> p (q c)"), ident[:]))
                        for s in range(8)])
                    S.op("act", [psb], [B_const], lambda k=k, ri=ri, pv=pv: A.activation(
                        out=Win_sb[:, k, :, ri, :], in_=pv, func=AF.Copy))
            for k in range(4):
                pst, psb = next_mm()
                pv = pst[:, 0:256].rearrange("p (j c) -> p j c", j=8)
                fns = []
                for i in range(4):
                    q = 4 * k + i
                    tp = (0, 32 * i)
                    fns.append(lambda i=i, q=q, tp=tp, pv=pv: PE.matmul(
                        pv[32 * i:32 * i + 32], lhsT=BbTr[:, q, :], rhs=G32[:, 0:8, 0, q, :],
                        start=True, stop=False, tile_position=tp))
                    fns.append(lambda i=i, q=q, tp=tp, pv=pv: PE.matmul(
                        pv[32 * i:32 * i + 32], lhsT=BbTi[:, q, :], rhs=G32[:, 0:8, 1, q, :],
                        start=False, stop=True, tile_position=tp))
                S.group("pe", [P_G, P_B], [psb], fns)
                for i in range(4):
                    S.op("act", [psb], [B_const], lambda i=i, k=k, pv=pv: A.activation(
                        out=Toep_sb[32 * i:32 * i + 32, k, :, 32 * i:32 * i + 32], in_=pv[32 * i:32 * i + 32], func=AF.Copy))
            S.barrier()
        ring = sbt("ring", [128, NRING, SLOT], BF16)
        ringB = [Buf("ring%d" % i) for i in range(NRING)]
        xT2 = [sbt("xT%d" % i, [128, 8, NT], BF16) for i in range(2)]; B_xT2 = [Buf("xT%d" % i) for i in range(2)]
        pT2 = [sbt("pT%d" % i, [128, 2, NT], BF16) for i in range(2)]; B_pT2 = [Buf("pT%d" % i) for i in range(2)]
        zded = [sbt("zded%d" % i, [128, D], F32) for i in range(2)]; B_zded = [Buf("zded%d" % i) for i in range(2)]
        ssm_u = sbt("ssm_u", [128, 4, 8, NCH], BF16); B_u = [Buf("u%d" % k) for k in range(4)]
        ssm_sg = sbt("ssm_sg", [128, 4, NT], BF16); B_sg = [Buf("sg%d" % k) for k in range(4)]
        Xp = [sbt("Xp%d" % r, [128, 16, NCH + 1], BF16) for r in range(2)]; B_Xp = [Buf("Xp_h%d" % h) for h in range(2)]
        Xc = [sbt("Xc%d" % r, [128, 16], F32) for r in range(2)]; B_Xc = [Buf("Xc_h%d" % h) for h in range(2)]
        RXc = [sbt("RXc%d" % r, [128, 16], F32) for r in range(2)]
        gbf = sbt("gbf", [128, 4, NT], BF16); B_g = [Buf("g%d" % k) for k in range(4)]
        y_ssm = sbt("y_ssm", [128, 4, NT], BF16); B_ys = [Buf("ys%d" % k) for k in range(4)]
        halo = sbt("halo", [128, 8, 16], F32); B_halo = [Buf("halo%d" % c) for c in range(8)]
        dbuf = [sbt("dbuf%d" % i, [128, 2, NT], BF16) for i in range(2)]; B_d = [[Buf("d%d_%d" % (i, c)) for c in range(2)] for i in range(2)]
        sgate = [sbt("sgate%d" % i, [128, 2, NT], BF16) for i in range(2)]; B_sgt = [[Buf("sgt%d_%d" % (i, c)) for c in range(2)] for i in range(2)]
        y_pool = sbt("y_pool", [128, 8, NT], BF16); B_yp = [Buf("yp%d" % c) for c in range(8)]
        merged = sbt("merged", [128, 8, NT], BF16); B_mg = [Buf("mg%d" % c) for c in range(8)]
        tmpc = sbt("tmpc", [128, 16], F32); B_tmpc = Buf("tmpc")
        lnst = sbt("lnst", [128, 4, 2, 6], F32); lnmv = sbt("lnmv", [128, 4, 2], F32)
        lnr = sbt("lnr", [128, 4], F32); lnn = sbt("lnn", [128, 4], F32); B_ln = Buf("ln")
        chainbufs = [(sbt("cb%d" % i, [128, 512], F32), Buf("cb%d" % i)) for i in range(6)]
        tmp = [sbt("tmp%d" % i, [128, TMPW], F32) for i in range(NTMP)]
        tmpB = [Buf("tmp%d" % i) for i in range(NTMP)]

        def next_tmp():
            i = rr["tmp"] % NTMP
            rr["tmp"] += 1
            return tmp[i], tmpB[i]

        z_ap = [y_pool[:, 0:4, :].rearrange("p a b -> p (a b)").bitcast(F32),
                y_pool[:, 4:8, :].rearrange("p a b -> p (a b)").bitcast(F32),
                zded[0][:], zded[1][:]]
        z_bufs = [B_yp[0:4], B_yp[4:8], [B_zded[0]], [B_zded[1]]]

        out_ap = out_t.ap()
        stream_pos = [0]

        def load_slot(expect_name):
            si = stream_pos[0] % nslot
            assert plan[si][0] == expect_name, (plan[si][0], expect_name)
            stream_pos[0] += 1
            r = rr["ring"] % NRING
            rr["ring"] += 1
            S.dma("sp", [scrB[si]], [ringB[r]], lambda: SP.dma_start(out=ring[:, r, :], in_=wscr_t.ap()[si]))
            return ring[:, r, :], ringB[r]

        cur = {}

        def proj_fm(slot_ap, slot_b, cl, evac):
            xT, B_xT = cur["xT"], cur["B_xT"]
            w = slot_ap.rearrange("p (kc n) -> p kc n", kc=8)
            pst, psb = next_mm()
            S.group("pe", [slot_b, B_xT], [psb], [
                (lambda kc=kc: PE.matmul(pst[:], lhsT=w[:, kc, cl * 128:(cl + 1) * 128], rhs=xT[:, kc, :],
                                         start=(kc == 0), stop=(kc == 7))) for kc in range(8)])
            evac(pst, psb)

        nmt = nseq * tps
        pending_tail = []

        def tail_items(b, tok0):
            items = []

            for t in range(4):
                S.dma("pool", [], z_bufs[t], lambda t=t: G.dma_start(
                    out=z_ap[t], in_=x_ap[b, tok0 + t * 128:tok0 + (t + 1) * 128, :], accum_op=ALU.add))

            def stats(t):
                for hh in range(2):
                    S.op("dve", z_bufs[t], [B_ln], lambda hh=hh: V.bn_stats(
                        out=lnst[:, t, hh, :], in_=z_ap[t][:, hh * 512:(hh + 1) * 512]))
                S.op("dve", [B_ln], [B_ln], lambda: V.bn_aggr(out=lnmv[:, t, :], in_=lnst[:, t]))

            def rstd():
                S.op("dve", [B_ln], [B_ln], lambda: V.tensor_scalar(
                    out=lnr[:], in0=lnmv[:, :, 1], scalar1=LN_EPS / (ALPHA * ALPHA), scalar2=None, op0=ALU.add))
                S.op("act", [B_ln], [B_ln], lambda: A.activation(out=lnr[:], in_=lnr[:], func=AF.Sqrt))
                S.op("dve", [B_ln], [B_ln], lambda: V.reciprocal(out=lnr[:], in_=lnr[:]))
                S.op("dve", [B_ln], [B_ln], lambda: V.scalar_tensor_tensor(
                    out=lnn[:], in0=lnmv[:, :, 0], scalar=-1.0, in1=lnr[:], op0=ALU.mult, op1=ALU.mult))

            def outp(t):
                S.op("act", [B_ln] + z_bufs[t], z_bufs[t], lambda: A.activation(
                    out=z_ap[t], in_=z_ap[t], func=AF.Identity, scale=lnr[:, t:t + 1], bias=lnn[:, t:t + 1]))
                S.op("dve", [B_const] + z_bufs[t], z_bufs[t], lambda: V.tensor_tensor(
                    out=z_ap[t], in0=z_ap[t], in1=lng_sb[:], op=ALU.mult))
                if t % 2 == 0:
                    S.op("dve", [B_const] + z_bufs[t], z_bufs[t], lambda: V.tensor_tensor(
                        out=z_ap[t], in0=z_ap[t], in1=lnb_sb[:], op=ALU.add))
                else:
                    S.op("pool", [B_const] + z_bufs[t], z_bufs[t], lambda: G.tensor_tensor(
                        out=z_ap[t], in0=z_ap[t], in1=lnb_sb[:], op=ALU.add))
                S.dma("pool", z_bufs[t], [], lambda: G.dma_start(
                    out=out_ap[b, tok0 + t * 128:tok0 + (t + 1) * 128, :], in_=z_ap[t]))
            for t in range(4):
                items.append(lambda t=t: stats(t))
            items.append(rstd)
            for t in range(4):
                items.append(lambda t=t: outp(t))
            return items

        def transpose_items(mt):
            xT, B_xT = xT2[mt % 2], B_xT2[mt % 2]
            pT, B_pT = pT2[mt % 2], B_pT2[mt % 2]
            items = []

            def xitem(t):
                pst, psb = next_tr()
                pv = pst[:].bitcast(BF16).rearrange("p (k c) -> p k c", k=8)
                S.group("pe", [B_xbf, B_const], [psb], [
                    (lambda kc=kc, t=t, pv=pv: PE.transpose(pv[:, kc, :], xbf[:, t, kc * 128:(kc + 1) * 128], ident[:]))
                    for kc in range(8)])
                S.op("act", [psb], [B_xT], lambda t=t, pv=pv: A.activation(
                    out=xT[:, :, t * 128:(t + 1) * 128], in_=pv, func=AF.Copy))

            def pitem():
                pst, psb = next_tr()
                pv = pst[:].bitcast(BF16).rearrange("p (t k c) -> p t k c", t=4, k=2)
                S.group("pe", [B_pbf, B_const], [psb], [
                    (lambda kc=kc, t=t, pv=pv: PE.transpose(pv[:, t, kc, :], pbf[:, t, kc * 128:(kc + 1) * 128], ident[:]))
                    for t in range(4) for kc in range(2)])
                for kc in range(2):
                    S.op("act", [psb], [B_pT], lambda kc=kc, pv=pv: A.activation(
                        out=pT[:, kc, :].rearrange("p (t c) -> p t c", t=4), in_=pv[:, :, kc, :], func=AF.Copy))
            for t in range(4):
                items.append(lambda t=t: xitem(t))
            items.append(pitem)
            return items

        def emit_transposes(mt):
            for it in transpose_items(mt):
                it()
        if phase < 1:
            nmt = 0
        for mt in range(nmt):
            b, ti = divmod(mt, tps)
            tok0 = ti * NT
            seq_start = (ti == 0)
            xT, B_xT = xT2[mt % 2], B_xT2[mt % 2]
            pT, B_pT = pT2[mt % 2], B_pT2[mt % 2]
            cur["xT"], cur["B_xT"] = xT, B_xT
            if mt == 0:
                emit_transposes(0)
                if nmt > 1:
                    load_x(1)
            if seq_start:
                S.op("pool", [], B_halo, lambda: G.memset(halo[:], 0.0))
            if phase < 2:
                continue
            for half in range(2):
                sl, slb = load_slot("ssm_in_%d" % half)
                for cl in range(2):
                    k = 2 * half + cl

                    def ev(pst, psb, k=k):
                        S.op("act", [psb], [B_u[k]], lambda: A.activation(
                            out=ssm_u[:, k], in_=pst[:].rearrange("p (c s) -> p s c", s=8), func=AF.Copy))
                    proj_fm(sl, slb, cl, ev)
                    if pending_tail:
                        pending_tail.pop(0)()
            for half in range(2):
                sl, slb = load_slot("ssm_gate_%d" % half)
                for cl in range(2):
                    k = 2 * half + cl

                    def ev(pst, psb, k=k):
                        S.op("act", [psb], [B_sg[k]], lambda: A.activation(out=ssm_sg[:, k, :], in_=pst[:], func=AF.Silu))
                    proj_fm(sl, slb, cl, ev)
                    if pending_tail:
                        pending_tail.pop(0)()
            while pending_tail:
                pending_tail.pop(0)()
            tail2 = []
            if phase < 3:
                continue
            def s5_half_items(h2):
                q0 = 8 * h2
                F = slice(0, 512)
                ech = Ec[:, q0:q0 + 8, :].rearrange("p q c -> p (q c)")
                esh = Es[:, q0:q0 + 8, :].rearrange("p q c -> p (q c)")
                rfh = Rf[:, q0:q0 + 8, :].rearrange("p q c -> p (q c)")
                (t1, t1b), (t2, t2b), (t3, t3b), (t4, t4b), (t5, t5b), (t6, t6b) = chainbufs
                st = {}

                def q3(tl):
                    return tl[:, F].rearrange("p (q c) -> p q c", q=8)

                def pe_stage():
                    vb = [next_mm() for _ in range(4)]
                    st["vb"] = vb
                    fns = []
                    for kk in range(2):
                        k = 2 * h2 + kk
                        for s_ in range(8):
                            for i in range(4):
                                for ri in range(2):
                                    first = (kk == 0 and s_ == 0 and ri == 0)
                                    col = (kk * 2 + ri) * NCH
                                    fns.append(lambda k=k, s_=s_, i=i, ri=ri, col=col, first=first: PE.matmul(
                                        vb[i][0][:, col:col + NCH],
                                        lhsT=Win_sb[32 * i:32 * i + 32, k, s_, ri, :],
                                        rhs=ssm_u[32 * i:32 * i + 32, k, s_, :],
                                        start=first, stop=False, tile_position=(32 * i, 0), skip_group_check=True))
                    S.group("pe", [B_const, B_u[2 * h2], B_u[2 * h2 + 1]], [b for _, b in vb], fns)

                def st1():
                    vb = st["vb"]
                    for i in range(4):
                        bv = vb[i][0][:, 0:4 * NCH].rearrange("p (kk ri c) -> p kk ri c", kk=2, ri=2)
                        vr_i, vi_i = bv[:, :, 0, :], bv[:, :, 1, :]
                        ec_i = Ec[:, q0 + i:q0 + 8:4, :]
                        es_i = Es[:, q0 + i:q0 + 8:4, :]
                        pb = vb[i][1]
                        S.op("dve", [pb, B_const], [t1b], lambda i=i, vr_i=vr_i, ec_i=ec_i: V.tensor_tensor(out=q3(t1)[:, i:8:4, :], in0=vr_i, in1=ec_i, op=ALU.mult))
                        S.op("dve", [pb, B_const], [t2b], lambda i=i, vi_i=vi_i, es_i=es_i: V.tensor_tensor(out=q3(t2)[:, i:8:4, :], in0=vi_i, in1=es_i, op=ALU.mult))
                        S.op("dve", [pb, B_const], [t3b], lambda i=i, vi_i=vi_i, ec_i=ec_i: V.tensor_tensor(out=q3(t3)[:, i:8:4, :], in0=vi_i, in1=ec_i, op=ALU.mult))
                        S.op("dve", [pb, B_const], [t4b], lambda i=i, vr_i=vr_i, es_i=es_i: V.tensor_tensor(out=q3(t4)[:, i:8:4, :], in0=vr_i, in1=es_i, op=ALU.mult))

                def st2():
                    S.op("pool", [t1b, t2b], [t1b], lambda: G.tensor_tensor(out=t1[:, F], in0=t1[:, F], in1=t2[:, F], op=ALU.add))
                    S.op("pool", [t3b, t4b], [t3b], lambda: G.tensor_tensor(out=t3[:, F], in0=t3[:, F], in1=t4[:, F], op=ALU.subtract))
                    if seq_start:
                        for r in range(2):
                            S.op("pool", [], [B_Xp[h2]], lambda r=r: G.memset(Xp[r][:, q0:q0 + 8, 0:1], 0.0))
                    else:
                        for r, (vm, vmb) in enumerate(((t1, t1b), (t3, t3b))):
                            S.op("pool", [B_Xc[h2]], [B_Xp[h2]], lambda r=r: G.tensor_copy(
                                out=Xp[r][:, q0:q0 + 8, 0:1], in_=Xc[r][:, q0:q0 + 8].unsqueeze(2)))
                            v3 = q3(vm)[:, :, 0:1]
                            S.op("pool", [B_Xc[h2], vmb], [vmb], lambda r=r, v3=v3: G.tensor_tensor(
                                out=v3, in0=v3, in1=RXc[r][:, q0:q0 + 8].unsqueeze(2), op=ALU.add))

                def st3():
                    S.op("dve", [t1b, B_const], [t2b], lambda: V.tensor_tensor_scan(
                        out=t2[:, F], data0=rfh, data1=t1[:, F], initial=0.0, op0=ALU.mult, op1=ALU.add))
                    S.op("dve", [t3b, B_const], [t4b], lambda: V.tensor_tensor_scan(
                        out=t4[:, F], data0=rfh, data1=t3[:, F], initial=0.0, op0=ALU.mult, op1=ALU.add))

                def st4():
                    S.op("pool", [t2b, B_const], [t1b], lambda: G.tensor_tensor(out=t1[:, F], in0=t2[:, F], in1=ech, op=ALU.mult))
                    S.op("dve", [t2b, B_const], [t5b], lambda: V.tensor_tensor(out=t5[:, F], in0=t2[:, F], in1=esh, op=ALU.mult))
                    S.op("pool", [t4b, B_const], [t3b], lambda: G.tensor_tensor(out=t3[:, F], in0=t4[:, F], in1=esh, op=ALU.mult))
                    S.op("dve", [t4b, B_const], [t6b], lambda: V.tensor_tensor(out=t6[:, F], in0=t4[:, F], in1=ech, op=ALU.mult))

                def st5():
                    S.op("pool", [t1b, t3b], [t1b], lambda: G.tensor_tensor(out=t1[:, F], in0=t1[:, F], in1=t3[:, F], op=ALU.subtract))
                    S.op("dve", [t5b, t6b], [t5b], lambda: V.tensor_tensor(out=t5[:, F], in0=t5[:, F], in1=t6[:, F], op=ALU.add))

                def st6():
                    for r, (vm, vmb) in enumerate(((t1, t1b), (t5, t5b))):
                        v3 = q3(vm)
                        S.op("act", [vmb], [B_Xp[h2]], lambda r=r, v3=v3: A.activation(
                            out=Xp[r][:, q0:q0 + 8, 1:NCH + 1], in_=v3, func=AF.Copy))
                        S.op("pool", [vmb], [B_Xc[h2]], lambda r=r, v3=v3: G.tensor_copy(
                            out=Xc[r][:, q0:q0 + 8].unsqueeze(2), in_=v3[:, :, NCH - 1:NCH]))
                        S.op("dve", [B_Xc[h2], B_const], [B_Xc[h2]], lambda r=r: V.tensor_tensor(
                            out=RXc[r][:, q0:q0 + 8], in0=Xc[r][:, q0:q0 + 8], in1=R_sb[:, q0:q0 + 8], op=ALU.mult))
                return [pe_stage, st1, st2, st3, st4, st5, st6]

            def pool_w_mm(gi, di):
                for oc in range(2):
                    pst, psb = next_mm()
                    S.group("pe", [B_const, B_d[di][0], B_d[di][1]], [psb], [
                        (lambda ic=ic, oc=oc, pst=pst: PE.matmul(
                            pst[:], lhsT=pw_sb[:, gi, ic, oc * 128:(oc + 1) * 128], rhs=dbuf[di][:, ic, :],
                            start=(ic == 0), stop=(ic == 1))) for ic in range(2)])
                    cc = 2 * gi + oc
                    S.op("dve", [psb, B_const, B_sgt[di][oc]], [B_yp[cc]], lambda pst=pst, cc=cc, oc=oc: V.scalar_tensor_tensor(
                        out=y_pool[:, cc, :], in0=pst[:], scalar=pscale[:, cc:cc + 1], in1=sgate[di][:, oc, :],
                        op0=ALU.mult, op1=ALU.mult))

            def pool_items():
                items = []
                slots = {}
                for gi in (3, 2, 1, 0):
                    di = gi % 2
                    w = POOL_WINDOWS[gi]
                    for cl in range(2):
                        cc = 2 * gi + cl

                        def item_in(gi=gi, cl=cl, cc=cc, w=w, di=di):
                            if cl == 0:
                                slots["in"] = load_slot("pool_in_%d" % gi)
                            sl, slb = slots["in"]

                            def ev(pst, psb):
                                ub, ubb = next_tmp()
                                S.op("pool", [B_halo[cc]], [ubb], lambda: G.tensor_copy(out=ub[:, 0:16], in_=halo[:, cc, :]))
                                S.op("act", [psb], [ubb], lambda: A.activation(out=ub[:, 16:16 + NT], in_=pst[:], func=AF.Copy))
                                S.op("pool", [ubb], [B_halo[cc]], lambda: G.tensor_copy(out=halo[:, cc, :], in_=ub[:, NT:NT + 16]))
                                cur_, curb = ub, ubb
                                sh = 1
                                while sh < w:
                                    nx, nxb = next_tmp()
                                    lo = 2 * sh - 1
                                    if cl == 0:
                                        S.op("dve", [curb], [nxb], lambda cur_=cur_, nx=nx, lo=lo, sh=sh: V.tensor_tensor(
                                            out=nx[:, lo:TMPW], in0=cur_[:, lo:TMPW], in1=cur_[:, lo - sh:TMPW - sh], op=ALU.add))
                                    else:
                                        S.op("pool", [curb], [nxb], lambda cur_=cur_, nx=nx, lo=lo, sh=sh: G.tensor_tensor(
                                            out=nx[:, lo:TMPW], in0=cur_[:, lo:TMPW], in1=cur_[:, lo - sh:TMPW - sh], op=ALU.add))
                                    cur_, curb = nx, nxb
                                    sh *= 2
                                S.op("dve", [curb, ubb], [B_d[di][cl]], lambda cur_=cur_: V.scalar_tensor_tensor(
                                    out=dbuf[di][:, cl, :], in0=cur_[:, 16:16 + NT], scalar=1.0 / w, in1=ub[:, 16:16 + NT],
                                    op0=ALU.mult, op1=ALU.subtract))
                                if seq_start:
                                    S.op("dve", [curb, B_const], [B_tmpc], lambda cur_=cur_: V.tensor_tensor(
                                        out=tmpc[:, 0:w - 1], in0=cur_[:, 16:16 + w - 1], in1=invc[:, 0:w - 1], op=ALU.mult))
                                    S.op("dve", [B_tmpc, ubb], [B_d[di][cl]], lambda: V.tensor_tensor(
                                        out=dbuf[di][:, cl, 0:w - 1], in0=tmpc[:, 0:w - 1], in1=ub[:, 16:16 + w - 1], op=ALU.subtract))
                            proj_fm(sl, slb, cl, ev)
                        items.append(item_in)
                    for cl in range(2):
                        def item_gate(gi=gi, cl=cl, di=di):
                            if cl == 0:
                                slots["gate"] = load_slot("pool_gate_%d" % gi)
                            sl, slb = slots["gate"]

                            def ev(pst, psb):
                                S.op("act", [psb], [B_sgt[di][cl]], lambda: A.activation(out=sgate[di][:, cl, :], in_=pst[:], func=AF.Silu))
                            proj_fm(sl, slb, cl, ev)
                        items.append(item_gate)
                    if gi < 3:
                        items.append(lambda gi=gi: pool_w_mm(gi + 1, (gi + 1) % 2))
                return items

            chain = s5_half_items(0) + s5_half_items(1)
            ditems = pool_items()
            chain.pop(0)()
            chain.pop(0)()
            nd = 0
            while chain or ditems or tail2:
                if ditems:
                    ditems.pop(0)()
                    nd += 1
                if chain:
                    chain.pop(0)()
                if tail2 and (nd % 3 == 0 or not ditems):
                    tail2.pop(0)()
            if phase < 5:
                continue
            for k in range(4):
                h2 = k // 2
                yt, yb = next_mm()
                yv = yt[:].rearrange("p (t c) -> p t c", t=8)
                fns = []
                for tau in range(8):
                    for s_ in range(tau + 1):
                        for i in range(4):
                            fns.append(lambda tau=tau, s_=s_, i=i: PE.matmul(
                                yv[32 * i:32 * i + 32, tau, :],
                                lhsT=Toep_sb[32 * i:32 * i + 32, k, tau - s_, 32 * i:32 * i + 32],
                                rhs=ssm_u[32 * i:32 * i + 32, k, s_, :],
                                start=(tau == 0 and s_ == 0), stop=False,
                                tile_position=(32 * i, 32 * i), skip_group_check=True))
                    for i in range(4):
                        q = 4 * k + i
                        for ri in range(2):
                            fns.append(lambda tau=tau, i=i, q=q, ri=ri: PE.matmul(
                                yv[32 * i:32 * i + 32, tau, :], lhsT=Wout_sb[:, q, tau, ri, :], rhs=Xp[ri][:, q, 0:NCH],
                                start=False, stop=False, tile_position=(0, 32 * i), skip_group_check=True))
                S.group("pe", [B_const, B_u[k], B_Xp[h2]], [yb], fns)
                if k == 0:
                    pool_w_mm(0, 0)
                y32, y32b = next_tmp(); qq, qqb = next_tmp(); sg, sgb = next_tmp()
                F = slice(0, NT)
                S.op("dve", [yb, B_u[k], B_const], [y32b], lambda k=k, yv=yv, y32=y32: V.scalar_tensor_tensor(
                    out=y32[:, F].rearrange("p (c s) -> p c s", s=8), in0=ssm_u[:, k].rearrange("p s c -> p c s"),
                    scalar=dsk[:, k:k + 1], in1=yv.rearrange("p t c -> p c t"), op0=ALU.mult, op1=ALU.add))
                S.op("act", [y32b], [qqb], lambda y32=y32, qq=qq: A.activation(out=qq[:, F], in_=y32[:, F], func=AF.Square))
                S.op("dve", [qqb], [qqb], lambda qq=qq: V.tensor_scalar(
                    out=qq[:, F], in0=qq[:, F], scalar1=GELU_C1, scalar2=1.0, op0=ALU.mult, op1=ALU.add))
                S.op("dve", [qqb, y32b], [qqb], lambda qq=qq, y32=y32: V.tensor_tensor(
                    out=qq[:, F], in0=qq[:, F], in1=y32[:, F], op=ALU.mult))
                S.op("act", [qqb], [sgb], lambda qq=qq, sg=sg: A.activation(
                    out=sg[:, F], in_=qq[:, F], func=AF.Sigmoid, scale=2.0 * GELU_C0))
                S.op("dve", [sgb, y32b], [B_g[k]], lambda k=k, sg=sg, y32=y32: V.tensor_tensor(
                    out=gbf[:, k, :], in0=y32[:, F], in1=sg[:, F], op=ALU.mult))
            tr_items = []
            if mt + 1 < nmt:
                for it in transpose_items(mt + 1):
                    it()
                if mt + 2 < nmt:
                    load_x(mt + 2)
            if phase < 6:
                continue
            for oc in range(4):
                pst, psb = next_mm()
                S.group("pe", [B_const] + B_g, [psb], [
                    (lambda kc=kc, oc=oc, pst=pst: PE.matmul(
                        pst[:], lhsT=gluw_sb[:, kc, oc * 128:(oc + 1) * 128], rhs=gbf[:, kc, :],
                        start=(kc == 0), stop=(kc == 3))) for kc in range(4)])
                s1, s1b = next_tmp()
                F = slice(0, NT)
                S.op("act", [psb, B_const], [s1b], lambda pst=pst, s1=s1, oc=oc: A.activation(
                    out=s1[:, F], in_=pst[:], func=AF.Sigmoid, bias=glub[:, oc:oc + 1]))
                S.op("dve", [s1b, B_g[oc]], [s1b], lambda s1=s1, oc=oc: V.tensor_tensor(
                    out=s1[:, F], in0=s1[:, F], in1=gbf[:, oc, :], op=ALU.mult))
                S.op("pool", [s1b, B_sg[oc]], [B_ys[oc]], lambda s1=s1, oc=oc: G.tensor_tensor(
                    out=y_ssm[:, oc, :], in0=s1[:, F], in1=ssm_sg[:, oc, :], op=ALU.mult))
            if phase < 7:
                continue
            wbs = None
            for jp in range(4):
                gp_sl, gp_b = load_slot("g_pool_%d" % jp)
                gs_sl, gs_b = load_slot("g_ssm_%d" % jp)
                bp_sl, bp_b = load_slot("w_bp_%d" % jp)
                if jp % 2 == 0:
                    wbs = load_slot("w_bs_%d" % (jp // 2))
                bs_sl, bs_b = wbs
                bpw = bp_sl.rearrange("p (kc n) -> p kc n", kc=8)
                bsw = bs_sl.rearrange("p (kc n) -> p kc n", kc=4)
                for cl in range(2):
                    j = 2 * jp + cl
                    F = slice(0, NT)
                    gates = []
                    for (sl_, b_) in ((gp_sl, gp_b), (gs_sl, gs_b)):
                        gt, gtb = next_tmp()

                        def ev(pst, psb, gt=gt, gtb=gtb):
                            S.op("act", [psb], [gtb], lambda: A.activation(out=gt[:, F], in_=pst[:], func=AF.Sigmoid))
                        proj_fm(sl_, b_, cl, ev)
                        gates.append((gt, gtb))
                    pa, pab = next_mm()
                    S.group("pe", [bp_b] + B_yp, [pab], [
                        (lambda kc=kc, pa=pa: PE.matmul(pa[:], lhsT=bpw[:, kc, cl * 128:(cl + 1) * 128], rhs=y_pool[:, kc, :],
                                                        start=(kc == 0), stop=(kc == 7))) for kc in range(8)])
                    pb, pbb = next_mm()
                    co = (jp % 2) * 256 + cl * 128
                    S.group("pe", [bs_b] + B_ys, [pbb], [
                        (lambda kc=kc, pb=pb, co=co: PE.matmul(pb[:], lhsT=bsw[:, kc, co:co + 128], rhs=y_ssm[:, kc, :],
                                                               start=(kc == 0), stop=(kc == 3))) for kc in range(4)])
                    m1, m1b = next_tmp(); m2, m2b = next_tmp()
                    S.op("dve", [pab, gates[0][1]], [m1b], lambda pa=pa, m1=m1, g=gates[0][0]: V.tensor_tensor(
                        out=m1[:, F], in0=pa[:], in1=g[:, F], op=ALU.mult))
                    S.op("dve", [pbb, gates[1][1]], [m2b], lambda pb=pb, m2=m2, g=gates[1][0]: V.tensor_tensor(
                        out=m2[:, F], in0=pb[:], in1=g[:, F], op=ALU.mult))
                    S.op("pool", [m1b, m2b], [B_mg[j]], lambda j=j, m1=m1, m2=m2: G.tensor_tensor(
                        out=merged[:, j, :], in0=m1[:, F], in1=m2[:, F], op=ALU.add))
                    if tr_items:
                        tr_items.pop(0)()
                        if not tr_items and mt + 2 < nmt:
                            load_x(mt + 2)
            if phase < 8:
                continue
            for h in range(2):
                pg = [load_slot("ple_gate_%d_lo" % h), load_slot("ple_gate_%d_hi" % h)]
                wo = [load_slot("w_out_%d_lo" % h), load_slot("w_out_%d_hi" % h)]
                pgw = [a.rearrange("p (kc n) -> p kc n", kc=4) for a, _ in pg]
                wow = [a.rearrange("p (kc n) -> p kc n", kc=4) for a, _ in wo]
                F = slice(0, NT)
                for t in range(4):
                    ts = slice(t * 128, (t + 1) * 128)
                    pgt, pgb = next_mm()
                    S.group("pe", [pg[0][1], pg[1][1], B_xT], [pgb], [
                        (lambda kc=kc, pgt=pgt, ts=ts: PE.matmul(pgt[:], lhsT=xT[:, kc, ts], rhs=pgw[kc // 4][:, kc % 4, :],
                                                                 start=(kc == 0), stop=(kc == 7))) for kc in range(8)])
                    sgp, sgpb = next_tmp()
                    S.op("act", [pgb], [sgpb], lambda pgt=pgt, sgp=sgp: A.activation(out=sgp[:, F], in_=pgt[:], func=AF.Sigmoid))
                    ppt, ppb = next_mm()
                    S.group("pe", [B_const, B_pT], [ppb], [
                        (lambda kc=kc, ppt=ppt, ts=ts, h=h: PE.matmul(ppt[:], lhsT=pT[:, kc, ts], rhs=wple_sb[:, kc, h * 512:(h + 1) * 512],
                                                                      start=(kc == 0), stop=(kc == 1))) for kc in range(2)])
                    S.op("dve", [ppb, sgpb], [sgpb], lambda ppt=ppt, sgp=sgp: V.scalar_tensor_tensor(
                        out=sgp[:, F], in0=ppt[:], scalar=1.0 / ALPHA, in1=sgp[:, F], op0=ALU.mult, op1=ALU.mult))
                    pmt, pmb = next_mm()
                    S.group("pe", [wo[0][1], wo[1][1]] + B_mg, [pmb], [
                        (lambda kc=kc, pmt=pmt, ts=ts: PE.matmul(pmt[:], lhsT=merged[:, kc, ts], rhs=wow[kc // 4][:, kc % 4, :],
                                                                 start=(kc == 0), stop=(kc == 7))) for kc in range(8)])
                    S.op("dve", [pmb, sgpb], z_bufs[t], lambda pmt=pmt, sgp=sgp, t=t, h=h: V.scalar_tensor_tensor(
                        out=z_ap[t][:, h * 512:(h + 1) * 512], in0=pmt[:], scalar=1.0 / ALPHA, in1=sgp[:, F],
                        op0=ALU.mult, op1=ALU.add))
            pending_tail.extend(tail_items(b, tok0))
        for it in pending_tail:
            it()
        del pending_tail[:]
        S.final_wait("pool")
        S.final_wait("sp")
    return nc


_NC_CACHE = {}


def kernel(**inputs):
    ncores = 8
    x = np.ascontiguousarray(inputs["x"], dtype=np.float32)
    p = np.ascontiguousarray(inputs["p"], dtype=np.float32)[0]
    bsz, seqlen, _ = x.shape
    per = bsz // ncores
    key = (per, seqlen)
    if key not in _NC_CACHE:
        _NC_CACHE[key] = build_nc(per, seqlen)
    nc = _NC_CACHE[key]
    shared = {}
    for name in ("w_in", "pool_w", "pool_scale", "ssm_a_re", "ssm_a_im", "ssm_log_dt", "ssm_b_re", "ssm_b_im",
                 "ssm_c_re", "ssm_c_im", "ssm_d", "glu_w", "glu_b", "w_branch_pool", "w_branch_ssm", "w_out",
                 "w_ple", "ln_g", "ln_b"):
        shared[name] = np.ascontiguousarray(np.asarray(inputs[name], dtype=np.float32)[0])
    in_maps = []
    for c in range(ncores):
        m = dict(shared)
        m["x"] = np.ascontiguousarray(x[c * per:(c + 1) * per])
        m["p"] = np.ascontiguousarray(p[c * per:(c + 1) * per])
        in_maps.append(m)
    res = run_bass_kernel_spmd(nc, in_maps, core_ids=list(range(ncores)))
    return np.concatenate([np.asarray(r["out"]) for r in res.results], axis=0).astype(np.float32)
```

```python
import math
from contextlib import ExitStack

import numpy as np
import concourse.bass as bass
import concourse.mybir as mybir
from concourse.bass_utils import run_bass_kernel_spmd

F32 = mybir.dt.float32
BF16 = mybir.dt.bfloat16
I32 = mybir.dt.int32
AF = mybir.ActivationFunctionType
ALU = mybir.AluOpType

D = 1024
PLE = 256
SSM_W = 512
NGRP = 32
NPAIR = 16
LCH = 8
NT = 512
NCH = NT // LCH
ALPHA = 2.0 ** 0.25
LN_EPS = 1e-5
TWO_PI = 2.0 * math.pi
GELU_C0 = math.sqrt(2.0 / math.pi)
GELU_C1 = 0.044715
POOL_WINDOWS = (2, 4, 8, 16)
SLOT = 2048
NRING = 8
NTMP = 6
TMPW = 528


class Buf:
    __slots__ = ("name", "w", "r")

    def __init__(self, name):
        self.name = name
        self.w = []
        self.r = []


class Sched:
    def __init__(self, nc, es):
        self.nc = nc
        self.engs = {"pe": nc.tensor, "dve": nc.vector, "act": nc.scalar, "pool": nc.gpsimd, "sp": nc.sync}
        self.sem = {e: es.enter_context(nc.semaphore("prog_" + e)) for e in ("pe", "dve", "act", "pool")}
        self.cnt = {e: 0 for e in ("pe", "dve", "act", "pool")}
        self.waited = {e: {} for e in self.engs}
        self.dsems = {
            "sp": [es.enter_context(nc.semaphore("dsp%d" % i)) for i in range(8)],
            "pool": [es.enter_context(nc.semaphore("dpl%d" % i)) for i in range(14)],
        }
        self.dn = {"sp": 0, "pool": 0}
        self.last_dma_tok = {}

    def _wait(self, e, tok):
        key, sem, val = tok
        if e == "pe" and key == "pe":
            return
        if self.waited[e].get(key, 0) >= val:
            return
        self.engs[e].wait_ge(sem, val)
        self.waited[e][key] = val

    def _deps(self, e, reads, writes):
        for b in reads:
            for t in b.w:
                self._wait(e, t)
        for b in writes:
            for t in b.w:
                self._wait(e, t)
            for t in b.r:
                self._wait(e, t)

    @staticmethod
    def _add(lst, tok):
        for i, t in enumerate(lst):
            if t[0] == tok[0]:
                if t[2] < tok[2]:
                    lst[i] = tok
                return
        lst.append(tok)

    def _commit(self, tok, reads, writes):
        for b in reads:
            self._add(b.r, tok)
        for b in writes:
            b.w = [tok]
            b.r = []

    def op(self, e, reads, writes, fn):
        self._deps(e, reads, writes)
        ins = fn()
        self.cnt[e] += 1
        ins.then_inc(self.sem[e], 1)
        self._commit((e, self.sem[e], self.cnt[e]), reads, writes)

    def group(self, e, reads, writes, fns):
        self._deps(e, reads, writes)
        ins = None
        for f in fns:
            ins = f()
        self.cnt[e] += 1
        ins.then_inc(self.sem[e], 1)
        self._commit((e, self.sem[e], self.cnt[e]), reads, writes)

    def dma(self, q, reads, writes, fn):
        n = self.dn[q]
        sems = self.dsems[q]
        r = n % len(sems)
        prev = 16 * (n // len(sems))
        key = "d%s%d" % (q, r)
        if prev > 0:
            self._wait(q, (key, sems[r], prev))
        self._deps(q, reads, writes)
        ins = fn()
        ins.then_inc(sems[r], 16)
        self.dn[q] += 1
        tok = (key, sems[r], prev + 16)
        self.last_dma_tok[key] = tok
        self._commit(tok, reads, writes)

    def barrier(self):
        toks = [(e, self.sem[e], self.cnt[e]) for e in self.cnt if self.cnt[e] > 0]
        toks += [t for k, t in self.last_dma_tok.items() if k.startswith("dsp")]
        for e in self.engs:
            for t in toks:
                if t[0] == e:
                    continue
                self._wait(e, t)

    def final_wait(self, q="sp"):
        for t in self.last_dma_tok.values():
            self._wait(q, t)


def slot_plan():
    plan = []

    def win(name, col0, ncol=256):
        plan.append((name, [(0, 8, ncol, "w_in", 0, col0)]))

    win("ssm_in_0", 2048)
    win("ssm_in_1", 2304)
    win("ssm_gate_0", 2560)
    win("ssm_gate_1", 2816)
    for gi in (3, 2, 1, 0):
        win("pool_in_%d" % gi, gi * 256)
        win("pool_gate_%d" % gi, 1024 + gi * 256)
    for jp in range(4):
        win("g_pool_%d" % jp, 3072 + jp * 256)
        win("g_ssm_%d" % jp, 4096 + jp * 256)
        plan.append(("w_bp_%d" % jp, [(0, 8, 256, "w_bp", 0, jp * 256)]))
        if jp % 2 == 0:
            plan.append(("w_bs_%d" % (jp // 2), [(0, 4, 512, "w_bs", 0, (jp // 2) * 512)]))
    for h in range(2):
        plan.append(("ple_gate_%d_lo" % h, [(0, 4, 512, "w_in", 0, 5120 + h * 512)]))
        plan.append(("ple_gate_%d_hi" % h, [(0, 4, 512, "w_in", 512, 5120 + h * 512)]))
        plan.append(("w_out_%d_lo" % h, [(0, 4, 512, "w_out", 0, h * 512)]))
        plan.append(("w_out_%d_hi" % h, [(0, 4, 512, "w_out", 512, h * 512)]))
    return plan


def build_nc(nseq=2, seqlen=2048, phase=99.0):
    assert seqlen % NT == 0
    tps = seqlen // NT
    nc = bass.Bass("TRN2", target_bir_lowering=False)

    def din(name, shape):
        return nc.dram_tensor(name, list(shape), F32, kind="ExternalInput")

    x_t = din("x", [nseq, seqlen, D])
    p_t = din("p", [nseq, seqlen, PLE])
    w_in_t = din("w_in", [D, 6144])
    pool_w_t = din("pool_w", [4, 256, 256])
    pool_scale_t = din("pool_scale", [D])
    a_re_t = din("ssm_a_re", [NGRP, 64])
    a_im_t = din("ssm_a_im", [NGRP, 64])
    log_dt_t = din("ssm_log_dt", [NGRP])
    b_re_t = din("ssm_b_re", [NGRP, 64, 16])
    b_im_t = din("ssm_b_im", [NGRP, 64, 16])
    c_re_t = din("ssm_c_re", [NGRP, 16, 64])
    c_im_t = din("ssm_c_im", [NGRP, 16, 64])
    ssm_d_t = din("ssm_d", [SSM_W])
    glu_w_t = din("glu_w", [SSM_W, SSM_W])
    glu_b_t = din("glu_b", [SSM_W])
    w_bp_t = din("w_branch_pool", [D, D])
    w_bs_t = din("w_branch_ssm", [SSM_W, D])
    w_out_t = din("w_out", [D, D])
    w_ple_t = din("w_ple", [PLE, D])
    ln_g_t = din("ln_g", [D])
    ln_b_t = din("ln_b", [D])
    out_t = nc.dram_tensor("out", [nseq, seqlen, D], F32, kind="ExternalOutput")

    plan = slot_plan()
    nslot = len(plan)
    wscr_t = nc.dram_tensor("wscr", [nslot, 128, SLOT], BF16, kind="Internal")
    wsrc = {"w_in": w_in_t, "w_bp": w_bp_t, "w_bs": w_bs_t, "w_out": w_out_t}

    with ExitStack() as es:
        S = Sched(nc, es)
        V, A, G, PE, SP = nc.vector, nc.scalar, nc.gpsimd, nc.tensor, nc.sync

        def sbt(name, shape, dt, stack=es):
            return stack.enter_context(nc.sbuf_tensor(name, list(shape), dt))

        ident = sbt("ident", [128, 128], BF16)
        Win_sb = sbt("Win_sb", [128, 4, 8, 2, 128], BF16)
        Wout_sb = sbt("Wout_sb", [128, 16, 8, 2, 32], BF16)
        Toep_sb = sbt("Toep_sb", [128, 4, 8, 128], BF16)
        Ec = sbt("Ec", [128, 16, NCH], F32)
        Es = sbt("Es", [128, 16, NCH], F32)
        Rf = sbt("Rf", [128, 16, NCH], F32)
        R_sb = sbt("R_sb", [128, 16], F32)
        pw_sb = sbt("pw_sb", [128, 4, 2, 256], BF16)
        gluw_sb = sbt("gluw_sb", [128, 4, 512], BF16)
        wple_sb = sbt("wple_sb", [128, 2, 1024], BF16)
        lng_sb = sbt("lng_sb", [128, D], F32)
        lnb_sb = sbt("lnb_sb", [128, D], F32)
        dsk = sbt("dsk", [128, 4], F32)
        glub = sbt("glub", [128, 4], F32)
        pscale = sbt("pscale", [128, 8], F32)
        invc = sbt("invc", [128, 16], F32)

        B_const = Buf("const")
        xbf = sbt("xbf", [128, 4, D], BF16); B_xbf = Buf("xbf")
        pbf = sbt("pbf", [128, 4, PLE], BF16); B_pbf = Buf("pbf")
        x_ap = x_t.ap()
        p_ap = p_t.ap()

        def load_x(mt):
            b, ti = divmod(mt, tps)
            tok0 = ti * NT
            S.dma("pool", [], [B_xbf], lambda: G.dma_start(
                out=xbf[:], in_=x_ap[b, tok0:tok0 + NT, :].rearrange("(t p) d -> p t d", p=128)))
            S.dma("pool", [], [B_pbf], lambda: G.dma_start(
                out=pbf[:], in_=p_ap[b, tok0:tok0 + NT, :].rearrange("(t p) d -> p t d", p=128)))
        psum = [es.enter_context(nc.psum_tensor("ps%d" % i, [128, 512], F32)) for i in range(8)]
        psB = [Buf("ps%d" % i) for i in range(8)]
        mm_banks = list(range(8))
        tr_banks = list(range(8))
        rr = {"mm": 0, "tr": 0, "tmp": 0, "ring": 0}

        def next_mm():
            i = mm_banks[rr["mm"] % len(mm_banks)]
            rr["mm"] += 1
            return psum[i], psB[i]

        def next_tr():
            return next_mm()

        S.op("pool", [], [B_const], lambda: G.memset(ident[:], 0.0))
        S.op("pool", [B_const], [B_const], lambda: G.affine_select(
            out=ident[:], in_=ident[:], pattern=[[-1, 128]], compare_op=ALU.not_equal,
            fill=1.0, base=0, channel_multiplier=1))
        for t in range(16):
            S.op("pool", [], [B_const], lambda t=t: G.memset(invc[:, t:t + 1], 1.0 / (t + 1)))
        S.dma("pool", [], [B_const], lambda: G.dma_start(
            out=pw_sb[:], in_=pool_w_t.ap().rearrange("g (ic p) o -> p g ic o", p=128)))
        S.dma("pool", [], [B_const], lambda: G.dma_start(
            out=gluw_sb[:], in_=glu_w_t.ap().rearrange("(kc p) n -> p kc n", p=128)))
        S.dma("pool", [], [B_const], lambda: G.dma_start(
            out=wple_sb[:], in_=w_ple_t.ap().rearrange("(kc p) n -> p kc n", p=128)))

        load_x(0)
        scrB = [Buf("scr%d" % i) for i in range(nslot)]

        def emit_casts(lo, hi):
            for si in range(lo, min(hi, nslot)):
                name, parts = plan[si]
                for (dst_off, kcn, ncol, src, row0, col0) in parts:
                    src_ap = wsrc[src].ap()[row0:row0 + kcn * 128, col0:col0 + ncol].rearrange("(kc p) n -> p kc n", p=128)
                    dst_ap = wscr_t.ap()[si, :, dst_off:dst_off + kcn * ncol].rearrange("p (kc n) -> p kc n", kc=kcn)
                    S.dma("pool", [], [scrB[si]], lambda d=dst_ap, s=src_ap: G.dma_start(out=d, in_=s))

        with ExitStack() as ps_:
            def pt(name, shape, dt=F32):
                return sbt("pre_" + name, shape, dt, stack=ps_)

            lr = pt("lr", [128, 16]); li = pt("li", [128, 16]); ldt = pt("ldt", [128, 16])
            dtt = pt("dtt", [128, 16]); ang = pt("ang", [128, 16]); lrdt = pt("lrdt", [128, 16])
            mag = pt("mag", [128, 9, 16]); xs = pt("xs", [128, 2, 9, 16]); xi32 = pt("xi32", [128, 2, 9, 16], I32)
            xk = pt("xk", [128, 2, 9, 16]); sc = pt("sc", [128, 2, 9, 16])
            Pr = pt("Pr", [128, 9, 16]); Pi = pt("Pi", [128, 9, 16])
            t16 = [pt("t16_%d" % i, [128, 16]) for i in range(6)]
            zr = pt("zr", [128, 16]); zi = pt("zi", [128, 16])
            BTr = pt("BTr", [128, 16, 32]); BTi = pt("BTi", [128, 16, 32])
            BbTr = pt("BbTr", [128, 16, 32]); BbTi = pt("BbTi", [128, 16, 32])
            CTr = pt("CTr", [128, 16, 32]); CTi = pt("CTi", [128, 16, 32])
            Cn = [pt("Cn%d" % r, [128, 4, 64]) for r in range(2)]
            Zc = [pt("Zc%d" % r, [128, 4, 128]) for r in range(2)]
            m0 = pt("m0", [128, 1]); m1 = pt("m1", [128, 1])
            identf = pt("identf", [128, 128])
            tA = pt("tA", [128, 16, 32]); tB = pt("tB", [128, 16, 32])
            tA9 = pt("tA9", [128, 9, 16, 32]); tB9 = pt("tB9", [128, 9, 16, 32])
            G32 = pt("G32", [128, 9, 2, 16, 32])
            Qb = pt("Qb", [128, 8, 2, 16, 32], BF16)
            tT = [pt("tT%d" % i, [128, 16, 32]) for i in range(2)]
            P = Buf("pre"); P_trig = Buf("pre_trig"); P_C = Buf("pre_C"); P_B = Buf("pre_B")
            P_G = Buf("pre_G"); P_Q = Buf("pre_Q"); P_tab = Buf("pre_tab"); P_m = Buf("pre_m")

            def dv(fn, r=(), w=()):
                S.op("dve", list(r), list(w), fn)

            def ac(fn, r=(), w=()):
                S.op("act", list(r), list(w), fn)

            def pl(fn, r=(), w=()):
                S.op("pool", list(r), list(w), fn)

            stg = {nm: pt("stg_" + nm, [16, 128]) for nm in ("lr", "li", "ldt", "dsk", "glub", "psc")}
            ldt2 = pt("ldt2", [16, 2])
            P_stg = Buf("pre_stg")
            S.dma("sp", [], [P_stg], lambda: SP.dma_start(out=stg["lr"][:], in_=a_re_t.ap().rearrange("(q g) p -> q (g p)", g=2)))
            S.dma("sp", [], [P_stg], lambda: SP.dma_start(out=stg["li"][:], in_=a_im_t.ap().rearrange("(q g) p -> q (g p)", g=2)))
            S.dma("sp", [], [P_stg], lambda: SP.dma_start(out=ldt2[:], in_=log_dt_t.ap().rearrange("(q g) -> q g", g=2)))
            S.dma("sp", [], [P_stg], lambda: SP.dma_start(out=stg["dsk"][0:4, :], in_=ssm_d_t.ap().rearrange("(k p) -> k p", p=128)))
            S.dma("sp", [], [P_stg], lambda: SP.dma_start(out=stg["glub"][0:4, :], in_=glu_b_t.ap().rearrange("(k p) -> k p", p=128)))
            S.dma("sp", [], [P_stg], lambda: SP.dma_start(out=stg["psc"][0:8, :], in_=pool_scale_t.ap().rearrange("(k p) -> k p", p=128)))
            for r, src in enumerate((c_re_t, c_im_t)):
                S.dma("sp", [], [P_C], lambda r=r, src=src: SP.dma_start(
                    out=Cn[r][:], in_=src.ap().rearrange("(k g) h p -> (g h) k p", g=8)))
            pl(lambda: G.memset(BTr[:], 0.0), w=[P_B])
            pl(lambda: G.memset(BTi[:], 0.0), r=[P_B], w=[P_B])
            for g2 in range(2):
                for (src, dst) in ((b_re_t, BTr), (b_im_t, BTi)):
                    S.dma("sp", [], [P_B], lambda g2=g2, src=src, dst=dst: SP.dma_start(
                        out=dst[g2 * 64:(g2 + 1) * 64, :, g2 * 16:(g2 + 1) * 16],
                        in_=bass.AP(src, g2 * 1024, [[16, 64], [2048, 16], [1, 16]])))
            S.dma("sp", [], [B_const], lambda: SP.dma_start(
                out=lng_sb[:], in_=bass.AP(ln_g_t, 0, [[0, 128], [1, D]])))
            S.dma("sp", [], [B_const], lambda: SP.dma_start(
                out=lnb_sb[:], in_=bass.AP(ln_b_t, 0, [[0, 128], [1, D]])))
            pl(lambda: G.memset(identf[:], 0.0), w=[P_m])
            pl(lambda: G.affine_select(out=identf[:], in_=identf[:], pattern=[[-1, 128]], compare_op=ALU.not_equal,
                                       fill=1.0, base=0, channel_multiplier=1), r=[P_m], w=[P_m])
            dv(lambda: V.tensor_copy(out=stg["ldt"][:].rearrange("q (g p) -> q g p", g=2),
                                     in_=ldt2[:].unsqueeze(2).to_broadcast([16, 2, 64])), r=[P_stg], w=[P_stg])
            for nm, n, dst, db in (("lr", 16, lr, P), ("li", 16, li, P), ("ldt", 16, ldt, P),
                                   ("dsk", 4, dsk, B_const), ("glub", 4, glub, B_const), ("psc", 8, pscale, B_const)):
                pst, psb = next_mm()
                S.group("pe", [P_stg, P_m], [psb], [
                    (lambda nm=nm, n=n, pst=pst: PE.transpose(pst[:, 0:n], stg[nm][0:n, :], identf[0:n, 0:n]))])
                ac(lambda dst=dst, n=n, pst=pst: A.activation(out=dst[:, 0:n], in_=pst[:, 0:n], func=AF.Copy), r=[psb], w=[db])
            pl(lambda: G.memset(m0[:], 0.0), r=[P_m], w=[P_m])
            for i in range(4):
                pl(lambda i=i: G.memset(m0[32 * i:32 * i + 16, :], 1.0), r=[P_m], w=[P_m])
            pl(lambda: G.tensor_scalar(out=m1[:], in0=m0[:], scalar1=-1.0, scalar2=1.0, op0=ALU.mult, op1=ALU.add),
               r=[P_m], w=[P_m])
            pl(lambda: G.memset(Toep_sb[:], 0.0), w=[B_const])
            pl(lambda: G.memset(Rf[:], 0.0), r=[B_const], w=[B_const])
            emit_casts(0, 10)
            for r in range(2):
                dv(lambda r=r: V.tensor_scalar(out=Zc[r][:, :, 0:64], in0=Cn[r][:], scalar1=m0[:, 0:1], scalar2=None, op0=ALU.mult),
                   r=[P_C, P_m], w=[P_C])
                dv(lambda r=r: V.tensor_scalar(out=Zc[r][:, :, 64:128], in0=Cn[r][:], scalar1=m1[:, 0:1], scalar2=None, op0=ALU.mult),
                   r=[P_C, P_m], w=[P_C])
            for r, dst in enumerate((CTr, CTi)):
                pst, psb = next_mm()
                S.group("pe", [P_C, P_m], [psb], [
                    (lambda k=k, r=r, pst=pst: PE.transpose(pst[:, k * 128:(k + 1) * 128], Zc[r][:, k, :], identf[:]))
                    for k in range(4)])
                ac(lambda dst=dst, pst=pst: A.activation(out=dst[:].rearrange("p q c -> p (q c)"), in_=pst[:], func=AF.Copy),
                   r=[psb], w=[P_G])
            ac(lambda: A.activation(out=dtt[:], in_=ldt[:], func=AF.Exp), r=[P], w=[P])
            dv(lambda: V.tensor_tensor(out=ang[:], in0=li[:], in1=dtt[:], op=ALU.mult), r=[P], w=[P])
            dv(lambda: V.tensor_tensor(out=lrdt[:], in0=lr[:], in1=dtt[:], op=ALU.mult), r=[P], w=[P])
            for j in range(9):
                ac(lambda j=j: A.activation(out=mag[:, j, :], in_=lrdt[:], func=AF.Exp, scale=float(j)), r=[P], w=[P_trig])
                dv(lambda j=j: V.tensor_scalar(out=xs[:, 0, j, :], in0=ang[:], scalar1=float(j) / TWO_PI,
                                               scalar2=None, op0=ALU.mult), r=[P], w=[P])
            dv(lambda: V.tensor_scalar(out=xs[:, 1], in0=xs[:, 0], scalar1=0.25, scalar2=None, op0=ALU.add), r=[P], w=[P])
            dv(lambda: V.tensor_copy(out=xi32[:], in_=xs[:]), r=[P], w=[P])
            dv(lambda: V.tensor_copy(out=xk[:], in_=xi32[:]), r=[P], w=[P])
            dv(lambda: V.tensor_tensor(out=xs[:], in0=xs[:], in1=xk[:], op=ALU.subtract), r=[P], w=[P])
            ac(lambda: A.activation(out=sc[:], in_=xs[:], func=AF.Sin, scale=TWO_PI), r=[P], w=[P_trig])
            dv(lambda: V.tensor_tensor(out=Pr[:], in0=mag[:], in1=sc[:, 1], op=ALU.mult), r=[P_trig], w=[P_trig])
            dv(lambda: V.tensor_tensor(out=Pi[:], in0=mag[:], in1=sc[:, 0], op=ALU.mult), r=[P_trig], w=[P_trig])
            pl(lambda: G.tensor_copy(out=R_sb[:], in_=mag[:, 8, :]), r=[P_trig, B_const], w=[B_const])
            pl(lambda: G.tensor_copy(out=Ec[:, :, 0], in_=sc[:, 1, 8, :]), r=[P_trig, B_const], w=[B_const])
            pl(lambda: G.tensor_copy(out=Es[:, :, 0], in_=sc[:, 0, 8, :]), r=[P_trig, B_const], w=[B_const])
            m = 1
            while m < NCH:
                umr = Ec[:, :, m - 1:m].to_broadcast([128, 16, m])
                umi = Es[:, :, m - 1:m].to_broadcast([128, 16, m])
                e0, s0 = Ec[:, :, 0:m], Es[:, :, 0:m]
                ta, tb = tT[0][:, :, 0:m], tT[1][:, :, 0:m]
                rw = dict(r=[B_const, P_tab], w=[B_const, P_tab])
                pl(lambda e0=e0, umr=umr, ta=ta: G.tensor_tensor(out=ta, in0=e0, in1=umr, op=ALU.mult), **rw)
                pl(lambda s0=s0, umi=umi, tb=tb: G.tensor_tensor(out=tb, in0=s0, in1=umi, op=ALU.mult), **rw)
                pl(lambda m=m, ta=ta, tb=tb: G.tensor_tensor(out=Ec[:, :, m:2 * m], in0=ta, in1=tb, op=ALU.subtract), **rw)
                pl(lambda e0=e0, umi=umi, ta=ta: G.tensor_tensor(out=ta, in0=e0, in1=umi, op=ALU.mult), **rw)
                pl(lambda s0=s0, umr=umr, tb=tb: G.tensor_tensor(out=tb, in0=s0, in1=umr, op=ALU.mult), **rw)
                pl(lambda m=m, ta=ta, tb=tb: G.tensor_tensor(out=Es[:, :, m:2 * m], in0=ta, in1=tb, op=ALU.add), **rw)
                m *= 2
            pl(lambda: G.tensor_copy(out=Rf[:, :, 1:NCH], in_=R_sb[:].unsqueeze(2).to_broadcast([128, 16, NCH - 1])),
               r=[B_const], w=[B_const])
            emit_casts(10, nslot)
            a1r, den, u1, u2, u3, u4 = t16
            pz = dict(r=[P, P_trig], w=[P])
            dv(lambda: V.tensor_scalar(out=a1r[:], in0=Pr[:, 1, :], scalar1=-1.0, scalar2=None, op0=ALU.add), **pz)
            dv(lambda: V.tensor_tensor(out=u1[:], in0=lr[:], in1=lr[:], op=ALU.mult), **pz)
            dv(lambda: V.tensor_tensor(out=u2[:], in0=li[:], in1=li[:], op=ALU.mult), **pz)
            dv(lambda: V.tensor_tensor(out=den[:], in0=u1[:], in1=u2[:], op=ALU.add), **pz)
            dv(lambda: V.reciprocal(out=den[:], in_=den[:]), **pz)
            dv(lambda: V.tensor_tensor(out=u1[:], in0=a1r[:], in1=lr[:], op=ALU.mult), **pz)
            dv(lambda: V.tensor_tensor(out=u2[:], in0=Pi[:, 1, :], in1=li[:], op=ALU.mult), **pz)
            dv(lambda: V.tensor_tensor(out=u1[:], in0=u1[:], in1=u2[:], op=ALU.add), **pz)
            dv(lambda: V.tensor_tensor(out=zr[:], in0=u1[:], in1=den[:], op=ALU.mult), **pz)
            dv(lambda: V.tensor_tensor(out=u3[:], in0=Pi[:, 1, :], in1=lr[:], op=ALU.mult), **pz)
            dv(lambda: V.tensor_tensor(out=u4[:], in0=a1r[:], in1=li[:], op=ALU.mult), **pz)
            dv(lambda: V.tensor_tensor(out=u3[:], in0=u3[:], in1=u4[:], op=ALU.subtract), **pz)
            dv(lambda: V.tensor_tensor(out=zi[:], in0=u3[:], in1=den[:], op=ALU.mult), **pz)

            def bc(ap16):
                return ap16.unsqueeze(2).to_broadcast([128, 16, 32])

            pb_ = dict(r=[P, P_B], w=[P_B])
            dv(lambda: V.tensor_tensor(out=tA[:], in0=BTr[:], in1=bc(zr[:]), op=ALU.mult), **pb_)
            dv(lambda: V.tensor_tensor(out=tB[:], in0=BTi[:], in1=bc(zi[:]), op=ALU.mult), **pb_)
            dv(lambda: V.tensor_tensor(out=BbTr[:], in0=tA[:], in1=tB[:], op=ALU.subtract), **pb_)
            dv(lambda: V.tensor_tensor(out=tA[:], in0=BTr[:], in1=bc(zi[:]), op=ALU.mult), **pb_)
            dv(lambda: V.tensor_tensor(out=tB[:], in0=BTi[:], in1=bc(zr[:]), op=ALU.mult), **pb_)
            dv(lambda: V.tensor_tensor(out=BbTi[:], in0=tA[:], in1=tB[:], op=ALU.add), **pb_)

            def big_cmul(n, out_r, out_i, ar3, ai3, neg_i, rds, wrs):
                ta, tb = tA9[:, 0:n], tB9[:, 0:n]
                ab = lambda a: a.unsqueeze(1).to_broadcast([128, n, 16, 32])
                pr = Pr[:, 0:n, :].unsqueeze(3).to_broadcast([128, n, 16, 32])
                pi = Pi[:, 0:n, :].unsqueeze(3).to_broadcast([128, n, 16, 32])
                kw = dict(r=list(rds) + [P_trig], w=list(wrs))
                dv(lambda: V.tensor_tensor(out=ta, in0=ab(ar3), in1=pr, op=ALU.mult), **kw)
                dv(lambda: V.tensor_tensor(out=tb, in0=ab(ai3), in1=pi, op=ALU.mult), **kw)
                dv(lambda: V.tensor_tensor(out=out_r, in0=ta, in1=tb, op=ALU.subtract), **kw)
                dv(lambda: V.tensor_tensor(out=ta, in0=ab(ar3), in1=pi, op=ALU.mult), **kw)
                dv(lambda: V.tensor_tensor(out=tb, in0=ab(ai3), in1=pr, op=ALU.mult), **kw)
                if neg_i:
                    dv(lambda: V.scalar_tensor_tensor(out=out_i, in0=ta, scalar=-1.0, in1=tb, op0=ALU.mult, op1=ALU.subtract), **kw)
                else:
                    dv(lambda: V.tensor_tensor(out=out_i, in0=ta, in1=tb, op=ALU.add), **kw)

            big_cmul(9, G32[:, :, 0], G32[:, :, 1], CTr[:], CTi[:], True, [P_G], [P_G])
            big_cmul(8, Qb[:, :, 0], Qb[:, :, 1], BbTr[:], BbTi[:], False, [P_B, P_G], [P_Q, P_G])
            for tau in range(8):
                ac(lambda tau=tau: A.activation(out=Wout_sb[:, :, tau, :, :].rearrange("p q r c -> p r q c"),
                                                in_=G32[:, tau + 1], func=AF.Copy), r=[P_G], w=[B_const])
            for k in range(4):
                for ri in range(2):
                    pst, psb = next_tr()
                    pv = pst[:].bitcast(BF16).rearrange("p (s c) -> p s c", s=8)
                    S.group("pe", [P_Q, B_const], [psb], [
                        (lambda s=s, k=k, ri=ri, pv=pv: PE.transpose(
                            pv[:, s, :], Qb[:, 7 - s, ri, 4 * k:4 * k + 4, :].rearrange("p q c -> p (q c)"), ident[:]))
                        for s in range(8)])
                    S.op("act", [psb], [B_const], lambda k=k, ri=ri, pv=pv: A.activation(
                        out=Win_sb[:, k, :, ri, :], in_=pv, func=AF.Copy))
            for k in range(4):
                pst, psb = next_mm()
                pv = pst[:, 0:256].rearrange("p (j c) -> p j c", j=8)
                fns = []
                for i in range(4):
                    q = 4 * k + i
                    tp = (0, 32 * i)
                    fns.append(lambda i=i, q=q, tp=tp, pv=pv: PE.matmul(
                        pv[32 * i:32 * i + 32], lhsT=BbTr[:, q, :], rhs=G32[:, 0:8, 0, q, :],
                        start=True, stop=False, tile_position=tp))
                    fns.append(lambda i=i, q=q, tp=tp, pv=pv: PE.matmul(
                        pv[32 * i:32 * i + 32], lhsT=BbTi[:, q, :], rhs=G32[:, 0:8, 1, q, :],
                        start=False, stop=True, tile_position=tp))
                S.group("pe", [P_G, P_B], [psb], fns)
                for i in range(4):
                    S.op("act", [psb], [B_const], lambda i=i, k=k, pv=pv: A.activation(
                        out=Toep_sb[32 * i:32 * i + 32, k, :, 32 * i:32 * i + 32], in_=pv[32 * i:32 * i + 32], func=AF.Copy))
            S.barrier()
        ring = sbt("ring", [128, NRING, SLOT], BF16)
        ringB = [Buf("ring%d" % i) for i in range(NRING)]
        xT2 = [sbt("xT%d" % i, [128, 8, NT], BF16) for i in range(2)]; B_xT2 = [Buf("xT%d" % i) for i in range(2)]
        pT2 = [sbt("pT%d" % i, [128, 2, NT], BF16) for i in range(2)]; B_pT2 = [Buf("pT%d" % i) for i in range(2)]
        ssm_u = sbt("ssm_u", [128, 4, 8, NCH], BF16); B_u = [Buf("u%d" % k) for k in range(4)]
        ssm_sg = sbt("ssm_sg", [128, 4, NT], BF16); B_sg = [Buf("sg%d" % k) for k in range(4)]
        Xp = [sbt("Xp%d" % r, [128, 16, NCH + 1], BF16) for r in range(2)]; B_Xp = [Buf("Xp_h%d" % h) for h in range(2)]
        Xc = [sbt("Xc%d" % r, [128, 16], F32) for r in range(2)]; B_Xc = [Buf("Xc_h%d" % h) for h in range(2)]
        RXc = [sbt("RXc%d" % r, [128, 16], F32) for r in range(2)]
        gbf = sbt("gbf", [128, 4, NT], BF16); B_g = [Buf("g%d" % k) for k in range(4)]
        y_ssm = sbt("y_ssm", [128, 4, NT], BF16); B_ys = [Buf("ys%d" % k) for k in range(4)]
        halo = sbt("halo", [128, 8, 16], F32); B_halo = [Buf("halo%d" % c) for c in range(8)]
        dall = sbt("dall", [128, 2, 2, NT], BF16)
        sall = sbt("sall", [128, 2, 2, NT], BF16)
        dbuf = [dall[:, i] for i in range(2)]; B_d = [[Buf("d%d_%d" % (i, c)) for c in range(2)] for i in range(2)]
        sgate = [sall[:, i] for i in range(2)]; B_sgt = [[Buf("sgt%d_%d" % (i, c)) for c in range(2)] for i in range(2)]
        y_pool = sbt("y_pool", [128, 8, NT], BF16); B_yp = [Buf("yp%d" % c) for c in range(8)]
        merged = sbt("merged", [128, 8, NT], BF16); B_mg = [Buf("mg%d" % c) for c in range(8)]
        tmpc = sbt("tmpc", [128, 16], F32); B_tmpc = Buf("tmpc")
        lnst = sbt("lnst", [128, 4, 2, 6], F32); lnmv = sbt("lnmv", [128, 4, 2], F32)
        lnr = sbt("lnr", [128, 4], F32); lnn = sbt("lnn", [128, 4], F32); B_ln = Buf("ln")
        chainbufs = [(sbt("cb%d" % i, [128, 512], F32), Buf("cb%d" % i)) for i in range(6)]
        tmp = [sbt("tmp%d" % i, [128, TMPW], F32) for i in range(NTMP)]
        tmpB = [Buf("tmp%d" % i) for i in range(NTMP)]

        def next_tmp():
            i = rr["tmp"] % NTMP
            rr["tmp"] += 1
            return tmp[i], tmpB[i]

        z_ap = [y_pool[:, 0:4, :].rearrange("p a b -> p (a b)").bitcast(F32),
                y_pool[:, 4:8, :].rearrange("p a b -> p (a b)").bitcast(F32),
                dall[:].rearrange("p a b c -> p (a b c)").bitcast(F32),
                sall[:].rearrange("p a b c -> p (a b c)").bitcast(F32)]
        z_bufs = [B_yp[0:4], B_yp[4:8], B_d[0] + B_d[1], B_sgt[0] + B_sgt[1]]

        out_ap = out_t.ap()
        stream_pos = [0]

        def load_slot(expect_name):
            si = stream_pos[0] % nslot
            assert plan[si][0] == expect_name, (plan[si][0], expect_name)
            stream_pos[0] += 1
            r = rr["ring"] % NRING
            rr["ring"] += 1
            S.dma("sp", [scrB[si]], [ringB[r]], lambda: SP.dma_start(out=ring[:, r, :], in_=wscr_t.ap()[si]))
            return ring[:, r, :], ringB[r]

        cur = {}

        def proj_fm(slot_ap, slot_b, cl, evac):
            xT, B_xT = cur["xT"], cur["B_xT"]
            w = slot_ap.rearrange("p (kc n) -> p kc n", kc=8)
            pst, psb = next_mm()
            S.group("pe", [slot_b, B_xT], [psb], [
                (lambda kc=kc: PE.matmul(pst[:], lhsT=w[:, kc, cl * 128:(cl + 1) * 128], rhs=xT[:, kc, :],
                                         start=(kc == 0), stop=(kc == 7))) for kc in range(8)])
            evac(pst, psb)

        nmt = nseq * tps
        pending_tail = []

        def tail_items(b, tok0):
            items = []

            for t in range(4):
                S.dma("pool", [], z_bufs[t], lambda t=t: G.dma_start(
                    out=z_ap[t], in_=x_ap[b, tok0 + t * 128:tok0 + (t + 1) * 128, :], accum_op=ALU.add))

            def stats(t):
                for hh in range(2):
                    S.op("dve", z_bufs[t], [B_ln], lambda hh=hh: V.bn_stats(
                        out=lnst[:, t, hh, :], in_=z_ap[t][:, hh * 512:(hh + 1) * 512]))
                S.op("dve", [B_ln], [B_ln], lambda: V.bn_aggr(out=lnmv[:, t, :], in_=lnst[:, t]))

            def rstd():
                S.op("dve", [B_ln], [B_ln], lambda: V.tensor_scalar(
                    out=lnr[:], in0=lnmv[:, :, 1], scalar1=LN_EPS / (ALPHA * ALPHA), scalar2=None, op0=ALU.add))
                S.op("act", [B_ln], [B_ln], lambda: A.activation(out=lnr[:], in_=lnr[:], func=AF.Sqrt))
                S.op("dve", [B_ln], [B_ln], lambda: V.reciprocal(out=lnr[:], in_=lnr[:]))
                S.op("dve", [B_ln], [B_ln], lambda: V.scalar_tensor_tensor(
                    out=lnn[:], in0=lnmv[:, :, 0], scalar=-1.0, in1=lnr[:], op0=ALU.mult, op1=ALU.mult))

            def outp(t):
                S.op("act", [B_ln] + z_bufs[t], z_bufs[t], lambda: A.activation(
                    out=z_ap[t], in_=z_ap[t], func=AF.Identity, scale=lnr[:, t:t + 1], bias=lnn[:, t:t + 1]))
                S.op("dve", [B_const] + z_bufs[t], z_bufs[t], lambda: V.tensor_tensor(
                    out=z_ap[t], in0=z_ap[t], in1=lng_sb[:], op=ALU.mult))
                if t % 2 == 0:
                    S.op("dve", [B_const] + z_bufs[t], z_bufs[t], lambda: V.tensor_tensor(
                        out=z_ap[t], in0=z_ap[t], in1=lnb_sb[:], op=ALU.add))
                else:
                    S.op("pool", [B_const] + z_bufs[t], z_bufs[t], lambda: G.tensor_tensor(
                        out=z_ap[t], in0=z_ap[t], in1=lnb_sb[:], op=ALU.add))
                S.dma("pool", z_bufs[t], [], lambda: G.dma_start(
                    out=out_ap[b, tok0 + t * 128:tok0 + (t + 1) * 128, :], in_=z_ap[t]))
            for t in range(4):
                items.append(lambda t=t: stats(t))
            items.append(rstd)
            for t in range(4):
                items.append(lambda t=t: outp(t))
            return items

        def transpose_items(mt):
            xT, B_xT = xT2[mt % 2], B_xT2[mt % 2]
            pT, B_pT = pT2[mt % 2], B_pT2[mt % 2]
            items = []

            def xitem(t):
                pst, psb = next_tr()
                pv = pst[:].bitcast(BF16).rearrange("p (k c) -> p k c", k=8)
                S.group("pe", [B_xbf, B_const], [psb], [
                    (lambda kc=kc, t=t, pv=pv: PE.transpose(pv[:, kc, :], xbf[:, t, kc * 128:(kc + 1) * 128], ident[:]))
                    for kc in range(8)])
                S.op("act", [psb], [B_xT], lambda t=t, pv=pv: A.activation(
                    out=xT[:, :, t * 128:(t + 1) * 128], in_=pv, func=AF.Copy))

            def pitem():
                pst, psb = next_tr()
                pv = pst[:].bitcast(BF16).rearrange("p (t k c) -> p t k c", t=4, k=2)
                S.group("pe", [B_pbf, B_const], [psb], [
                    (lambda kc=kc, t=t, pv=pv: PE.transpose(pv[:, t, kc, :], pbf[:, t, kc * 128:(kc + 1) * 128], ident[:]))
                    for t in range(4) for kc in range(2)])
                for kc in range(2):
                    S.op("act", [psb], [B_pT], lambda kc=kc, pv=pv: A.activation(
                        out=pT[:, kc, :].rearrange("p (t c) -> p t c", t=4), in_=pv[:, :, kc, :], func=AF.Copy))
            for t in range(4):
                items.append(lambda t=t: xitem(t))
            items.append(pitem)
            return items

        def emit_transposes(mt):
            for it in transpose_items(mt):
                it()
        if phase < 1:
            nmt = 0
        for mt in range(nmt):
            b, ti = divmod(mt, tps)
            tok0 = ti * NT
            seq_start = (ti == 0)
            xT, B_xT = xT2[mt % 2], B_xT2[mt % 2]
            pT, B_pT = pT2[mt % 2], B_pT2[mt % 2]
            cur["xT"], cur["B_xT"] = xT, B_xT
            if mt == 0:
                emit_transposes(0)
                if nmt > 1:
                    load_x(1)
            if seq_start:
                S.op("pool", [], B_halo, lambda: G.memset(halo[:], 0.0))
            if phase < 2:
                continue
            for half in range(2):
                sl, slb = load_slot("ssm_in_%d" % half)
                for cl in range(2):
                    k = 2 * half + cl

                    def ev(pst, psb, k=k):
                        S.op("act", [psb], [B_u[k]], lambda: A.activation(
                            out=ssm_u[:, k], in_=pst[:].rearrange("p (c s) -> p s c", s=8), func=AF.Copy))
                    proj_fm(sl, slb, cl, ev)
                    if pending_tail:
                        pending_tail.pop(0)()
            for half in range(2):
                sl, slb = load_slot("ssm_gate_%d" % half)
                for cl in range(2):
                    k = 2 * half + cl

                    def ev(pst, psb, k=k):
                        S.op("act", [psb], [B_sg[k]], lambda: A.activation(out=ssm_sg[:, k, :], in_=pst[:], func=AF.Silu))
                    proj_fm(sl, slb, cl, ev)
                    if pending_tail:
                        pending_tail.pop(0)()
            while pending_tail:
                pending_tail.pop(0)()
            tail2 = []
            if phase < 3:
                continue
            def s5_half_items(h2):
                q0 = 8 * h2
                F = slice(0, 512)
                ech = Ec[:, q0:q0 + 8, :].rearrange("p q c -> p (q c)")
                esh = Es[:, q0:q0 + 8, :].rearrange("p q c -> p (q c)")
                rfh = Rf[:, q0:q0 + 8, :].rearrange("p q c -> p (q c)")
                (t1, t1b), (t2, t2b), (t3, t3b), (t4, t4b), (t5, t5b), (t6, t6b) = chainbufs
                st = {}

                def q3(tl):
                    return tl[:, F].rearrange("p (q c) -> p q c", q=8)

                def pe_stage():
                    vb = [next_mm() for _ in range(4)]
                    st["vb"] = vb
                    fns = []
                    for kk in range(2):
                        k = 2 * h2 + kk
                        for s_ in range(8):
                            for i in range(4):
                                for ri in range(2):
                                    first = (kk == 0 and s_ == 0 and ri == 0)
                                    col = (kk * 2 + ri) * NCH
                                    fns.append(lambda k=k, s_=s_, i=i, ri=ri, col=col, first=first: PE.matmul(
                                        vb[i][0][:, col:col + NCH],
                                        lhsT=Win_sb[32 * i:32 * i + 32, k, s_, ri, :],
                                        rhs=ssm_u[32 * i:32 * i + 32, k, s_, :],
                                        start=first, stop=False, tile_position=(32 * i, 0), skip_group_check=True))
                    S.group("pe", [B_const, B_u[2 * h2], B_u[2 * h2 + 1]], [b for _, b in vb], fns)

                def st1():
                    vb = st["vb"]
                    for i in range(4):
                        bv = vb[i][0][:, 0:4 * NCH].rearrange("p (kk ri c) -> p kk ri c", kk=2, ri=2)
                        vr_i, vi_i = bv[:, :, 0, :], bv[:, :, 1, :]
                        ec_i = Ec[:, q0 + i:q0 + 8:4, :]
                        es_i = Es[:, q0 + i:q0 + 8:4, :]
                        pb = vb[i][1]
                        S.op("dve", [pb, B_const], [t1b], lambda i=i, vr_i=vr_i, ec_i=ec_i: V.tensor_tensor(out=q3(t1)[:, i:8:4, :], in0=vr_i, in1=ec_i, op=ALU.mult))
                        S.op("dve", [pb, B_const], [t2b], lambda i=i, vi_i=vi_i, es_i=es_i: V.tensor_tensor(out=q3(t2)[:, i:8:4, :], in0=vi_i, in1=es_i, op=ALU.mult))
                        S.op("dve", [pb, B_const], [t3b], lambda i=i, vi_i=vi_i, ec_i=ec_i: V.tensor_tensor(out=q3(t3)[:, i:8:4, :], in0=vi_i, in1=ec_i, op=ALU.mult))
                        S.op("dve", [pb, B_const], [t4b], lambda i=i, vr_i=vr_i, es_i=es_i: V.tensor_tensor(out=q3(t4)[:, i:8:4, :], in0=vr_i, in1=es_i, op=ALU.mult))

                def st2():
                    S.op("pool", [t1b, t2b], [t1b], lambda: G.tensor_tensor(out=t1[:, F], in0=t1[:, F], in1=t2[:, F], op=ALU.add))
                    S.op("pool", [t3b, t4b], [t3b], lambda: G.tensor_tensor(out=t3[:, F], in0=t3[:, F], in1=t4[:, F], op=ALU.subtract))
                    if seq_start:
                        for r in range(2):
                            S.op("pool", [], [B_Xp[h2]], lambda r=r: G.memset(Xp[r][:, q0:q0 + 8, 0:1], 0.0))
                    else:
                        for r, (vm, vmb) in enumerate(((t1, t1b), (t3, t3b))):
                            S.op("pool", [B_Xc[h2]], [B_Xp[h2]], lambda r=r: G.tensor_copy(
                                out=Xp[r][:, q0:q0 + 8, 0:1], in_=Xc[r][:, q0:q0 + 8].unsqueeze(2)))
                            v3 = q3(vm)[:, :, 0:1]
                            S.op("pool", [B_Xc[h2], vmb], [vmb], lambda r=r, v3=v3: G.tensor_tensor(
                                out=v3, in0=v3, in1=RXc[r][:, q0:q0 + 8].unsqueeze(2), op=ALU.add))

                def st3():
                    S.op("dve", [t1b, B_const], [t2b], lambda: V.tensor_tensor_scan(
                        out=t2[:, F], data0=rfh, data1=t1[:, F], initial=0.0, op0=ALU.mult, op1=ALU.add))
                    S.op("dve", [t3b, B_const], [t4b], lambda: V.tensor_tensor_scan(
                        out=t4[:, F], data0=rfh, data1=t3[:, F], initial=0.0, op0=ALU.mult, op1=ALU.add))

                def st4():
                    S.op("pool", [t2b, B_const], [t1b], lambda: G.tensor_tensor(out=t1[:, F], in0=t2[:, F], in1=ech, op=ALU.mult))
                    S.op("dve", [t2b, B_const], [t5b], lambda: V.tensor_tensor(out=t5[:, F], in0=t2[:, F], in1=esh, op=ALU.mult))
                    S.op("pool", [t4b, B_const], [t3b], lambda: G.tensor_tensor(out=t3[:, F], in0=t4[:, F], in1=esh, op=ALU.mult))
                    S.op("dve", [t4b, B_const], [t6b], lambda: V.tensor_tensor(out=t6[:, F], in0=t4[:, F], in1=ech, op=ALU.mult))

                def st5():
                    S.op("pool", [t1b, t3b], [t1b], lambda: G.tensor_tensor(out=t1[:, F], in0=t1[:, F], in1=t3[:, F], op=ALU.subtract))
                    S.op("dve", [t5b, t6b], [t5b], lambda: V.tensor_tensor(out=t5[:, F], in0=t5[:, F], in1=t6[:, F], op=ALU.add))

                def st6():
                    for r, (vm, vmb) in enumerate(((t1, t1b), (t5, t5b))):
                        v3 = q3(vm)
                        S.op("act", [vmb], [B_Xp[h2]], lambda r=r, v3=v3: A.activation(
                            out=Xp[r][:, q0:q0 + 8, 1:NCH + 1], in_=v3, func=AF.Copy))
                        S.op("pool", [vmb], [B_Xc[h2]], lambda r=r, v3=v3: G.tensor_copy(
                            out=Xc[r][:, q0:q0 + 8].unsqueeze(2), in_=v3[:, :, NCH - 1:NCH]))
                        S.op("dve", [B_Xc[h2], B_const], [B_Xc[h2]], lambda r=r: V.tensor_tensor(
                            out=RXc[r][:, q0:q0 + 8], in0=Xc[r][:, q0:q0 + 8], in1=R_sb[:, q0:q0 + 8], op=ALU.mult))
                return [pe_stage, st1, st2, st3, st4, st5, st6]

            def pool_w_mm(gi, di):
                for oc in range(2):
                    pst, psb = next_mm()
                    S.group("pe", [B_const, B_d[di][0], B_d[di][1]], [psb], [
                        (lambda ic=ic, oc=oc, pst=pst: PE.matmul(
                            pst[:], lhsT=pw_sb[:, gi, ic, oc * 128:(oc + 1) * 128], rhs=dbuf[di][:, ic, :],
                            start=(ic == 0), stop=(ic == 1))) for ic in range(2)])
                    cc = 2 * gi + oc
                    S.op("dve", [psb, B_const, B_sgt[di][oc]], [B_yp[cc]], lambda pst=pst, cc=cc, oc=oc: V.scalar_tensor_tensor(
                        out=y_pool[:, cc, :], in0=pst[:], scalar=pscale[:, cc:cc + 1], in1=sgate[di][:, oc, :],
                        op0=ALU.mult, op1=ALU.mult))

            def pool_items():
                items = []
                slots = {}
                for gi in (3, 2, 1, 0):
                    di = gi % 2
                    w = POOL_WINDOWS[gi]
                    for cl in range(2):
                        cc = 2 * gi + cl

                        def item_in(gi=gi, cl=cl, cc=cc, w=w, di=di):
                            if cl == 0:
                                slots["in"] = load_slot("pool_in_%d" % gi)
                            sl, slb = slots["in"]

                            def ev(pst, psb):
                                ub, ubb = next_tmp()
                                S.op("pool", [B_halo[cc]], [ubb], lambda: G.tensor_copy(out=ub[:, 0:16], in_=halo[:, cc, :]))
                                S.op("act", [psb], [ubb], lambda: A.activation(out=ub[:, 16:16 + NT], in_=pst[:], func=AF.Copy))
                                S.op("pool", [ubb], [B_halo[cc]], lambda: G.tensor_copy(out=halo[:, cc, :], in_=ub[:, NT:NT + 16]))
                                cur_, curb = ub, ubb
                                sh = 1
                                while sh < w:
                                    nx, nxb = next_tmp()
                                    lo = 2 * sh - 1
                                    if cl == 0:
                                        S.op("dve", [curb], [nxb], lambda cur_=cur_, nx=nx, lo=lo, sh=sh: V.tensor_tensor(
                                            out=nx[:, lo:TMPW], in0=cur_[:, lo:TMPW], in1=cur_[:, lo - sh:TMPW - sh], op=ALU.add))
                                    else:
                                        S.op("pool", [curb], [nxb], lambda cur_=cur_, nx=nx, lo=lo, sh=sh: G.tensor_tensor(
                                            out=nx[:, lo:TMPW], in0=cur_[:, lo:TMPW], in1=cur_[:, lo - sh:TMPW - sh], op=ALU.add))
                                    cur_, curb = nx, nxb
                                    sh *= 2
                                S.op("dve", [curb, ubb], [B_d[di][cl]], lambda cur_=cur_: V.scalar_tensor_tensor(
                                    out=dbuf[di][:, cl, :], in0=cur_[:, 16:16 + NT], scalar=1.0 / w, in1=ub[:, 16:16 + NT],
                                    op0=ALU.mult, op1=ALU.subtract))
                                if seq_start:
                                    S.op("dve", [curb, B_const], [B_tmpc], lambda cur_=cur_: V.tensor_tensor(
                                        out=tmpc[:, 0:w - 1], in0=cur_[:, 16:16 + w - 1], in1=invc[:, 0:w - 1], op=ALU.mult))
                                    S.op("dve", [B_tmpc, ubb], [B_d[di][cl]], lambda: V.tensor_tensor(
                                        out=dbuf[di][:, cl, 0:w - 1], in0=tmpc[:, 0:w - 1], in1=ub[:, 16:16 + w - 1], op=ALU.subtract))
                            proj_fm(sl, slb, cl, ev)
                        items.append(item_in)
                    for cl in range(2):
                        def item_gate(gi=gi, cl=cl, di=di):
                            if cl == 0:
                                slots["gate"] = load_slot("pool_gate_%d" % gi)
                            sl, slb = slots["gate"]

                            def ev(pst, psb):
                                S.op("act", [psb], [B_sgt[di][cl]], lambda: A.activation(out=sgate[di][:, cl, :], in_=pst[:], func=AF.Silu))
                            proj_fm(sl, slb, cl, ev)
                        items.append(item_gate)
                    if gi < 3:
                        items.append(lambda gi=gi: pool_w_mm(gi + 1, (gi + 1) % 2))
                return items

            chain = s5_half_items(0) + s5_half_items(1)
            ditems = pool_items()
            chain.pop(0)()
            chain.pop(0)()
            nd = 0
            while chain or ditems or tail2:
                if ditems:
                    ditems.pop(0)()
                    nd += 1
                if chain:
                    chain.pop(0)()
                if tail2 and (nd % 3 == 0 or not ditems):
                    tail2.pop(0)()
            if phase < 5:
                continue
            for k in range(4):
                h2 = k // 2
                yt, yb = next_mm()
                yv = yt[:].rearrange("p (t c) -> p t c", t=8)
                fns = []
                for tau in range(8):
                    for s_ in range(tau + 1):
                        fns.append(lambda tau=tau, s_=s_: PE.matmul(
                            yv[:, tau, :], lhsT=Toep_sb[:, k, tau - s_, :], rhs=ssm_u[:, k, s_, :],
                            start=(tau == 0 and s_ == 0), stop=False, skip_group_check=True))
                    for i in range(4):
                        q = 4 * k + i
                        for ri in range(2):
                            fns.append(lambda tau=tau, i=i, q=q, ri=ri: PE.matmul(
                                yv[32 * i:32 * i + 32, tau, :], lhsT=Wout_sb[:, q, tau, ri, :], rhs=Xp[ri][:, q, 0:NCH],
                                start=False, stop=False, tile_position=(0, 32 * i), skip_group_check=True))
                S.group("pe", [B_const, B_u[k], B_Xp[h2]], [yb], fns)
                if k == 0:
                    pool_w_mm(0, 0)
                y32, y32b = next_tmp(); qq, qqb = next_tmp(); sg, sgb = next_tmp()
                F = slice(0, NT)
                S.op("dve", [yb, B_u[k], B_const], [y32b], lambda k=k, yv=yv, y32=y32: V.scalar_tensor_tensor(
                    out=y32[:, F].rearrange("p (c s) -> p c s", s=8), in0=ssm_u[:, k].rearrange("p s c -> p c s"),
                    scalar=dsk[:, k:k + 1], in1=yv.rearrange("p t c -> p c t"), op0=ALU.mult, op1=ALU.add))
                S.op("act", [y32b], [qqb], lambda y32=y32, qq=qq: A.activation(out=qq[:, F], in_=y32[:, F], func=AF.Square))
                S.op("dve", [qqb], [qqb], lambda qq=qq: V.tensor_scalar(
                    out=qq[:, F], in0=qq[:, F], scalar1=GELU_C1, scalar2=1.0, op0=ALU.mult, op1=ALU.add))
                S.op("dve", [qqb, y32b], [qqb], lambda qq=qq, y32=y32: V.tensor_tensor(
                    out=qq[:, F], in0=qq[:, F], in1=y32[:, F], op=ALU.mult))
                S.op("act", [qqb], [sgb], lambda qq=qq, sg=sg: A.activation(
                    out=sg[:, F], in_=qq[:, F], func=AF.Sigmoid, scale=2.0 * GELU_C0))
                S.op("dve", [sgb, y32b], [B_g[k]], lambda k=k, sg=sg, y32=y32: V.tensor_tensor(
                    out=gbf[:, k, :], in0=y32[:, F], in1=sg[:, F], op=ALU.mult))
            tr_items = []
            if mt + 1 < nmt:
                for it in transpose_items(mt + 1):
                    it()
                if mt + 2 < nmt:
                    load_x(mt + 2)
            if phase < 6:
                continue
            for oc in range(4):
                pst, psb = next_mm()
                S.group("pe", [B_const] + B_g, [psb], [
                    (lambda kc=kc, oc=oc, pst=pst: PE.matmul(
                        pst[:], lhsT=gluw_sb[:, kc, oc * 128:(oc + 1) * 128], rhs=gbf[:, kc, :],
                        start=(kc == 0), stop=(kc == 3))) for kc in range(4)])
                s1, s1b = next_tmp()
                F = slice(0, NT)
                S.op("act", [psb, B_const], [s1b], lambda pst=pst, s1=s1, oc=oc: A.activation(
                    out=s1[:, F], in_=pst[:], func=AF.Sigmoid, bias=glub[:, oc:oc + 1]))
                S.op("dve", [s1b, B_g[oc]], [s1b], lambda s1=s1, oc=oc: V.tensor_tensor(
                    out=s1[:, F], in0=s1[:, F], in1=gbf[:, oc, :], op=ALU.mult))
                S.op("pool", [s1b, B_sg[oc]], [B_ys[oc]], lambda s1=s1, oc=oc: G.tensor_tensor(
                    out=y_ssm[:, oc, :], in0=s1[:, F], in1=ssm_sg[:, oc, :], op=ALU.mult))
            if phase < 7:
                continue
            wbs = None
            for jp in range(4):
                gp_sl, gp_b = load_slot("g_pool_%d" % jp)
                gs_sl, gs_b = load_slot("g_ssm_%d" % jp)
                bp_sl, bp_b = load_slot("w_bp_%d" % jp)
                if jp % 2 == 0:
                    wbs = load_slot("w_bs_%d" % (jp // 2))
                bs_sl, bs_b = wbs
                bpw = bp_sl.rearrange("p (kc n) -> p kc n", kc=8)
                bsw = bs_sl.rearrange("p (kc n) -> p kc n", kc=4)
                for cl in range(2):
                    j = 2 * jp + cl
                    F = slice(0, NT)
                    gates = []
                    for (sl_, b_) in ((gp_sl, gp_b), (gs_sl, gs_b)):
                        gt, gtb = next_tmp()

                        def ev(pst, psb, gt=gt, gtb=gtb):
                            S.op("act", [psb], [gtb], lambda: A.activation(out=gt[:, F], in_=pst[:], func=AF.Sigmoid))
                        proj_fm(sl_, b_, cl, ev)
                        gates.append((gt, gtb))
                    pa, pab = next_mm()
                    S.group("pe", [bp_b] + B_yp, [pab], [
                        (lambda kc=kc, pa=pa: PE.matmul(pa[:], lhsT=bpw[:, kc, cl * 128:(cl + 1) * 128], rhs=y_pool[:, kc, :],
                                                        start=(kc == 0), stop=(kc == 7))) for kc in range(8)])
                    pb, pbb = next_mm()
                    co = (jp % 2) * 256 + cl * 128
                    S.group("pe", [bs_b] + B_ys, [pbb], [
                        (lambda kc=kc, pb=pb, co=co: PE.matmul(pb[:], lhsT=bsw[:, kc, co:co + 128], rhs=y_ssm[:, kc, :],
                                                               start=(kc == 0), stop=(kc == 3))) for kc in range(4)])
                    m1, m1b = next_tmp(); m2, m2b = next_tmp()
                    S.op("dve", [pab, gates[0][1]], [m1b], lambda pa=pa, m1=m1, g=gates[0][0]: V.tensor_tensor(
                        out=m1[:, F], in0=pa[:], in1=g[:, F], op=ALU.mult))
                    S.op("dve", [pbb, gates[1][1]], [m2b], lambda pb=pb, m2=m2, g=gates[1][0]: V.tensor_tensor(
                        out=m2[:, F], in0=pb[:], in1=g[:, F], op=ALU.mult))
                    S.op("pool", [m1b, m2b], [B_mg[j]], lambda j=j, m1=m1, m2=m2: G.tensor_tensor(
                        out=merged[:, j, :], in0=m1[:, F], in1=m2[:, F], op=ALU.add))
                    if tr_items:
                        tr_items.pop(0)()
                        if not tr_items and mt + 2 < nmt:
                            load_x(mt + 2)
            if phase < 8:
                continue
            for h in range(2):
                pg = [load_slot("ple_gate_%d_lo" % h), load_slot("ple_gate_%d_hi" % h)]
                wo = [load_slot("w_out_%d_lo" % h), load_slot("w_out_%d_hi" % h)]
                pgw = [a.rearrange("p (kc n) -> p kc n", kc=4) for a, _ in pg]
                wow = [a.rearrange("p (kc n) -> p kc n", kc=4) for a, _ in wo]
                F = slice(0, NT)
                for t in range(4):
                    ts = slice(t * 128, (t + 1) * 128)
                    pgt, pgb = next_mm()
                    S.group("pe", [pg[0][1], pg[1][1], B_xT], [pgb], [
                        (lambda kc=kc, pgt=pgt, ts=ts: PE.matmul(pgt[:], lhsT=xT[:, kc, ts], rhs=pgw[kc // 4][:, kc % 4, :],
                                                                 start=(kc == 0), stop=(kc == 7))) for kc in range(8)])
                    sgp, sgpb = next_tmp()
                    S.op("act", [pgb], [sgpb], lambda pgt=pgt, sgp=sgp: A.activation(out=sgp[:, F], in_=pgt[:], func=AF.Sigmoid))
                    ppt, ppb = next_mm()
                    S.group("pe", [B_const, B_pT], [ppb], [
                        (lambda kc=kc, ppt=ppt, ts=ts, h=h: PE.matmul(ppt[:], lhsT=pT[:, kc, ts], rhs=wple_sb[:, kc, h * 512:(h + 1) * 512],
                                                                      start=(kc == 0), stop=(kc == 1))) for kc in range(2)])
                    S.op("dve", [ppb, sgpb], [sgpb], lambda ppt=ppt, sgp=sgp: V.scalar_tensor_tensor(
                        out=sgp[:, F], in0=ppt[:], scalar=1.0 / ALPHA, in1=sgp[:, F], op0=ALU.mult, op1=ALU.mult))
                    pmt, pmb = next_mm()
                    S.group("pe", [wo[0][1], wo[1][1]] + B_mg, [pmb], [
                        (lambda kc=kc, pmt=pmt, ts=ts: PE.matmul(pmt[:], lhsT=merged[:, kc, ts], rhs=wow[kc // 4][:, kc % 4, :],
                                                                 start=(kc == 0), stop=(kc == 7))) for kc in range(8)])
                    S.op("dve", [pmb, sgpb], z_bufs[t], lambda pmt=pmt, sgp=sgp, t=t, h=h: V.scalar_tensor_tensor(
                        out=z_ap[t][:, h * 512:(h + 1) * 512], in0=pmt[:], scalar=1.0 / ALPHA, in1=sgp[:, F],
                        op0=ALU.mult, op1=ALU.add))
            pending_tail.extend(tail_items(b, tok0))
        for it in pending_tail:
            it()
        del pending_tail[:]
        S.final_wait("pool")
        S.final_wait("sp")
    return nc


_NC_CACHE = {}


def kernel(**inputs):
    ncores = 8
    x = np.ascontiguousarray(inputs["x"], dtype=np.float32)
    p = np.ascontiguousarray(inputs["p"], dtype=np.float32)[0]
    bsz, seqlen, _ = x.shape
    per = bsz // ncores
    key = (per, seqlen)
    if key not in _NC_CACHE:
        _NC_CACHE[key] = build_nc(per, seqlen)
    nc = _NC_CACHE[key]
    shared = {}
    for name in ("w_in", "pool_w", "pool_scale", "ssm_a_re", "ssm_a_im", "ssm_log_dt", "ssm_b_re", "ssm_b_im",
                 "ssm_c_re", "ssm_c_im", "ssm_d", "glu_w", "glu_b", "w_branch_pool", "w_branch_ssm", "w_out",
                 "w_ple", "ln_g", "ln_b"):
        shared[name] = np.ascontiguousarray(np.asarray(inputs[name], dtype=np.float32)[0])
    in_maps = []
    for c in range(ncores):
        m = dict(shared)
        m["x"] = np.ascontiguousarray(x[c * per:(c + 1) * per])
        m["p"] = np.ascontiguousarray(p[c * per:(c + 1) * per])
        in_maps.append(m)
    res = run_bass_kernel_spmd(nc, in_maps, core_ids=list(range(ncores)))
    return np.concatenate([np.asarray(r["out"]) for r in res.results], axis=0).astype(np.float32)
```

```python
import math
from contextlib import ExitStack

import numpy as np
import concourse.bass as bass
import concourse.mybir as mybir
from concourse.bass_utils import run_bass_kernel_spmd

F32 = mybir.dt.float32
BF16 = mybir.dt.bfloat16
I32 = mybir.dt.int32
AF = mybir.ActivationFunctionType
ALU = mybir.AluOpType

D = 1024
PLE = 256
SSM_W = 512
NGRP = 32
NPAIR = 16
LCH = 8
NT = 512
NCH = NT // LCH
ALPHA = 2.0 ** 0.25
LN_EPS = 1e-5
TWO_PI = 2.0 * math.pi
GELU_C0 = math.sqrt(2.0 / math.pi)
GELU_C1 = 0.044715
POOL_WINDOWS = (2, 4, 8, 16)
SLOT = 2048
NRING = 8
NTMP = 6
TMPW = 528


class Buf:
    __slots__ = ("name", "w", "r")

    def __init__(self, name):
        self.name = name
        self.w = []
        self.r = []


class Sched:
    def __init__(self, nc, es):
        self.nc = nc
        self.engs = {"pe": nc.tensor, "dve": nc.vector, "act": nc.scalar, "pool": nc.gpsimd, "sp": nc.sync}
        self.sem = {e: es.enter_context(nc.semaphore("prog_" + e)) for e in ("pe", "dve", "act", "pool")}
        self.cnt = {e: 0 for e in ("pe", "dve", "act", "pool")}
        self.waited = {e: {} for e in self.engs}
        self.dsems = {
            "sp": [es.enter_context(nc.semaphore("dsp%d" % i)) for i in range(8)],
            "pool": [es.enter_context(nc.semaphore("dpl%d" % i)) for i in range(14)],
        }
        self.dn = {"sp": 0, "pool": 0}
        self.last_dma_tok = {}

    def _wait(self, e, tok):
        key, sem, val = tok
        if e == "pe" and key == "pe":
            return
        if self.waited[e].get(key, 0) >= val:
            return
        self.engs[e].wait_ge(sem, val)
        self.waited[e][key] = val

    def _deps(self, e, reads, writes):
        for b in reads:
            for t in b.w:
                self._wait(e, t)
        for b in writes:
            for t in b.w:
                self._wait(e, t)
            for t in b.r:
                self._wait(e, t)

    @staticmethod
    def _add(lst, tok):
        for i, t in enumerate(lst):
            if t[0] == tok[0]:
                if t[2] < tok[2]:
                    lst[i] = tok
                return
        lst.append(tok)

    def _commit(self, tok, reads, writes):
        for b in reads:
            self._add(b.r, tok)
        for b in writes:
            b.w = [tok]
            b.r = []

    def op(self, e, reads, writes, fn):
        self._deps(e, reads, writes)
        ins = fn()
        self.cnt[e] += 1
        ins.then_inc(self.sem[e], 1)
        self._commit((e, self.sem[e], self.cnt[e]), reads, writes)

    def group(self, e, reads, writes, fns):
        self._deps(e, reads, writes)
        ins = None
        for f in fns:
            ins = f()
        self.cnt[e] += 1
        ins.then_inc(self.sem[e], 1)
        self._commit((e, self.sem[e], self.cnt[e]), reads, writes)

    def dma(self, q, reads, writes, fn):
        n = self.dn[q]
        sems = self.dsems[q]
        r = n % len(sems)
        prev = 16 * (n // len(sems))
        key = "d%s%d" % (q, r)
        if prev > 0:
            self._wait(q, (key, sems[r], prev))
        self._deps(q, reads, writes)
        ins = fn()
        ins.then_inc(sems[r], 16)
        self.dn[q] += 1
        tok = (key, sems[r], prev + 16)
        self.last_dma_tok[key] = tok
        self._commit(tok, reads, writes)

    def barrier(self):
        toks = [(e, self.sem[e], self.cnt[e]) for e in self.cnt if self.cnt[e] > 0]
        toks += [t for k, t in self.last_dma_tok.items() if k.startswith("dsp")]
        for e in self.engs:
            for t in toks:
                if t[0] == e:
                    continue
                self._wait(e, t)

    def final_wait(self, q="sp"):
        for t in self.last_dma_tok.values():
            self._wait(q, t)


def slot_plan():
    plan = []

    def win(name, col0, ncol=256):
        plan.append((name, [(0, 8, ncol, "w_in", 0, col0)]))

    win("ssm_in_0", 2048)
    win("ssm_in_1", 2304)
    win("ssm_gate_0", 2560)
    win("ssm_gate_1", 2816)
    for gi in (3, 2, 1, 0):
        win("pool_in_%d" % gi, gi * 256)
        win("pool_gate_%d" % gi, 1024 + gi * 256)
    for jp in range(4):
        win("g_pool_%d" % jp, 3072 + jp * 256)
        win("g_ssm_%d" % jp, 4096 + jp * 256)
        plan.append(("w_bp_%d" % jp, [(0, 8, 256, "w_bp", 0, jp * 256)]))
        if jp % 2 == 0:
            plan.append(("w_bs_%d" % (jp // 2), [(0, 4, 512, "w_bs", 0, (jp // 2) * 512)]))
    for h in range(2):
        plan.append(("ple_gate_%d_lo" % h, [(0, 4, 512, "w_in", 0, 5120 + h * 512)]))
        plan.append(("ple_gate_%d_hi" % h, [(0, 4, 512, "w_in", 512, 5120 + h * 512)]))
        plan.append(("w_out_%d_lo" % h, [(0, 4, 512, "w_out", 0, h * 512)]))
        plan.append(("w_out_%d_hi" % h, [(0, 4, 512, "w_out", 512, h * 512)]))
    return plan


def build_nc(nseq=2, seqlen=2048, phase=99.0):
    assert seqlen % NT == 0
    tps = seqlen // NT
    nc = bass.Bass("TRN2", target_bir_lowering=False)

    def din(name, shape):
        return nc.dram_tensor(name, list(shape), F32, kind="ExternalInput")

    x_t = din("x", [nseq, seqlen, D])
    p_t = din("p", [nseq, seqlen, PLE])
    w_in_t = din("w_in", [D, 6144])
    pool_w_t = din("pool_w", [4, 256, 256])
    pool_scale_t = din("pool_scale", [D])
    a_re_t = din("ssm_a_re", [NGRP, 64])
    a_im_t = din("ssm_a_im", [NGRP, 64])
    log_dt_t = din("ssm_log_dt", [NGRP])
    b_re_t = din("ssm_b_re", [NGRP, 64, 16])
    b_im_t = din("ssm_b_im", [NGRP, 64, 16])
    c_re_t = din("ssm_c_re", [NGRP, 16, 64])
    c_im_t = din("ssm_c_im", [NGRP, 16, 64])
    ssm_d_t = din("ssm_d", [SSM_W])
    glu_w_t = din("glu_w", [SSM_W, SSM_W])
    glu_b_t = din("glu_b", [SSM_W])
    w_bp_t = din("w_branch_pool", [D, D])
    w_bs_t = din("w_branch_ssm", [SSM_W, D])
    w_out_t = din("w_out", [D, D])
    w_ple_t = din("w_ple", [PLE, D])
    ln_g_t = din("ln_g", [D])
    ln_b_t = din("ln_b", [D])
    out_t = nc.dram_tensor("out", [nseq, seqlen, D], F32, kind="ExternalOutput")

    plan = slot_plan()
    nslot = len(plan)
    wscr_t = nc.dram_tensor("wscr", [nslot, 128, SLOT], BF16, kind="Internal")
    wsrc = {"w_in": w_in_t, "w_bp": w_bp_t, "w_bs": w_bs_t, "w_out": w_out_t}

    with ExitStack() as es:
        S = Sched(nc, es)
        V, A, G, PE, SP = nc.vector, nc.scalar, nc.gpsimd, nc.tensor, nc.sync

        def sbt(name, shape, dt, stack=es):
            return stack.enter_context(nc.sbuf_tensor(name, list(shape), dt))

        ident = sbt("ident", [128, 128], BF16)
        Win_sb = sbt("Win_sb", [128, 4, 8, 2, 128], BF16)
        Wout_sb = sbt("Wout_sb", [128, 16, 8, 2, 32], BF16)
        Toep_sb = sbt("Toep_sb", [128, 4, 8, 128], BF16)
        Ec = sbt("Ec", [128, 16, NCH], F32)
        Es = sbt("Es", [128, 16, NCH], F32)
        Rf = sbt("Rf", [128, 16, NCH], F32)
        R_sb = sbt("R_sb", [128, 16], F32)
        pw_sb = sbt("pw_sb", [128, 4, 2, 256], BF16)
        gluw_sb = sbt("gluw_sb", [128, 4, 512], BF16)
        wple_sb = sbt("wple_sb", [128, 2, 1024], BF16)
        lng_sb = sbt("lng_sb", [128, D], F32)
        lnb_sb = sbt("lnb_sb", [128, D], F32)
        dsk = sbt("dsk", [128, 4], F32)
        glub = sbt("glub", [128, 4], F32)
        pscale = sbt("pscale", [128, 8], F32)
        invc = sbt("invc", [128, 16], F32)

        B_const = Buf("const")
        xbf = sbt("xbf", [128, 4, D], BF16); B_xbf = Buf("xbf")
        pbf = sbt("pbf", [128, 4, PLE], BF16); B_pbf = Buf("pbf")
        x_ap = x_t.ap()
        p_ap = p_t.ap()

        def load_x(mt):
            b, ti = divmod(mt, tps)
            tok0 = ti * NT
            S.dma("pool", [], [B_xbf], lambda: G.dma_start(
                out=xbf[:], in_=x_ap[b, tok0:tok0 + NT, :].rearrange("(t p) d -> p t d", p=128)))
            S.dma("pool", [], [B_pbf], lambda: G.dma_start(
                out=pbf[:], in_=p_ap[b, tok0:tok0 + NT, :].rearrange("(t p) d -> p t d", p=128)))
        psum = [es.enter_context(nc.psum_tensor("ps%d" % i, [128, 512], F32)) for i in range(8)]
        psB = [Buf("ps%d" % i) for i in range(8)]
        mm_banks = list(range(8))
        tr_banks = list(range(8))
        rr = {"mm": 0, "tr": 0, "tmp": 0, "ring": 0}

        def next_mm():
            i = mm_banks[rr["mm"] % len(mm_banks)]
            rr["mm"] += 1
            return psum[i], psB[i]

        def next_tr():
            return next_mm()

        S.op("pool", [], [B_const], lambda: G.memset(ident[:], 0.0))
        S.op("pool", [B_const], [B_const], lambda: G.affine_select(
            out=ident[:], in_=ident[:], pattern=[[-1, 128]], compare_op=ALU.not_equal,
            fill=1.0, base=0, channel_multiplier=1))
        for t in range(16):
            S.op("pool", [], [B_const], lambda t=t: G.memset(invc[:, t:t + 1], 1.0 / (t + 1)))
        S.dma("pool", [], [B_const], lambda: G.dma_start(
            out=pw_sb[:], in_=pool_w_t.ap().rearrange("g (ic p) o -> p g ic o", p=128)))
        S.dma("pool", [], [B_const], lambda: G.dma_start(
            out=gluw_sb[:], in_=glu_w_t.ap().rearrange("(kc p) n -> p kc n", p=128)))
        S.dma("pool", [], [B_const], lambda: G.dma_start(
            out=wple_sb[:], in_=w_ple_t.ap().rearrange("(kc p) n -> p kc n", p=128)))

        load_x(0)
        scrB = [Buf("scr%d" % i) for i in range(nslot)]

        def emit_casts(lo, hi):
            for si in range(lo, min(hi, nslot)):
                name, parts = plan[si]
                for (dst_off, kcn, ncol, src, row0, col0) in parts:
                    src_ap = wsrc[src].ap()[row0:row0 + kcn * 128, col0:col0 + ncol].rearrange("(kc p) n -> p kc n", p=128)
                    dst_ap = wscr_t.ap()[si, :, dst_off:dst_off + kcn * ncol].rearrange("p (kc n) -> p kc n", kc=kcn)
                    S.dma("pool", [], [scrB[si]], lambda d=dst_ap, s=src_ap: G.dma_start(out=d, in_=s))

        with ExitStack() as ps_:
            def pt(name, shape, dt=F32):
                return sbt("pre_" + name, shape, dt, stack=ps_)

            lr = pt("lr", [128, 16]); li = pt("li", [128, 16]); ldt = pt("ldt", [128, 16])
            dtt = pt("dtt", [128, 16]); ang = pt("ang", [128, 16]); lrdt = pt("lrdt", [128, 16])
            mag = pt("mag", [128, 9, 16]); xs = pt("xs", [128, 2, 9, 16]); xi32 = pt("xi32", [128, 2, 9, 16], I32)
            xk = pt("xk", [128, 2, 9, 16]); sc = pt("sc", [128, 2, 9, 16])
            Pr = pt("Pr", [128, 9, 16]); Pi = pt("Pi", [128, 9, 16])
            t16 = [pt("t16_%d" % i, [128, 16]) for i in range(6)]
            zr = pt("zr", [128, 16]); zi = pt("zi", [128, 16])
            BTr = pt("BTr", [128, 16, 32]); BTi = pt("BTi", [128, 16, 32])
            BbTr = pt("BbTr", [128, 16, 32]); BbTi = pt("BbTi", [128, 16, 32])
            CTr = pt("CTr", [128, 16, 32]); CTi = pt("CTi", [128, 16, 32])
            Cn = [pt("Cn%d" % r, [128, 4, 64]) for r in range(2)]
            Zc = [pt("Zc%d" % r, [128, 4, 128]) for r in range(2)]
            m0 = pt("m0", [128, 1]); m1 = pt("m1", [128, 1])
            identf = pt("identf", [128, 128])
            tA = pt("tA", [128, 16, 32]); tB = pt("tB", [128, 16, 32])
            tA9 = pt("tA9", [128, 9, 16, 32]); tB9 = pt("tB9", [128, 9, 16, 32])
            G32 = pt("G32", [128, 9, 2, 16, 32])
            Qb = pt("Qb", [128, 8, 2, 16, 32], BF16)
            tT = [pt("tT%d" % i, [128, 16, 32]) for i in range(2)]
            P = Buf("pre"); P_trig = Buf("pre_trig"); P_C = Buf("pre_C"); P_B = Buf("pre_B")
            P_G = Buf("pre_G"); P_Q = Buf("pre_Q"); P_tab = Buf("pre_tab"); P_m = Buf("pre_m")

            def dv(fn, r=(), w=()):
                S.op("dve", list(r), list(w), fn)

            def ac(fn, r=(), w=()):
                S.op("act", list(r), list(w), fn)

            def pl(fn, r=(), w=()):
                S.op("pool", list(r), list(w), fn)

            stg = {nm: pt("stg_" + nm, [16, 128]) for nm in ("lr", "li", "ldt", "dsk", "glub", "psc")}
            ldt2 = pt("ldt2", [16, 2])
            P_stg = Buf("pre_stg")
            S.dma("sp", [], [P_stg], lambda: SP.dma_start(out=stg["lr"][:], in_=a_re_t.ap().rearrange("(q g) p -> q (g p)", g=2)))
            S.dma("sp", [], [P_stg], lambda: SP.dma_start(out=stg["li"][:], in_=a_im_t.ap().rearrange("(q g) p -> q (g p)", g=2)))
            S.dma("sp", [], [P_stg], lambda: SP.dma_start(out=ldt2[:], in_=log_dt_t.ap().rearrange("(q g) -> q g", g=2)))
            S.dma("sp", [], [P_stg], lambda: SP.dma_start(out=stg["dsk"][0:4, :], in_=ssm_d_t.ap().rearrange("(k p) -> k p", p=128)))
            S.dma("sp", [], [P_stg], lambda: SP.dma_start(out=stg["glub"][0:4, :], in_=glu_b_t.ap().rearrange("(k p) -> k p", p=128)))
            S.dma("sp", [], [P_stg], lambda: SP.dma_start(out=stg["psc"][0:8, :], in_=pool_scale_t.ap().rearrange("(k p) -> k p", p=128)))
            for r, src in enumerate((c_re_t, c_im_t)):
                S.dma("sp", [], [P_C], lambda r=r, src=src: SP.dma_start(
                    out=Cn[r][:], in_=src.ap().rearrange("(k g) h p -> (g h) k p", g=8)))
            pl(lambda: G.memset(BTr[:], 0.0), w=[P_B])
            pl(lambda: G.memset(BTi[:], 0.0), r=[P_B], w=[P_B])
            for g2 in range(2):
                for (src, dst) in ((b_re_t, BTr), (b_im_t, BTi)):
                    S.dma("sp", [], [P_B], lambda g2=g2, src=src, dst=dst: SP.dma_start(
                        out=dst[g2 * 64:(g2 + 1) * 64, :, g2 * 16:(g2 + 1) * 16],
                        in_=bass.AP(src, g2 * 1024, [[16, 64], [2048, 16], [1, 16]])))
            S.dma("sp", [], [B_const], lambda: SP.dma_start(
                out=lng_sb[:], in_=bass.AP(ln_g_t, 0, [[0, 128], [1, D]])))
            S.dma("sp", [], [B_const], lambda: SP.dma_start(
                out=lnb_sb[:], in_=bass.AP(ln_b_t, 0, [[0, 128], [1, D]])))
            pl(lambda: G.memset(identf[:], 0.0), w=[P_m])
            pl(lambda: G.affine_select(out=identf[:], in_=identf[:], pattern=[[-1, 128]], compare_op=ALU.not_equal,
                                       fill=1.0, base=0, channel_multiplier=1), r=[P_m], w=[P_m])
            dv(lambda: V.tensor_copy(out=stg["ldt"][:].rearrange("q (g p) -> q g p", g=2),
                                     in_=ldt2[:].unsqueeze(2).to_broadcast([16, 2, 64])), r=[P_stg], w=[P_stg])
            for nm, n, dst, db in (("lr", 16, lr, P), ("li", 16, li, P), ("ldt", 16, ldt, P),
                                   ("dsk", 4, dsk, B_const), ("glub", 4, glub, B_const), ("psc", 8, pscale, B_const)):
                pst, psb = next_mm()
                S.group("pe", [P_stg, P_m], [psb], [
                    (lambda nm=nm, n=n, pst=pst: PE.transpose(pst[:, 0:n], stg[nm][0:n, :], identf[0:n, 0:n]))])
                ac(lambda dst=dst, n=n, pst=pst: A.activation(out=dst[:, 0:n], in_=pst[:, 0:n], func=AF.Copy), r=[psb], w=[db])
            pl(lambda: G.memset(m0[:], 0.0), r=[P_m], w=[P_m])
            for i in range(4):
                pl(lambda i=i: G.memset(m0[32 * i:32 * i + 16, :], 1.0), r=[P_m], w=[P_m])
            pl(lambda: G.tensor_scalar(out=m1[:], in0=m0[:], scalar1=-1.0, scalar2=1.0, op0=ALU.mult, op1=ALU.add),
               r=[P_m], w=[P_m])
            pl(lambda: G.memset(Toep_sb[:], 0.0), w=[B_const])
            pl(lambda: G.memset(Rf[:], 0.0), r=[B_const], w=[B_const])
            emit_casts(0, 10)
            for r in range(2):
                dv(lambda r=r: V.tensor_scalar(out=Zc[r][:, :, 0:64], in0=Cn[r][:], scalar1=m0[:, 0:1], scalar2=None, op0=ALU.mult),
                   r=[P_C, P_m], w=[P_C])
                dv(lambda r=r: V.tensor_scalar(out=Zc[r][:, :, 64:128], in0=Cn[r][:], scalar1=m1[:, 0:1], scalar2=None, op0=ALU.mult),
                   r=[P_C, P_m], w=[P_C])
            for r, dst in enumerate((CTr, CTi)):
                pst, psb = next_mm()
                S.group("pe", [P_C, P_m], [psb], [
                    (lambda k=k, r=r, pst=pst: PE.transpose(pst[:, k * 128:(k + 1) * 128], Zc[r][:, k, :], identf[:]))
                    for k in range(4)])
                ac(lambda dst=dst, pst=pst: A.activation(out=dst[:].rearrange("p q c -> p (q c)"), in_=pst[:], func=AF.Copy),
                   r=[psb], w=[P_G])
            ac(lambda: A.activation(out=dtt[:], in_=ldt[:], func=AF.Exp), r=[P], w=[P])
            dv(lambda: V.tensor_tensor(out=ang[:], in0=li[:], in1=dtt[:], op=ALU.mult), r=[P], w=[P])
            dv(lambda: V.tensor_tensor(out=lrdt[:], in0=lr[:], in1=dtt[:], op=ALU.mult), r=[P], w=[P])
            for j in range(9):
                ac(lambda j=j: A.activation(out=mag[:, j, :], in_=lrdt[:], func=AF.Exp, scale=float(j)), r=[P], w=[P_trig])
                dv(lambda j=j: V.tensor_scalar(out=xs[:, 0, j, :], in0=ang[:], scalar1=float(j) / TWO_PI,
                                               scalar2=None, op0=ALU.mult), r=[P], w=[P])
            dv(lambda: V.tensor_scalar(out=xs[:, 1], in0=xs[:, 0], scalar1=0.25, scalar2=None, op0=ALU.add), r=[P], w=[P])
            dv(lambda: V.tensor_copy(out=xi32[:], in_=xs[:]), r=[P], w=[P])
            dv(lambda: V.tensor_copy(out=xk[:], in_=xi32[:]), r=[P], w=[P])
            dv(lambda: V.tensor_tensor(out=xs[:], in0=xs[:], in1=xk[:], op=ALU.subtract), r=[P], w=[P])
            ac(lambda: A.activation(out=sc[:], in_=xs[:], func=AF.Sin, scale=TWO_PI), r=[P], w=[P_trig])
            dv(lambda: V.tensor_tensor(out=Pr[:], in0=mag[:], in1=sc[:, 1], op=ALU.mult), r=[P_trig], w=[P_trig])
            dv(lambda: V.tensor_tensor(out=Pi[:], in0=mag[:], in1=sc[:, 0], op=ALU.mult), r=[P_trig], w=[P_trig])
            pl(lambda: G.tensor_copy(out=R_sb[:], in_=mag[:, 8, :]), r=[P_trig, B_const], w=[B_const])
            pl(lambda: G.tensor_copy(out=Ec[:, :, 0], in_=sc[:, 1, 8, :]), r=[P_trig, B_const], w=[B_const])
            pl(lambda: G.tensor_copy(out=Es[:, :, 0], in_=sc[:, 0, 8, :]), r=[P_trig, B_const], w=[B_const])
            m = 1
            while m < NCH:
                umr = Ec[:, :, m - 1:m].to_broadcast([128, 16, m])
                umi = Es[:, :, m - 1:m].to_broadcast([128, 16, m])
                e0, s0 = Ec[:, :, 0:m], Es[:, :, 0:m]
                ta, tb = tT[0][:, :, 0:m], tT[1][:, :, 0:m]
                rw = dict(r=[B_const, P_tab], w=[B_const, P_tab])
                pl(lambda e0=e0, umr=umr, ta=ta: G.tensor_tensor(out=ta, in0=e0, in1=umr, op=ALU.mult), **rw)
                pl(lambda s0=s0, umi=umi, tb=tb: G.tensor_tensor(out=tb, in0=s0, in1=umi, op=ALU.mult), **rw)
                pl(lambda m=m, ta=ta, tb=tb: G.tensor_tensor(out=Ec[:, :, m:2 * m], in0=ta, in1=tb, op=ALU.subtract), **rw)
                pl(lambda e0=e0, umi=umi, ta=ta: G.tensor_tensor(out=ta, in0=e0, in1=umi, op=ALU.mult), **rw)
                pl(lambda s0=s0, umr=umr, tb=tb: G.tensor_tensor(out=tb, in0=s0, in1=umr, op=ALU.mult), **rw)
                pl(lambda m=m, ta=ta, tb=tb: G.tensor_tensor(out=Es[:, :, m:2 * m], in0=ta, in1=tb, op=ALU.add), **rw)
                m *= 2
            pl(lambda: G.tensor_copy(out=Rf[:, :, 1:NCH], in_=R_sb[:].unsqueeze(2).to_broadcast([128, 16, NCH - 1])),
               r=[B_const], w=[B_const])
            emit_casts(10, nslot)
            a1r, den, u1, u2, u3, u4 = t16
            pz = dict(r=[P, P_trig], w=[P])
            dv(lambda: V.tensor_scalar(out=a1r[:], in0=Pr[:, 1, :], scalar1=-1.0, scalar2=None, op0=ALU.add), **pz)
            dv(lambda: V.tensor_tensor(out=u1[:], in0=lr[:], in1=lr[:], op=ALU.mult), **pz)
            dv(lambda: V.tensor_tensor(out=u2[:], in0=li[:], in1=li[:], op=ALU.mult), **pz)
            dv(lambda: V.tensor_tensor(out=den[:], in0=u1[:], in1=u2[:], op=ALU.add), **pz)
            dv(lambda: V.reciprocal(out=den[:], in_=den[:]), **pz)
            dv(lambda: V.tensor_tensor(out=u1[:], in0=a1r[:], in1=lr[:], op=ALU.mult), **pz)
            dv(lambda: V.tensor_tensor(out=u2[:], in0=Pi[:, 1, :], in1=li[:], op=ALU.mult), **pz)
            dv(lambda: V.tensor_tensor(out=u1[:], in0=u1[:], in1=u2[:], op=ALU.add), **pz)
            dv(lambda: V.tensor_tensor(out=zr[:], in0=u1[:], in1=den[:], op=ALU.mult), **pz)
            dv(lambda: V.tensor_tensor(out=u3[:], in0=Pi[:, 1, :], in1=lr[:], op=ALU.mult), **pz)
            dv(lambda: V.tensor_tensor(out=u4[:], in0=a1r[:], in1=li[:], op=ALU.mult), **pz)
            dv(lambda: V.tensor_tensor(out=u3[:], in0=u3[:], in1=u4[:], op=ALU.subtract), **pz)
            dv(lambda: V.tensor_tensor(out=zi[:], in0=u3[:], in1=den[:], op=ALU.mult), **pz)

            def bc(ap16):
                return ap16.unsqueeze(2).to_broadcast([128, 16, 32])

            pb_ = dict(r=[P, P_B], w=[P_B])
            dv(lambda: V.tensor_tensor(out=tA[:], in0=BTr[:], in1=bc(zr[:]), op=ALU.mult), **pb_)
            dv(lambda: V.tensor_tensor(out=tB[:], in0=BTi[:], in1=bc(zi[:]), op=ALU.mult), **pb_)
            dv(lambda: V.tensor_tensor(out=BbTr[:], in0=tA[:], in1=tB[:], op=ALU.subtract), **pb_)
            dv(lambda: V.tensor_tensor(out=tA[:], in0=BTr[:], in1=bc(zi[:]), op=ALU.mult), **pb_)
            dv(lambda: V.tensor_tensor(out=tB[:], in0=BTi[:], in1=bc(zr[:]), op=ALU.mult), **pb_)
            dv(lambda: V.tensor_tensor(out=BbTi[:], in0=tA[:], in1=tB[:], op=ALU.add), **pb_)

            def big_cmul(n, out_r, out_i, ar3, ai3, neg_i, rds, wrs):
                ta, tb = tA9[:, 0:n], tB9[:, 0:n]
                ab = lambda a: a.unsqueeze(1).to_broadcast([128, n, 16, 32])
                pr = Pr[:, 0:n, :].unsqueeze(3).to_broadcast([128, n, 16, 32])
                pi = Pi[:, 0:n, :].unsqueeze(3).to_broadcast([128, n, 16, 32])
                kw = dict(r=list(rds) + [P_trig], w=list(wrs))
                dv(lambda: V.tensor_tensor(out=ta, in0=ab(ar3), in1=pr, op=ALU.mult), **kw)
                dv(lambda: V.tensor_tensor(out=tb, in0=ab(ai3), in1=pi, op=ALU.mult), **kw)
                dv(lambda: V.tensor_tensor(out=out_r, in0=ta, in1=tb, op=ALU.subtract), **kw)
                dv(lambda: V.tensor_tensor(out=ta, in0=ab(ar3), in1=pi, op=ALU.mult), **kw)
                dv(lambda: V.tensor_tensor(out=tb, in0=ab(ai3), in1=pr, op=ALU.mult), **kw)
                if neg_i:
                    dv(lambda: V.scalar_tensor_tensor(out=out_i, in0=ta, scalar=-1.0, in1=tb, op0=ALU.mult, op1=ALU.subtract), **kw)
                else:
                    dv(lambda: V.tensor_tensor(out=out_i, in0=ta, in1=tb, op=ALU.add), **kw)

            big_cmul(9, G32[:, :, 0], G32[:, :, 1], CTr[:], CTi[:], True, [P_G], [P_G])
            big_cmul(8, Qb[:, :, 0], Qb[:, :, 1], BbTr[:], BbTi[:], False, [P_B, P_G], [P_Q, P_G])
            for tau in range(8):
                ac(lambda tau=tau: A.activation(out=Wout_sb[:, :, tau, :, :].rearrange("p q r c -> p r q c"),
                                                in_=G32[:, tau + 1], func=AF.Copy), r=[P_G], w=[B_const])
            for k in range(4):
                for ri in range(2):
                    pst, psb = next_tr()
                    pv = pst[:].bitcast(BF16).rearrange("p (s c) -> p s c", s=8)
                    S.group("pe", [P_Q, B_const], [psb], [
                        (lambda s=s, k=k, ri=ri, pv=pv: PE.transpose(
                            pv[:, s, :], Qb[:, 7 - s, ri, 4 * k:4 * k + 4, :].rearrange("p q c -> p (q c)"), ident[:]))
                        for s in range(8)])
                    S.op("act", [psb], [B_const], lambda k=k, ri=ri, pv=pv: A.activation(
                        out=Win_sb[:, k, :, ri, :], in_=pv, func=AF.Copy))
            for k in range(4):
                pst, psb = next_mm()
                pv = pst[:, 0:256].rearrange("p (j c) -> p j c", j=8)
                fns = []
                for i in range(4):
                    q = 4 * k + i
                    tp = (0, 32 * i)
                    fns.append(lambda i=i, q=q, tp=tp, pv=pv: PE.matmul(
                        pv[32 * i:32 * i + 32], lhsT=BbTr[:, q, :], rhs=G32[:, 0:8, 0, q, :],
                        start=True, stop=False, tile_position=tp))
                    fns.append(lambda i=i, q=q, tp=tp, pv=pv: PE.matmul(
                        pv[32 * i:32 * i + 32], lhsT=BbTi[:, q, :], rhs=G32[:, 0:8, 1, q, :],
                        start=False, stop=True, tile_position=tp))
                S.group("pe", [P_G, P_B], [psb], fns)
                for i in range(4):
                    S.op("act", [psb], [B_const], lambda i=i, k=k, pv=pv: A.activation(
                        out=Toep_sb[32 * i:32 * i + 32, k, :, 32 * i:32 * i + 32], in_=pv[32 * i:32 * i + 32], func=AF.Copy))
            S.barrier()
        ring = sbt("ring", [128, NRING, SLOT], BF16)
        ringB = [Buf("ring%d" % i) for i in range(NRING)]
        xT2 = [sbt("xT%d" % i, [128, 8, NT], BF16) for i in range(2)]; B_xT2 = [Buf("xT%d" % i) for i in range(2)]
        pT2 = [sbt("pT%d" % i, [128, 2, NT], BF16) for i in range(2)]; B_pT2 = [Buf("pT%d" % i) for i in range(2)]
        ssm_u = sbt("ssm_u", [128, 4, 8, NCH], BF16); B_u = [Buf("u%d" % k) for k in range(4)]
        ssm_sg = sbt("ssm_sg", [128, 4, NT], BF16); B_sg = [Buf("sg%d" % k) for k in range(4)]
        Xp = [sbt("Xp%d" % r, [128, 16, NCH + 1], BF16) for r in range(2)]; B_Xp = [Buf("Xp_h%d" % h) for h in range(2)]
        Xc = [sbt("Xc%d" % r, [128, 16], F32) for r in range(2)]; B_Xc = [Buf("Xc_h%d" % h) for h in range(2)]
        RXc = [sbt("RXc%d" % r, [128, 16], F32) for r in range(2)]
        gbf = sbt("gbf", [128, 4, NT], BF16); B_g = [Buf("g%d" % k) for k in range(4)]
        y_ssm = sbt("y_ssm", [128, 4, NT], BF16); B_ys = [Buf("ys%d" % k) for k in range(4)]
        halo = sbt("halo", [128, 8, 16], F32); B_halo = [Buf("halo%d" % c) for c in range(8)]
        dall = sbt("dall", [128, 2, 2, NT], BF16)
        sall = sbt("sall", [128, 2, 2, NT], BF16)
        dbuf = [dall[:, i] for i in range(2)]; B_d = [[Buf("d%d_%d" % (i, c)) for c in range(2)] for i in range(2)]
        sgate = [sall[:, i] for i in range(2)]; B_sgt = [[Buf("sgt%d_%d" % (i, c)) for c in range(2)] for i in range(2)]
        y_pool = sbt("y_pool", [128, 8, NT], BF16); B_yp = [Buf("yp%d" % c) for c in range(8)]
        merged = sbt("merged", [128, 8, NT], BF16); B_mg = [Buf("mg%d" % c) for c in range(8)]
        tmpc = sbt("tmpc", [128, 16], F32); B_tmpc = Buf("tmpc")
        lnst = sbt("lnst", [128, 4, 2, 6], F32); lnmv = sbt("lnmv", [128, 4, 2], F32)
        lnr = sbt("lnr", [128, 4], F32); lnn = sbt("lnn", [128, 4], F32); B_ln = Buf("ln")
        chainbufs = [(sbt("cb%d" % i, [128, 512], F32), Buf("cb%d" % i)) for i in range(6)]
        tmp = [sbt("tmp%d" % i, [128, TMPW], F32) for i in range(NTMP)]
        tmpB = [Buf("tmp%d" % i) for i in range(NTMP)]

        def next_tmp():
            i = rr["tmp"] % NTMP
            rr["tmp"] += 1
            return tmp[i], tmpB[i]

        z_ap = [y_pool[:, 0:4, :].rearrange("p a b -> p (a b)").bitcast(F32),
                y_pool[:, 4:8, :].rearrange("p a b -> p (a b)").bitcast(F32),
                dall[:].rearrange("p a b c -> p (a b c)").bitcast(F32),
                sall[:].rearrange("p a b c -> p (a b c)").bitcast(F32)]
        z_bufs = [B_yp[0:4], B_yp[4:8], B_d[0] + B_d[1], B_sgt[0] + B_sgt[1]]

        out_ap = out_t.ap()
        stream_pos = [0]

        def load_slot(expect_name):
            si = stream_pos[0] % nslot
            assert plan[si][0] == expect_name, (plan[si][0], expect_name)
            stream_pos[0] += 1
            r = rr["ring"] % NRING
            rr["ring"] += 1
            S.dma("sp", [scrB[si]], [ringB[r]], lambda: SP.dma_start(out=ring[:, r, :], in_=wscr_t.ap()[si]))
            return ring[:, r, :], ringB[r]

        cur = {}

        def proj_fm(slot_ap, slot_b, cl, evac):
            xT, B_xT = cur["xT"], cur["B_xT"]
            w = slot_ap.rearrange("p (kc n) -> p kc n", kc=8)
            pst, psb = next_mm()
            S.group("pe", [slot_b, B_xT], [psb], [
                (lambda kc=kc: PE.matmul(pst[:], lhsT=w[:, kc, cl * 128:(cl + 1) * 128], rhs=xT[:, kc, :],
                                         start=(kc == 0), stop=(kc == 7))) for kc in range(8)])
            evac(pst, psb)

        nmt = nseq * tps
        pending_tail = []

        def tail_items(b, tok0):
            items = []

            for t in range(4):
                S.dma("pool", [], z_bufs[t], lambda t=t: G.dma_start(
                    out=z_ap[t], in_=x_ap[b, tok0 + t * 128:tok0 + (t + 1) * 128, :], accum_op=ALU.add))

            def stats(t):
                for hh in range(2):
                    S.op("dve", z_bufs[t], [B_ln], lambda hh=hh: V.bn_stats(
                        out=lnst[:, t, hh, :], in_=z_ap[t][:, hh * 512:(hh + 1) * 512]))
                S.op("dve", [B_ln], [B_ln], lambda: V.bn_aggr(out=lnmv[:, t, :], in_=lnst[:, t]))

            def rstd():
                S.op("dve", [B_ln], [B_ln], lambda: V.tensor_scalar(
                    out=lnr[:], in0=lnmv[:, :, 1], scalar1=LN_EPS / (ALPHA * ALPHA), scalar2=None, op0=ALU.add))
                S.op("act", [B_ln], [B_ln], lambda: A.activation(out=lnr[:], in_=lnr[:], func=AF.Sqrt))
                S.op("dve", [B_ln], [B_ln], lambda: V.reciprocal(out=lnr[:], in_=lnr[:]))
                S.op("dve", [B_ln], [B_ln], lambda: V.scalar_tensor_tensor(
                    out=lnn[:], in0=lnmv[:, :, 0], scalar=-1.0, in1=lnr[:], op0=ALU.mult, op1=ALU.mult))

            def outp(t):
                S.op("act", [B_ln] + z_bufs[t], z_bufs[t], lambda: A.activation(
                    out=z_ap[t], in_=z_ap[t], func=AF.Identity, scale=lnr[:, t:t + 1], bias=lnn[:, t:t + 1]))
                S.op("dve", [B_const] + z_bufs[t], z_bufs[t], lambda: V.tensor_tensor(
                    out=z_ap[t], in0=z_ap[t], in1=lng_sb[:], op=ALU.mult))
                if t % 2 == 0:
                    S.op("dve", [B_const] + z_bufs[t], z_bufs[t], lambda: V.tensor_tensor(
                        out=z_ap[t], in0=z_ap[t], in1=lnb_sb[:], op=ALU.add))
                else:
                    S.op("pool", [B_const] + z_bufs[t], z_bufs[t], lambda: G.tensor_tensor(
                        out=z_ap[t], in0=z_ap[t], in1=lnb_sb[:], op=ALU.add))
                S.dma("pool", z_bufs[t], [], lambda: G.dma_start(
                    out=out_ap[b, tok0 + t * 128:tok0 + (t + 1) * 128, :], in_=z_ap[t]))
            for t in range(4):
                items.append(lambda t=t: stats(t))
            items.append(rstd)
            for t in range(4):
                items.append(lambda t=t: outp(t))
            return items

        def transpose_items(mt):
            xT, B_xT = xT2[mt % 2], B_xT2[mt % 2]
            pT, B_pT = pT2[mt % 2], B_pT2[mt % 2]
            items = []

            def xitem(t):
                pst, psb = next_tr()
                pv = pst[:].bitcast(BF16).rearrange("p (k c) -> p k c", k=8)
                S.group("pe", [B_xbf, B_const], [psb], [
                    (lambda kc=kc, t=t, pv=pv: PE.transpose(pv[:, kc, :], xbf[:, t, kc * 128:(kc + 1) * 128], ident[:]))
                    for kc in range(8)])
                S.op("act", [psb], [B_xT], lambda t=t, pv=pv: A.activation(
                    out=xT[:, :, t * 128:(t + 1) * 128], in_=pv, func=AF.Copy))

            def pitem():
                pst, psb = next_tr()
                pv = pst[:].bitcast(BF16).rearrange("p (t k c) -> p t k c", t=4, k=2)
                S.group("pe", [B_pbf, B_const], [psb], [
                    (lambda kc=kc, t=t, pv=pv: PE.transpose(pv[:, t, kc, :], pbf[:, t, kc * 128:(kc + 1) * 128], ident[:]))
                    for t in range(4) for kc in range(2)])
                for kc in range(2):
                    S.op("act", [psb], [B_pT], lambda kc=kc, pv=pv: A.activation(
                        out=pT[:, kc, :].rearrange("p (t c) -> p t c", t=4), in_=pv[:, :, kc, :], func=AF.Copy))
            for t in range(4):
                items.append(lambda t=t: xitem(t))
            items.append(pitem)
            return items

        def emit_transposes(mt):
            for it in transpose_items(mt):
                it()
        if phase < 1:
            nmt = 0
        for mt in range(nmt):
            b, ti = divmod(mt, tps)
            tok0 = ti * NT
            seq_start = (ti == 0)
            xT, B_xT = xT2[mt % 2], B_xT2[mt % 2]
            pT, B_pT = pT2[mt % 2], B_pT2[mt % 2]
            cur["xT"], cur["B_xT"] = xT, B_xT
            if mt == 0:
                emit_transposes(0)
                if nmt > 1:
                    load_x(1)
            if seq_start:
                S.op("pool", [], B_halo, lambda: G.memset(halo[:], 0.0))
            if phase < 2:
                continue
            for half in range(2):
                sl, slb = load_slot("ssm_in_%d" % half)
                for cl in range(2):
                    k = 2 * half + cl

                    def ev(pst, psb, k=k):
                        S.op("act", [psb], [B_u[k]], lambda: A.activation(
                            out=ssm_u[:, k], in_=pst[:].rearrange("p (c s) -> p s c", s=8), func=AF.Copy))
                    proj_fm(sl, slb, cl, ev)
                    if pending_tail:
                        pending_tail.pop(0)()
            for half in range(2):
                sl, slb = load_slot("ssm_gate_%d" % half)
                for cl in range(2):
                    k = 2 * half + cl

                    def ev(pst, psb, k=k):
                        S.op("act", [psb], [B_sg[k]], lambda: A.activation(out=ssm_sg[:, k, :], in_=pst[:], func=AF.Silu))
                    proj_fm(sl, slb, cl, ev)
                    if pending_tail:
                        pending_tail.pop(0)()
            while pending_tail:
                pending_tail.pop(0)()
            tail2 = []
            if phase < 3:
                continue
            def s5_half_items(h2):
                q0 = 8 * h2
                F = slice(0, 512)
                ech = Ec[:, q0:q0 + 8, :].rearrange("p q c -> p (q c)")
                esh = Es[:, q0:q0 + 8, :].rearrange("p q c -> p (q c)")
                rfh = Rf[:, q0:q0 + 8, :].rearrange("p q c -> p (q c)")
                (t1, t1b), (t2, t2b), (t3, t3b), (t4, t4b), (t5, t5b), (t6, t6b) = chainbufs
                st = {}

                def q3(tl):
                    return tl[:, F].rearrange("p (q c) -> p q c", q=8)

                def pe_stage():
                    vb = [next_mm() for _ in range(4)]
                    st["vb"] = vb
                    fns = []
                    for kk in range(2):
                        k = 2 * h2 + kk
                        for s_ in range(8):
                            for i in range(4):
                                for ri in range(2):
                                    first = (kk == 0 and s_ == 0 and ri == 0)
                                    col = (kk * 2 + ri) * NCH
                                    fns.append(lambda k=k, s_=s_, i=i, ri=ri, col=col, first=first: PE.matmul(
                                        vb[i][0][:, col:col + NCH],
                                        lhsT=Win_sb[32 * i:32 * i + 32, k, s_, ri, :],
                                        rhs=ssm_u[32 * i:32 * i + 32, k, s_, :],
                                        start=first, stop=False, tile_position=(32 * i, 0), skip_group_check=True))
                    S.group("pe", [B_const, B_u[2 * h2], B_u[2 * h2 + 1]], [b for _, b in vb], fns)

                def st1():
                    vb = st["vb"]
                    for i in range(4):
                        bv = vb[i][0][:, 0:4 * NCH].rearrange("p (kk ri c) -> p kk ri c", kk=2, ri=2)
                        vr_i, vi_i = bv[:, :, 0, :], bv[:, :, 1, :]
                        ec_i = Ec[:, q0 + i:q0 + 8:4, :]
                        es_i = Es[:, q0 + i:q0 + 8:4, :]
                        pb = vb[i][1]
                        S.op("dve", [pb, B_const], [t1b], lambda i=i, vr_i=vr_i, ec_i=ec_i: V.tensor_tensor(out=q3(t1)[:, i:8:4, :], in0=vr_i, in1=ec_i, op=ALU.mult))
                        S.op("dve", [pb, B_const], [t2b], lambda i=i, vi_i=vi_i, es_i=es_i: V.tensor_tensor(out=q3(t2)[:, i:8:4, :], in0=vi_i, in1=es_i, op=ALU.mult))
                        S.op("dve", [pb, B_const], [t3b], lambda i=i, vi_i=vi_i, ec_i=ec_i: V.tensor_tensor(out=q3(t3)[:, i:8:4, :], in0=vi_i, in1=ec_i, op=ALU.mult))
                        S.op("dve", [pb, B_const], [t4b], lambda i=i, vr_i=vr_i, es_i=es_i: V.tensor_tensor(out=q3(t4)[:, i:8:4, :], in0=vr_i, in1=es_i, op=ALU.mult))

                def st2():
                    S.op("pool", [t1b, t2b], [t1b], lambda: G.tensor_tensor(out=t1[:, F], in0=t1[:, F], in1=t2[:, F], op=ALU.add))
                    S.op("pool", [t3b, t4b], [t3b], lambda: G.tensor_tensor(out=t3[:, F], in0=t3[:, F], in1=t4[:, F], op=ALU.subtract))
                    if seq_start:
                        for r in range(2):
                            S.op("pool", [], [B_Xp[h2]], lambda r=r: G.memset(Xp[r][:, q0:q0 + 8, 0:1], 0.0))
                    else:
                        for r, (vm, vmb) in enumerate(((t1, t1b), (t3, t3b))):
                            S.op("pool", [B_Xc[h2]], [B_Xp[h2]], lambda r=r: G.tensor_copy(
                                out=Xp[r][:, q0:q0 + 8, 0:1], in_=Xc[r][:, q0:q0 + 8].unsqueeze(2)))
                            v3 = q3(vm)[:, :, 0:1]
                            S.op("pool", [B_Xc[h2], vmb], [vmb], lambda r=r, v3=v3: G.tensor_tensor(
                                out=v3, in0=v3, in1=RXc[r][:, q0:q0 + 8].unsqueeze(2), op=ALU.add))

                def st3():
                    S.op("dve", [t1b, B_const], [t2b], lambda: V.tensor_tensor_scan(
                        out=t2[:, F], data0=rfh, data1=t1[:, F], initial=0.0, op0=ALU.mult, op1=ALU.add))
                    S.op("dve", [t3b, B_const], [t4b], lambda: V.tensor_tensor_scan(
                        out=t4[:, F], data0=rfh, data1=t3[:, F], initial=0.0, op0=ALU.mult, op1=ALU.add))

                def st4():
                    S.op("pool", [t2b, B_const], [t1b], lambda: G.tensor_tensor(out=t1[:, F], in0=t2[:, F], in1=ech, op=ALU.mult))
                    S.op("dve", [t2b, B_const], [t5b], lambda: V.tensor_tensor(out=t5[:, F], in0=t2[:, F], in1=esh, op=ALU.mult))
                    S.op("pool", [t4b, B_const], [t3b], lambda: G.tensor_tensor(out=t3[:, F], in0=t4[:, F], in1=esh, op=ALU.mult))
                    S.op("dve", [t4b, B_const], [t6b], lambda: V.tensor_tensor(out=t6[:, F], in0=t4[:, F], in1=ech, op=ALU.mult))

                def st5():
                    S.op("pool", [t1b, t3b], [t1b], lambda: G.tensor_tensor(out=t1[:, F], in0=t1[:, F], in1=t3[:, F], op=ALU.subtract))
                    S.op("dve", [t5b, t6b], [t5b], lambda: V.tensor_tensor(out=t5[:, F], in0=t5[:, F], in1=t6[:, F], op=ALU.add))

                def st6():
                    for r, (vm, vmb) in enumerate(((t1, t1b), (t5, t5b))):
                        v3 = q3(vm)
                        S.op("act", [vmb], [B_Xp[h2]], lambda r=r, v3=v3: A.activation(
                            out=Xp[r][:, q0:q0 + 8, 1:NCH + 1], in_=v3, func=AF.Copy))
                        S.op("pool", [vmb], [B_Xc[h2]], lambda r=r, v3=v3: G.tensor_copy(
                            out=Xc[r][:, q0:q0 + 8].unsqueeze(2), in_=v3[:, :, NCH - 1:NCH]))
                        S.op("dve", [B_Xc[h2], B_const], [B_Xc[h2]], lambda r=r: V.tensor_tensor(
                            out=RXc[r][:, q0:q0 + 8], in0=Xc[r][:, q0:q0 + 8], in1=R_sb[:, q0:q0 + 8], op=ALU.mult))
                return [pe_stage, st1, st2, st3, st4, st5, st6]

            def pool_w_mm(gi, di):
                for oc in range(2):
                    pst, psb = next_mm()
                    S.group("pe", [B_const, B_d[di][0], B_d[di][1]], [psb], [
                        (lambda ic=ic, oc=oc, pst=pst: PE.matmul(
                            pst[:], lhsT=pw_sb[:, gi, ic, oc * 128:(oc + 1) * 128], rhs=dbuf[di][:, ic, :],
                            start=(ic == 0), stop=(ic == 1))) for ic in range(2)])
                    cc = 2 * gi + oc
                    S.op("dve", [psb, B_const, B_sgt[di][oc]], [B_yp[cc]], lambda pst=pst, cc=cc, oc=oc: V.scalar_tensor_tensor(
                        out=y_pool[:, cc, :], in0=pst[:], scalar=pscale[:, cc:cc + 1], in1=sgate[di][:, oc, :],
                        op0=ALU.mult, op1=ALU.mult))

            def pool_items():
                items = []
                slots = {}
                for gi in (3, 2, 1, 0):
                    di = gi % 2
                    w = POOL_WINDOWS[gi]
                    for cl in range(2):
                        cc = 2 * gi + cl

                        def item_in(gi=gi, cl=cl, cc=cc, w=w, di=di):
                            if cl == 0:
                                slots["in"] = load_slot("pool_in_%d" % gi)
                            sl, slb = slots["in"]

                            def ev(pst, psb):
                                ub, ubb = next_tmp()
                                S.op("pool", [B_halo[cc]], [ubb], lambda: G.tensor_copy(out=ub[:, 0:16], in_=halo[:, cc, :]))
                                S.op("act", [psb], [ubb], lambda: A.activation(out=ub[:, 16:16 + NT], in_=pst[:], func=AF.Copy))
                                S.op("pool", [ubb], [B_halo[cc]], lambda: G.tensor_copy(out=halo[:, cc, :], in_=ub[:, NT:NT + 16]))
                                cur_, curb = ub, ubb
                                sh = 1
                                while sh < w:
                                    nx, nxb = next_tmp()
                                    lo = 2 * sh - 1
                                    if gi == 3 or (cl == 0 and gi > 0):
                                        S.op("dve", [curb], [nxb], lambda cur_=cur_, nx=nx, lo=lo, sh=sh: V.tensor_tensor(
                                            out=nx[:, lo:TMPW], in0=cur_[:, lo:TMPW], in1=cur_[:, lo - sh:TMPW - sh], op=ALU.add))
                                    else:
                                        S.op("pool", [curb], [nxb], lambda cur_=cur_, nx=nx, lo=lo, sh=sh: G.tensor_tensor(
                                            out=nx[:, lo:TMPW], in0=cur_[:, lo:TMPW], in1=cur_[:, lo - sh:TMPW - sh], op=ALU.add))
                                    cur_, curb = nx, nxb
                                    sh *= 2
                                S.op("dve", [curb, ubb], [B_d[di][cl]], lambda cur_=cur_: V.scalar_tensor_tensor(
                                    out=dbuf[di][:, cl, :], in0=cur_[:, 16:16 + NT], scalar=1.0 / w, in1=ub[:, 16:16 + NT],
                                    op0=ALU.mult, op1=ALU.subtract))
                                if seq_start:
                                    S.op("dve", [curb, B_const], [B_tmpc], lambda cur_=cur_: V.tensor_tensor(
                                        out=tmpc[:, 0:w - 1], in0=cur_[:, 16:16 + w - 1], in1=invc[:, 0:w - 1], op=ALU.mult))
                                    S.op("dve", [B_tmpc, ubb], [B_d[di][cl]], lambda: V.tensor_tensor(
                                        out=dbuf[di][:, cl, 0:w - 1], in0=tmpc[:, 0:w - 1], in1=ub[:, 16:16 + w - 1], op=ALU.subtract))
                            proj_fm(sl, slb, cl, ev)
                        items.append(item_in)
                    for cl in range(2):
                        def item_gate(gi=gi, cl=cl, di=di):
                            if cl == 0:
                                slots["gate"] = load_slot("pool_gate_%d" % gi)
                            sl, slb = slots["gate"]

                            def ev(pst, psb):
                                S.op("act", [psb], [B_sgt[di][cl]], lambda: A.activation(out=sgate[di][:, cl, :], in_=pst[:], func=AF.Silu))
                            proj_fm(sl, slb, cl, ev)
                        items.append(item_gate)
                    if gi < 3:
                        items.append(lambda gi=gi: pool_w_mm(gi + 1, (gi + 1) % 2))
                return items

            chain = s5_half_items(0) + s5_half_items(1)
            ditems = pool_items()
            chain.pop(0)()
            chain.pop(0)()
            nd = 0
            while chain or ditems or tail2:
                if ditems:
                    ditems.pop(0)()
                    nd += 1
                if chain:
                    chain.pop(0)()
                if tail2 and (nd % 3 == 0 or not ditems):
                    tail2.pop(0)()
            if phase < 5:
                continue
            for k in range(4):
                h2 = k // 2
                yt, yb = next_mm()
                yv = yt[:].rearrange("p (t c) -> p t c", t=8)
                fns = []
                for tau in range(8):
                    for s_ in range(tau + 1):
                        fns.append(lambda tau=tau, s_=s_: PE.matmul(
                            yv[:, tau, :], lhsT=Toep_sb[:, k, tau - s_, :], rhs=ssm_u[:, k, s_, :],
                            start=(tau == 0 and s_ == 0), stop=False, skip_group_check=True))
                    for i in range(4):
                        q = 4 * k + i
                        for ri in range(2):
                            fns.append(lambda tau=tau, i=i, q=q, ri=ri: PE.matmul(
                                yv[32 * i:32 * i + 32, tau, :], lhsT=Wout_sb[:, q, tau, ri, :], rhs=Xp[ri][:, q, 0:NCH],
                                start=False, stop=False, tile_position=(0, 32 * i), skip_group_check=True))
                S.group("pe", [B_const, B_u[k], B_Xp[h2]], [yb], fns)
                if k == 0:
                    pool_w_mm(0, 0)
                y32, y32b = next_tmp(); qq, qqb = next_tmp(); sg, sgb = next_tmp()
                F = slice(0, NT)
                S.op("dve", [yb, B_u[k], B_const], [y32b], lambda k=k, yv=yv, y32=y32: V.scalar_tensor_tensor(
                    out=y32[:, F].rearrange("p (c s) -> p c s", s=8), in0=ssm_u[:, k].rearrange("p s c -> p c s"),
                    scalar=dsk[:, k:k + 1], in1=yv.rearrange("p t c -> p c t"), op0=ALU.mult, op1=ALU.add))
                S.op("act", [y32b], [qqb], lambda y32=y32, qq=qq: A.activation(out=qq[:, F], in_=y32[:, F], func=AF.Square))
                S.op("dve", [qqb], [qqb], lambda qq=qq: V.tensor_scalar(
                    out=qq[:, F], in0=qq[:, F], scalar1=GELU_C1, scalar2=1.0, op0=ALU.mult, op1=ALU.add))
                S.op("dve", [qqb, y32b], [qqb], lambda qq=qq, y32=y32: V.tensor_tensor(
                    out=qq[:, F], in0=qq[:, F], in1=y32[:, F], op=ALU.mult))
                S.op("act", [qqb], [sgb], lambda qq=qq, sg=sg: A.activation(
                    out=sg[:, F], in_=qq[:, F], func=AF.Sigmoid, scale=2.0 * GELU_C0))
                S.op("dve", [sgb, y32b], [B_g[k]], lambda k=k, sg=sg, y32=y32: V.tensor_tensor(
                    out=gbf[:, k, :], in0=y32[:, F], in1=sg[:, F], op=ALU.mult))
            tr_items = []
            if mt + 1 < nmt:
                for it in transpose_items(mt + 1):
                    it()
                if mt + 2 < nmt:
                    load_x(mt + 2)
            if phase < 6:
                continue
            for oc in range(4):
                pst, psb = next_mm()
                S.group("pe", [B_const] + B_g, [psb], [
                    (lambda kc=kc, oc=oc, pst=pst: PE.matmul(
                        pst[:], lhsT=gluw_sb[:, kc, oc * 128:(oc + 1) * 128], rhs=gbf[:, kc, :],
                        start=(kc == 0), stop=(kc == 3))) for kc in range(4)])
                s1, s1b = next_tmp()
                F = slice(0, NT)
                S.op("act", [psb, B_const], [s1b], lambda pst=pst, s1=s1, oc=oc: A.activation(
                    out=s1[:, F], in_=pst[:], func=AF.Sigmoid, bias=glub[:, oc:oc + 1]))
                S.op("dve", [s1b, B_g[oc]], [s1b], lambda s1=s1, oc=oc: V.tensor_tensor(
                    out=s1[:, F], in0=s1[:, F], in1=gbf[:, oc, :], op=ALU.mult))
                S.op("pool", [s1b, B_sg[oc]], [B_ys[oc]], lambda s1=s1, oc=oc: G.tensor_tensor(
                    out=y_ssm[:, oc, :], in0=s1[:, F], in1=ssm_sg[:, oc, :], op=ALU.mult))
            if phase < 7:
                continue
            wbs = None
            for jp in range(4):
                gp_sl, gp_b = load_slot("g_pool_%d" % jp)
                gs_sl, gs_b = load_slot("g_ssm_%d" % jp)
                bp_sl, bp_b = load_slot("w_bp_%d" % jp)
                if jp % 2 == 0:
                    wbs = load_slot("w_bs_%d" % (jp // 2))
                bs_sl, bs_b = wbs
                bpw = bp_sl.rearrange("p (kc n) -> p kc n", kc=8)
                bsw = bs_sl.rearrange("p (kc n) -> p kc n", kc=4)
                for cl in range(2):
                    j = 2 * jp + cl
                    F = slice(0, NT)
                    gates = []
                    for (sl_, b_) in ((gp_sl, gp_b), (gs_sl, gs_b)):
                        gt, gtb = next_tmp()

                        def ev(pst, psb, gt=gt, gtb=gtb):
                            S.op("act", [psb], [gtb], lambda: A.activation(out=gt[:, F], in_=pst[:], func=AF.Sigmoid))
                        proj_fm(sl_, b_, cl, ev)
                        gates.append((gt, gtb))
                    pa, pab = next_mm()
                    S.group("pe", [bp_b] + B_yp, [pab], [
                        (lambda kc=kc, pa=pa: PE.matmul(pa[:], lhsT=bpw[:, kc, cl * 128:(cl + 1) * 128], rhs=y_pool[:, kc, :],
                                                        start=(kc == 0), stop=(kc == 7))) for kc in range(8)])
                    pb, pbb = next_mm()
                    co = (jp % 2) * 256 + cl * 128
                    S.group("pe", [bs_b] + B_ys, [pbb], [
                        (lambda kc=kc, pb=pb, co=co: PE.matmul(pb[:], lhsT=bsw[:, kc, co:co + 128], rhs=y_ssm[:, kc, :],
                                                               start=(kc == 0), stop=(kc == 3))) for kc in range(4)])
                    m1, m1b = next_tmp(); m2, m2b = next_tmp()
                    S.op("dve", [pab, gates[0][1]], [m1b], lambda pa=pa, m1=m1, g=gates[0][0]: V.tensor_tensor(
                        out=m1[:, F], in0=pa[:], in1=g[:, F], op=ALU.mult))
                    S.op("dve", [pbb, gates[1][1]], [m2b], lambda pb=pb, m2=m2, g=gates[1][0]: V.tensor_tensor(
                        out=m2[:, F], in0=pb[:], in1=g[:, F], op=ALU.mult))
                    S.op("pool", [m1b, m2b], [B_mg[j]], lambda j=j, m1=m1, m2=m2: G.tensor_tensor(
                        out=merged[:, j, :], in0=m1[:, F], in1=m2[:, F], op=ALU.add))
                    if tr_items:
                        tr_items.pop(0)()
                        if not tr_items and mt + 2 < nmt:
                            load_x(mt + 2)
            if phase < 8:
                continue
            for h in range(2):
                pg = [load_slot("ple_gate_%d_lo" % h), load_slot("ple_gate_%d_hi" % h)]
                wo = [load_slot("w_out_%d_lo" % h), load_slot("w_out_%d_hi" % h)]
                pgw = [a.rearrange("p (kc n) -> p kc n", kc=4) for a, _ in pg]
                wow = [a.rearrange("p (kc n) -> p kc n", kc=4) for a, _ in wo]
                F = slice(0, NT)
                for t in range(4):
                    ts = slice(t * 128, (t + 1) * 128)
                    pgt, pgb = next_mm()
                    S.group("pe", [pg[0][1], pg[1][1], B_xT], [pgb], [
                        (lambda kc=kc, pgt=pgt, ts=ts: PE.matmul(pgt[:], lhsT=xT[:, kc, ts], rhs=pgw[kc // 4][:, kc % 4, :],
                                                                 start=(kc == 0), stop=(kc == 7))) for kc in range(8)])
                    sgp, sgpb = next_tmp()
                    S.op("act", [pgb], [sgpb], lambda pgt=pgt, sgp=sgp: A.activation(out=sgp[:, F], in_=pgt[:], func=AF.Sigmoid))
                    ppt, ppb = next_mm()
                    S.group("pe", [B_const, B_pT], [ppb], [
                        (lambda kc=kc, ppt=ppt, ts=ts, h=h: PE.matmul(ppt[:], lhsT=pT[:, kc, ts], rhs=wple_sb[:, kc, h * 512:(h + 1) * 512],
                                                                      start=(kc == 0), stop=(kc == 1))) for kc in range(2)])
                    S.op("dve", [ppb, sgpb], [sgpb], lambda ppt=ppt, sgp=sgp: V.scalar_tensor_tensor(
                        out=sgp[:, F], in0=ppt[:], scalar=1.0 / ALPHA, in1=sgp[:, F], op0=ALU.mult, op1=ALU.mult))
                    pmt, pmb = next_mm()
                    S.group("pe", [wo[0][1], wo[1][1]] + B_mg, [pmb], [
                        (lambda kc=kc, pmt=pmt, ts=ts: PE.matmul(pmt[:], lhsT=merged[:, kc, ts], rhs=wow[kc // 4][:, kc % 4, :],
                                                                 start=(kc == 0), stop=(kc == 7))) for kc in range(8)])
                    S.op("dve", [pmb, sgpb], z_bufs[t], lambda pmt=pmt, sgp=sgp, t=t, h=h: V.scalar_tensor_tensor(
                        out=z_ap[t][:, h * 512:(h + 1) * 512], in0=pmt[:], scalar=1.0 / ALPHA, in1=sgp[:, F],
                        op0=ALU.mult, op1=ALU.add))
            pending_tail.extend(tail_items(b, tok0))
        for it in pending_tail:
            it()
        del pending_tail[:]
        S.final_wait("pool")
        S.final_wait("sp")
    return nc


_NC_CACHE = {}


def kernel(**inputs):
    ncores = 8
    x = np.ascontiguousarray(inputs["x"], dtype=np.float32)
    p = np.ascontiguousarray(inputs["p"], dtype=np.float32)[0]
    bsz, seqlen, _ = x.shape
    per = bsz // ncores
    key = (per, seqlen)
    if key not in _NC_CACHE:
        _NC_CACHE[key] = build_nc(per, seqlen)
    nc = _NC_CACHE[key]
    shared = {}
    for name in ("w_in", "pool_w", "pool_scale", "ssm_a_re", "ssm_a_im", "ssm_log_dt", "ssm_b_re", "ssm_b_im",
                 "ssm_c_re", "ssm_c_im", "ssm_d", "glu_w", "glu_b", "w_branch_pool", "w_branch_ssm", "w_out",
                 "w_ple", "ln_g", "ln_b"):
        shared[name] = np.ascontiguousarray(np.asarray(inputs[name], dtype=np.float32)[0])
    in_maps = []
    for c in range(ncores):
        m = dict(shared)
        m["x"] = np.ascontiguousarray(x[c * per:(c + 1) * per])
        m["p"] = np.ascontiguousarray(p[c * per:(c + 1) * per])
        in_maps.append(m)
    res = run_bass_kernel_spmd(nc, in_maps, core_ids=list(range(ncores)))
    return np.concatenate([np.asarray(r["out"]) for r in res.results], axis=0).astype(np.float32)
```
